# Optimizing a Trainium2 kernel written in Bass

```python
import jax
import jax.numpy as jnp
from jax import lax
import numpy as np

D_MODEL = 4096
BATCH = 4
SEQ = 2048
DEPTH = 1

HEAD_DIM = 128
NSA_HEADS = 16
NSA_KV_GROUPS = 4
NSA_Q_PER_KV = NSA_HEADS // NSA_KV_GROUPS
FOX_HEADS = 16
NSA_WIDTH = NSA_HEADS * HEAD_DIM
FOX_WIDTH = FOX_HEADS * HEAD_DIM
N_NSA_BRANCHES = 3
N_MERGE_BRANCHES = 2
CMP_BLOCK = 32
CMP_STRIDE = 16
SEL_BLOCK = 64
SEL_TOPN = 16
WINDOW = 512
Q_BLOCK = 128
SEL_Q_BLOCK = 16
ROPE_THETA = 500000.0
ROT_DIM = HEAD_DIM // 4
D_FF = 4 * D_MODEL
RMS_EPS = 1e-6
ATTN_SCALE = HEAD_DIM ** -0.5

COLS_NSA_Q = NSA_WIDTH
COLS_NSA_KV = N_NSA_BRANCHES * 2 * NSA_KV_GROUPS * HEAD_DIM
COLS_NSA_GATE = N_NSA_BRANCHES * NSA_HEADS
COLS_FOX_QKV = 3 * FOX_WIDTH
COLS_FOX_F = FOX_HEADS
COLS_MERGE = N_MERGE_BRANCHES * D_MODEL
D_IN = COLS_NSA_Q + COLS_NSA_KV + COLS_NSA_GATE + COLS_FOX_QKV + COLS_FOX_F + COLS_MERGE
SPLIT_POINTS = [COLS_NSA_Q,
                COLS_NSA_Q + COLS_NSA_KV,
                COLS_NSA_Q + COLS_NSA_KV + COLS_NSA_GATE,
                COLS_NSA_Q + COLS_NSA_KV + COLS_NSA_GATE + COLS_FOX_QKV,
                COLS_NSA_Q + COLS_NSA_KV + COLS_NSA_GATE + COLS_FOX_QKV + COLS_FOX_F]

kernel_name = "hybrid_nsa_fox_gated_block"


def _rmsnorm(x, g):
    xf = x.astype(jnp.float32)
    y = xf * lax.rsqrt(jnp.mean(xf * xf, axis=-1, keepdims=True) + RMS_EPS)
    return (y * g.astype(jnp.float32)).astype(x.dtype)


def _partial_rope(x, pos):
    half = ROT_DIM // 2
    inv = ROPE_THETA ** (-jnp.arange(half, dtype=jnp.float32) / half)
    ang = jnp.asarray(pos, dtype=jnp.float32)[:, None] * inv[None, :]
    cos = jnp.cos(ang)[:, None, :].astype(x.dtype)
    sin = jnp.sin(ang)[:, None, :].astype(x.dtype)
    x1 = x[..., :half]
    x2 = x[..., half:ROT_DIM]
    return jnp.concatenate([x1 * cos - x2 * sin, x1 * sin + x2 * cos, x[..., ROT_DIM:]], axis=-1)


def _masked_softmax(s, mask):
    s = jnp.where(mask, s.astype(jnp.float32), -jnp.inf)
    m = jnp.max(s, axis=-1, keepdims=True)
    m = jnp.where(jnp.isfinite(m), m, 0.0)
    p = jnp.exp(s - m)
    d = jnp.sum(p, axis=-1, keepdims=True)
    return p / jnp.where(d > 0, d, 1.0)


def _compress(x, pos_emb, w1, w2):
    B, T, G, dh = x.shape
    n_cmp = (T - CMP_BLOCK) // CMP_STRIDE + 1
    idx = np.arange(n_cmp)[:, None] * CMP_STRIDE + np.arange(CMP_BLOCK)[None, :]
    blocks = x[:, idx] + pos_emb[None, None, :, None, :]
    blocks = blocks.transpose(0, 1, 3, 2, 4).reshape(B, n_cmp, G, CMP_BLOCK * dh)
    return jax.nn.gelu(blocks @ w1) @ w2


def _nsa(q, kv, gate_logits, nsa_k_norm, cmp_pos_k, cmp_pos_v, w_cmp_k1, w_cmp_k2, w_cmp_v1, w_cmp_v2):
    B, T, H, dh = q.shape
    G, R = NSA_KV_GROUPS, NSA_Q_PER_KV
    pos = np.arange(T)
    q_g = q.reshape(B, T, G, R, dh)
    kc, vc = kv[:, :, 0, 0], kv[:, :, 0, 1]
    ks = _partial_rope(_rmsnorm(kv[:, :, 1, 0], nsa_k_norm), pos)
    vs = kv[:, :, 1, 1]
    kw = _partial_rope(_rmsnorm(kv[:, :, 2, 0], nsa_k_norm), pos)
    vw = kv[:, :, 2, 1]

    n_cmp = (T - CMP_BLOCK) // CMP_STRIDE + 1
    end_pos = np.arange(n_cmp) * CMP_STRIDE + CMP_BLOCK - 1
    k_cmp = _compress(kc, cmp_pos_k, w_cmp_k1, w_cmp_k2)
    k_cmp = _partial_rope(_rmsnorm(k_cmp, nsa_k_norm), end_pos)
    v_cmp = _compress(vc, cmp_pos_v, w_cmp_v1, w_cmp_v2)
    s_c = jnp.einsum('btgrd,bngd->bgrtn', q_g, k_cmp) * ATTN_SCALE
    p_c = _masked_softmax(s_c, end_pos[None, :] <= pos[:, None])
    o_c = jnp.einsum('bgrtn,bngd->btgrd', p_c.astype(v_cmp.dtype), v_cmp)

    n_slc = T // SEL_BLOCK
    ci = np.arange(n_cmp)[:, None] * CMP_STRIDE
    sj = np.arange(n_slc)[None, :] * SEL_BLOCK
    overlap = ((ci < sj + SEL_BLOCK) & (ci + CMP_BLOCK > sj)).astype(np.float32)
    imp = jnp.einsum('bgtn,nj->bgtj', jnp.sum(p_c, axis=2), overlap)
    cur = pos // SEL_BLOCK
    jj = np.arange(n_slc)
    forced = (jj[None, :] == 0) | (jj[None, :] == cur[:, None]) | (jj[None, :] == cur[:, None] - 1)
    causal = jj[None, :] <= cur[:, None]
    score = jnp.where(forced, jnp.inf, jnp.where(causal, imp, -jnp.inf))
    top_n = min(SEL_TOPN, n_slc)
    sel_val, sel_idx = lax.top_k(score, top_n)
    sel_ok = sel_val > -jnp.inf

    kb = ks.reshape(B, n_slc, SEL_BLOCK, G, dh).transpose(0, 3, 1, 2, 4)
    vb = vs.reshape(B, n_slc, SEL_BLOCK, G, dh).transpose(0, 3, 1, 2, 4)
    nb = T // SEL_Q_BLOCK
    bi = jnp.arange(B)[:, None, None, None]
    gi = jnp.arange(G)[None, :, None, None]

    def sel_block(args):
        cblk, qb, ib, okb = args
        kg = kb[bi, gi, ib]
        vg = vb[bi, gi, ib]
        tq = cblk * SEL_Q_BLOCK + jnp.arange(SEL_Q_BLOCK)
        kpos = ib[..., None] * SEL_BLOCK + jnp.arange(SEL_BLOCK)
        mask = okb[..., None] & (kpos <= tq[None, None, :, None, None])
        s = jnp.einsum('bqgrd,bgqnld->bgrqnl', qb, kg) * ATTN_SCALE
        s = s.reshape(B, G, R, SEL_Q_BLOCK, top_n * SEL_BLOCK)
        p = _masked_softmax(s, mask.reshape(B, G, 1, SEL_Q_BLOCK, top_n * SEL_BLOCK))
        p = p.reshape(B, G, R, SEL_Q_BLOCK, top_n, SEL_BLOCK).astype(vg.dtype)
        return jnp.einsum('bgrqnl,bgqnld->bqgrd', p, vg)

    xs_sel = (jnp.arange(nb),
              q_g.reshape(B, nb, SEL_Q_BLOCK, G, R, dh).swapaxes(0, 1),
              sel_idx.reshape(B, G, nb, SEL_Q_BLOCK, top_n).transpose(2, 0, 1, 3, 4),
              sel_ok.reshape(B, G, nb, SEL_Q_BLOCK, top_n).transpose(2, 0, 1, 3, 4))
    o_s = lax.map(sel_block, xs_sel).swapaxes(0, 1).reshape(B, T, G, R, dh)

    kpad = jnp.pad(kw, ((0, 0), (WINDOW, 0), (0, 0), (0, 0)))
    vpad = jnp.pad(vw, ((0, 0), (WINDOW, 0), (0, 0), (0, 0)))
    nq = T // Q_BLOCK
    span = WINDOW + Q_BLOCK

    def win_block(args):
        cblk, qb = args
        start = cblk * Q_BLOCK
        kband = lax.dynamic_slice_in_dim(kpad, start, span, axis=1)
        vband = lax.dynamic_slice_in_dim(vpad, start, span, axis=1)
        tq = start + jnp.arange(Q_BLOCK)
        kpos = start - WINDOW + jnp.arange(span)
        mask = (kpos[None, :] <= tq[:, None]) & (kpos[None, :] > tq[:, None] - WINDOW) & (kpos[None, :] >= 0)
        s = jnp.einsum('bqgrd,bkgd->bgrqk', qb, kband) * ATTN_SCALE
        p = _masked_softmax(s, mask).astype(vband.dtype)
        return jnp.einsum('bgrqk,bkgd->bqgrd', p, vband)

    xs_win = (jnp.arange(nq), q_g.reshape(B, nq, Q_BLOCK, G, R, dh).swapaxes(0, 1))
    o_w = lax.map(win_block, xs_win).swapaxes(0, 1).reshape(B, T, G, R, dh)

    g = jax.nn.sigmoid(gate_logits).reshape(B, T, G, R, N_NSA_BRANCHES)
    o = g[..., 0:1] * o_c + g[..., 1:2] * o_s + g[..., 2:3] * o_w
    return o.reshape(B, T, NSA_WIDTH)


def _fox(q, k, v, f_logit):
    B, T, H, dh = q.shape
    cum = jnp.cumsum(jax.nn.log_sigmoid(f_logit.astype(jnp.float32)), axis=1).transpose(0, 2, 1)
    outs = []
    for i in range(T // Q_BLOCK):
        qs, qe = i * Q_BLOCK, (i + 1) * Q_BLOCK
        s = jnp.einsum('bqhd,bkhd->bhqk', q[:, qs:qe], k[:, :qe]).astype(jnp.float32) * ATTN_SCALE
        bias = cum[:, :, qs:qe, None] - cum[:, :, None, :qe]
        mask = np.arange(qs, qe)[:, None] >= np.arange(qe)[None, :]
        p = _masked_softmax(s + bias, mask).astype(v.dtype)
        outs.append(jnp.einsum('bhqk,bkhd->bqhd', p, v[:, :qe]))
    return jnp.concatenate(outs, axis=1).reshape(B, T, FOX_WIDTH)


def setup_inputs(seed: int = 0) -> dict:
    key = jax.random.key(seed)
    ks = jax.random.split(key, 24)
    f32 = jnp.float32
    L = DEPTH

    def nrm(k, shape, fan_in, mult=1.0):
        return jax.random.normal(k, shape, f32) * (mult * fan_in ** -0.5)

    def gain(k, shape):
        return 1.0 + 0.02 * jax.random.normal(k, shape, f32)

    return {
        "x": jax.random.normal(ks[0], (BATCH, SEQ, D_MODEL), f32),
        "c": jax.random.normal(ks[1], (BATCH, D_MODEL), f32),
        "w_ada": nrm(ks[2], (L, D_MODEL, 6 * D_MODEL), D_MODEL, 0.5),
        "b_ada": 0.01 * jax.random.normal(ks[3], (L, 6 * D_MODEL), f32),
        "norm1_g": gain(ks[4], (L, D_MODEL)),
        "norm2_g": gain(ks[5], (L, D_MODEL)),
        "w_in": nrm(ks[6], (L, D_MODEL, D_IN), D_MODEL),
        "b_forget": jax.random.uniform(ks[7], (L, FOX_HEADS), f32, 1.0, 4.0),
        "nsa_q_norm": gain(ks[8], (L, HEAD_DIM)),
        "nsa_k_norm": gain(ks[9], (L, HEAD_DIM)),
        "fox_q_norm": gain(ks[10], (L, HEAD_DIM)),
        "fox_k_norm": gain(ks[11], (L, HEAD_DIM)),
        "cmp_pos_k": 0.1 * jax.random.normal(ks[12], (L, CMP_BLOCK, HEAD_DIM), f32),
        "cmp_pos_v": 0.1 * jax.random.normal(ks[13], (L, CMP_BLOCK, HEAD_DIM), f32),
        "w_cmp_k1": nrm(ks[14], (L, CMP_BLOCK * HEAD_DIM, HEAD_DIM), CMP_BLOCK * HEAD_DIM),
        "w_cmp_k2": nrm(ks[15], (L, HEAD_DIM, HEAD_DIM), HEAD_DIM),
        "w_cmp_v1": nrm(ks[16], (L, CMP_BLOCK * HEAD_DIM, HEAD_DIM), CMP_BLOCK * HEAD_DIM),
        "w_cmp_v2": nrm(ks[17], (L, HEAD_DIM, HEAD_DIM), HEAD_DIM),
        "w_up_nsa": nrm(ks[18], (L, NSA_WIDTH, D_MODEL), NSA_WIDTH),
        "w_up_fox": nrm(ks[19], (L, FOX_WIDTH, D_MODEL), FOX_WIDTH),
        "w_out": nrm(ks[20], (L, D_MODEL, D_MODEL), D_MODEL),
        "w_ff1": nrm(ks[21], (L, D_MODEL, D_FF), D_MODEL),
        "w_ff2": nrm(ks[22], (L, D_FF, D_MODEL), D_FF),
    }


def reference(x, c, w_ada, b_ada, norm1_g, norm2_g, w_in, b_forget, nsa_q_norm, nsa_k_norm,
              fox_q_norm, fox_k_norm, cmp_pos_k, cmp_pos_v, w_cmp_k1, w_cmp_k2, w_cmp_v1, w_cmp_v2,
              w_up_nsa, w_up_fox, w_out, w_ff1, w_ff2):
    B, T, D = x.shape
    pos = np.arange(T)
    for l in range(DEPTH):
        mod = jax.nn.silu(c) @ w_ada[l] + b_ada[l]
        shift1, scale1, gate1, shift2, scale2, gate2 = [m[:, None, :] for m in jnp.split(mod, 6, axis=-1)]

        h = _rmsnorm(x, norm1_g[l]) * (1.0 + scale1) + shift1
        proj = h @ w_in[l]
        nsa_q, nsa_kv, nsa_gate, fox_qkv, fox_f, merge_g = jnp.split(proj, SPLIT_POINTS, axis=-1)

        q_n = _partial_rope(_rmsnorm(nsa_q.reshape(B, T, NSA_HEADS, HEAD_DIM), nsa_q_norm[l]), pos)
        kv_n = nsa_kv.reshape(B, T, N_NSA_BRANCHES, 2, NSA_KV_GROUPS, HEAD_DIM)
        o_nsa = _nsa(q_n, kv_n, nsa_gate, nsa_k_norm[l], cmp_pos_k[l], cmp_pos_v[l],
                     w_cmp_k1[l], w_cmp_k2[l], w_cmp_v1[l], w_cmp_v2[l])

        qkv_f = fox_qkv.reshape(B, T, 3, FOX_HEADS, HEAD_DIM)
        q_f = _rmsnorm(qkv_f[:, :, 0], fox_q_norm[l])
        k_f = _rmsnorm(qkv_f[:, :, 1], fox_k_norm[l])
        o_fox = _fox(q_f, k_f, qkv_f[:, :, 2], fox_f + b_forget[l])

        g_merge = jax.nn.sigmoid(merge_g)
        y = g_merge[..., :D] * (o_nsa @ w_up_nsa[l]) + g_merge[..., D:] * (o_fox @ w_up_fox[l])
        x = x + gate1 * (y @ w_out[l])

        h2 = _rmsnorm(x, norm2_g[l]) * (1.0 + scale2) + shift2
        x = x + gate2 * (jnp.square(jax.nn.relu(h2 @ w_ff1[l])) @ w_ff2[l])
    return x
```

```python
import numpy as np
import concourse.bass as bass
import concourse.mybir as mybir
from concourse.alu_op_type import AluOpType as ALU
from concourse.bass_utils import run_bass_kernel_spmd

AF = mybir.ActivationFunctionType
F32 = mybir.dt.float32
BF16 = mybir.dt.bfloat16

D = 4096
TOWN = 1024
TCTX = 2048
DIN = 19520
DFF = 16384
SCALE = 128 ** -0.5
EPS = 1e-6
NEG = -30000.0
BIG = 1.0e4

CB_ID, CB_ONES, CB_PERM, CB_MDIAG, CB_MWIN, CB_ESEL, CB_OV, CB_CMPB = 0, 128, 256, 384, 896, 1408, 3456, 3488
NCB = 3488 + 1024
CF_ID, CF_FTAB, CF_PREV, CF_ONES, CF_ID16 = 0, 128, 384, 385, 513
NCF = 513 + 16


class Buf:
    __slots__ = ("w", "r")

    def __init__(self):
        self.w = None
        self.r = {}


class Sem:
    __slots__ = ("h", "val")

    def __init__(self, h):
        self.h = h
        self.val = 0


class Queue:
    def __init__(self, name, sem, inorder=False):
        self.name = name
        self.sem = sem
        self.ops = []
        self.waited = {}
        self.inorder = inorder
        self.ring = []
        self.ring_i = 0


class Sched:
    def __init__(self, nc, n_dma_sems=20):
        self.nc = nc
        mk = lambda n: Sem(nc.alloc_semaphore(n))
        self.pe = Queue("pe", mk("s_pe"), inorder=True)
        self.act = Queue("act", mk("s_act"))
        self.dve = Queue("dve", mk("s_dve"))
        self.pool = Queue("pool", mk("s_pool"))
        self.sp = Queue("sp", mk("s_sp"))
        self.queues = [self.pe, self.act, self.dve, self.pool, self.sp]
        for q in (self.sp, self.pool):
            q.ring = [mk(f"d_{q.name}{i}") for i in range(n_dma_sems)]
        self.n_ops = 0

    def _wait(self, q, sem, val):
        if q.waited.get(sem, 0) >= val:
            return
        q.waited[sem] = val
        q.ops.append(("w", sem, val))

    def emit(self, q, fn, reads=(), writes=(), dma=False):
        deps = {}
        for b in reads:
            if b.w is not None:
                s, v = b.w
                if deps.get(s, 0) < v:
                    deps[s] = v
        for b in writes:
            if b.w is not None:
                s, v = b.w
                if deps.get(s, 0) < v:
                    deps[s] = v
            for s, v in b.r.items():
                if deps.get(s, 0) < v:
                    deps[s] = v
        for s, v in deps.items():
            if s is q.sem and q.inorder and not dma:
                continue
            self._wait(q, s, v)
        if dma:
            sem = q.ring[q.ring_i % len(q.ring)]
            q.ring_i += 1
            if sem.val > 0:
                self._wait(q, sem, sem.val)
            sem.val += 16
            tok = (sem, sem.val)
            q.ops.append(("o", fn, sem, 16))
        else:
            q.sem.val += 1
            tok = (q.sem, q.sem.val)
            q.ops.append(("o", fn, q.sem, 1))
        for b in reads:
            if b.r.get(tok[0], 0) < tok[1]:
                b.r[tok[0]] = tok[1]
        for b in writes:
            b.w = tok
            b.r = {}
        self.n_ops += 1
        return tok

    def barrier(self):
        sems = [q.sem for q in self.queues] + [s for q in self.queues for s in q.ring]
        for q in self.queues:
            for s in sems:
                if s.val > 0 and s is not q.sem:
                    self._wait(q, s, s.val)

    def finish(self, final_bufs):
        q = self.sp
        for b in final_bufs:
            if b.w is not None:
                self._wait(q, b.w[0], b.w[1])
        nc = self.nc

        def run(queue):
            def body(e):
                for op in queue.ops:
                    if op[0] == "w":
                        e.wait_ge(op[1].h, op[2])
                    else:
                        op[1](e).then_inc(op[2].h, op[3])
            return body

        with nc.Block() as block:
            block.tensor(run(self.pe))
            block.scalar(run(self.act))
            block.vector(run(self.dve))
            block.gpsimd(run(self.pool))
            block.sync(run(self.sp))


class TB:
    __slots__ = ("t", "b")

    def __init__(self, t, b=None):
        self.t = t
        self.b = b or Buf()


class Ring:
    def __init__(self, items):
        self.items = items
        self.i = 0

    def next(self):
        x = self.items[self.i % len(self.items)]
        self.i += 1
        return x


class Arena:
    def __init__(self, nc):
        self.nc = nc
        self.off = (nc.sbuf_base + 31) // 32 * 32
        self.top = nc.sbuf_top
        self.n = 0

    def alloc(self, shape, dtype, name=None):
        per = 1
        for s in shape[1:]:
            per *= s
        size = per * (2 if dtype == BF16 else 4)
        off = self.off
        self.off += (size + 31) // 32 * 32
        assert self.off <= self.top, f"SBUF overflow {self.off} > {self.top} ({name})"
        self.n += 1
        return TB(self.nc.alloc_sbuf_tensor_at(name or f"t{self.n}", list(shape), dtype, offset=off))

    def mark(self):
        return self.off

    def reset(self, m):
        self.off = m


def build_program(phases=9, dbg=()):
    nc = bass.Bass("TRN2", target_bir_lowering=False)
    S = Sched(nc)
    A = Arena(nc)
    PE, ACT, DVE, POOL, SP = S.pe, S.act, S.dve, S.pool, S.sp

    def din(name, shape):
        return nc.dram_tensor(name, list(shape), F32, kind="ExternalInput").ap()

    xc = din("xc", [TCTX, D])
    cT_d = din("cT", [128, 32])
    w_ada = din("w_ada", [D, 6 * D])
    badaT_d = din("b_adaT", [128, 192])
    gT_d = din("gT", [128, 64])
    w_in = din("w_in", [D, DIN])
    bfg_d = din("bfg", [16, 1])
    qkg_d = din("qkg", [128, 4])
    cposT_d = din("cposT", [128, 64])
    w_ck1 = din("w_ck1", [4096, 128])
    w_ck2 = din("w_ck2", [128, 128])
    w_cv1 = din("w_cv1", [4096, 128])
    w_cv2 = din("w_cv2", [128, 128])
    w_upa = din("w_upa", [2048, D])
    w_upb = din("w_upb", [2048, D])
    w_out = din("w_out", [D, D])
    w_ff1 = din("w_ff1", [D, DFF])
    w_ff2 = din("w_ff2", [DFF, D])
    ropeC_d = din("ropeC", [128, TCTX])
    ropeS_d = din("ropeS", [128, TCTX])
    ropeCc_d = din("ropeCc", [128, 128])
    ropeSc_d = din("ropeSc", [128, 128])
    cb_d = din("cb", [128, NCB])
    cf_d = din("cf", [128, NCF])
    out = nc.dram_tensor("out", [TOWN, D], F32, kind="ExternalOutput").ap()
    dbg_out = {}

    def dscr(name, shape, dt=BF16):
        return nc.dram_tensor(name, list(shape), dt).ap()

    d_q = dscr("d_q", [16, 128, TOWN])
    d_kc = dscr("d_kc", [4, 128, TCTX])
    d_vc = dscr("d_vc", [4, 128, TCTX])
    d_ks = dscr("d_ks", [4, 128, TCTX])
    d_kw = dscr("d_kw", [4, 128, TCTX])
    d_vs = dscr("d_vs", [TCTX, 4, 128])
    d_vw = dscr("d_vw", [TCTX, 4, 128])
    d_fq = dscr("d_fq", [16, 128, TOWN])
    d_fk = dscr("d_fk", [16, 128, TCTX])
    d_fv = dscr("d_fv", [TCTX, 16, 128])
    d_mg = dscr("d_mg", [64, 128, TOWN])
    grid = lambda n: [[Buf() for _ in range(4)] for _ in range(n)]
    g_q, g_kc, g_vc, g_ks, g_kw, g_vs, g_vw = grid(16), grid(4), grid(4), grid(4), grid(4), grid(4), grid(4)
    g_fq, g_fk, g_fv, g_mg = grid(16), grid(16), grid(16), grid(64)
    g_out = [[Buf() for _ in range(2)] for _ in range(32)]

    def mm(out_, lhsT, rhs, start, stop, reads, writes):
        S.emit(PE, lambda e: e.matmul(out_, lhsT, rhs, start=start, stop=stop, skip_group_check=True), reads, writes)

    def tr(out_, in_, ident, reads, writes):
        S.emit(PE, lambda e: e.transpose(out_, in_, ident), reads, writes)

    def act(out_, in_, func, reads, writes, **kw):
        S.emit(ACT, lambda e: e.activation(out=out_, in_=in_, func=func, **kw), reads, writes)

    def tt(q, out_, in0, in1, op, reads, writes):
        S.emit(q, lambda e: e.tensor_tensor(out=out_, in0=in0, in1=in1, op=op), reads, writes)

    def ts(q, out_, in0, s1, s2, op0, op1, reads, writes):
        if op1 is None:
            S.emit(q, lambda e: e.tensor_scalar(out=out_, in0=in0, scalar1=s1, scalar2=None, op0=op0), reads, writes)
        else:
            S.emit(q, lambda e: e.tensor_scalar(out=out_, in0=in0, scalar1=s1, scalar2=s2, op0=op0, op1=op1),
                   reads, writes)

    def stt(out_, in0, scalar, in1, op0, op1, reads, writes):
        S.emit(DVE, lambda e: e.scalar_tensor_tensor(out=out_, in0=in0, scalar=scalar, in1=in1, op0=op0, op1=op1),
               reads, writes)

    def cp(q, out_, in_, reads, writes):
        if q is ACT:
            S.emit(q, lambda e: e.activation(out=out_, in_=in_, func=AF.Copy), reads, writes)
        else:
            S.emit(q, lambda e: e.tensor_copy(out=out_, in_=in_), reads, writes)

    def dma(q, out_, in_, reads, writes):
        S.emit(q, lambda e: e.dma_start(out=out_, in_=in_), reads, writes, dma=True)

    def dump(name, src_ap, shape, reads, dt=F32, q=None):
        if name not in dbg:
            return
        t = nc.dram_tensor("dbg_" + name, list(shape), dt, kind="ExternalOutput").ap()
        b = Buf()
        dma(q or (POOL if dt != src_ap.dtype else SP), t, src_ap, reads, [b])
        dbg_out[name] = b

    PS = [TB(nc.alloc_psum_tensor(f"ps{i}", [128, 512], F32)) for i in range(8)]
    PSB = [p.t.bitcast(BF16) for p in PS]

    cb = A.alloc([128, NCB], BF16, "cb")
    cf = A.alloc([128, NCF], F32, "cf")
    modT = A.alloc([128, 192], F32, "modT")
    gT = A.alloc([128, 64], F32, "gT")
    s1T = A.alloc([128, 32], F32, "s1T")
    s2T = A.alloc([128, 32], F32, "s2T")
    qkg = A.alloc([128, 4], F32, "qkg")
    gate_tok = A.alloc([128, 8, 48], F32, "gate_tok")
    lsp = TB(nc.alloc_sbuf_tensor_at("lsp", [16, TCTX], F32, offset=(A.top - 8192) // 32 * 32))
    NSLOT = 5
    wslots = [A.alloc([128, 32, 128], BF16, f"ws{i}") for i in range(NSLOT)]
    wring = Ring(wslots)
    m0 = A.mark()

    ident = cb.t[:, CB_ID:CB_ID + 128]
    onesb = cb.t[:, CB_ONES:CB_ONES + 128]
    permT = cb.t[:, CB_PERM:CB_PERM + 32]
    mdiag4 = cb.t[:, CB_MDIAG:CB_MDIAG + 512]
    mwin4 = cb.t[:, CB_MWIN:CB_MWIN + 512]
    identf = cf.t[:, CF_ID:CF_ID + 128]

    dma(POOL, cb.t[:, 0:2048], cb_d[:, 0:2048], [], [cb.b])
    dma(POOL, cb.t[:, 2048:NCB], cb_d[:, 2048:NCB], [], [cb.b])
    dma(SP, cf.t[:], cf_d, [], [cf.b])
    dma(SP, gT.t[:], gT_d, [], [gT.b])
    dma(SP, qkg.t[:], qkg_d, [], [qkg.b])

    def gemm(srcs, ncols, T, prings, epilogue, defer=1, tok0=0):
        ttiles = [(t0, min(512, T - t0)) for t0 in range(0, T, 512)]
        ctiles = [(ci, c, min(128, ncols - c)) for ci, c in enumerate(range(0, ncols, 128))]
        loads = {}

        def issue(ci):
            _, c, ncol = ctiles[ci]
            sl = []
            for s in srcs:
                KC = s["K"] // 128
                Wv = s["W"].rearrange("(kc p) n -> p kc n", p=128)
                for kc0 in range(0, KC, 32):
                    nk = min(32, KC - kc0)
                    w = wring.next()
                    c0 = s["col0"] + c
                    S.emit(POOL, (lambda e, w=w, Wv=Wv, kc0=kc0, nk=nk, c0=c0, ncol=ncol:
                                  e.dma_start(out=w.t[:, 0:nk, 0:ncol], in_=Wv[:, kc0:kc0 + nk, c0:c0 + ncol])),
                           [], [w.b], dma=True)
                    sl.append((s, w, kc0, nk))
            loads[ci] = sl

        nslot_per = sum((s["K"] // 128 + 31) // 32 for s in srcs)
        ahead = max(1, NSLOT // nslot_per - 1)
        for ci in range(min(ahead, len(ctiles))):
            issue(ci)
        pending = []
        for ci, c, ncol in ctiles:
            if ci + ahead < len(ctiles):
                issue(ci + ahead)
            outs = []
            for si, s in enumerate(srcs):
                KC = s["K"] // 128
                row = []
                for (t0, tn) in ttiles:
                    p = prings[si].next()
                    i = 0
                    for (s2, w, kc0, nk) in loads[ci]:
                        if s2 is not s:
                            continue
                        for k in range(nk):
                            mm(p.t[0:ncol, 0:tn], w.t[:, k, 0:ncol],
                               s["inT"].t[:, s["kc_off"] + kc0 + k, tok0 + t0:tok0 + t0 + tn],
                               i == 0, i == KC - 1, [w.b, s["inT"].b], [p.b])
                            i += 1
                    row.append((p, t0, tn))
                outs.append(row)
            del loads[ci]
            pending.append((ci, ncol, outs))
            if len(pending) > defer:
                epilogue(*pending.pop(0))
        for pnd in pending:
            epilogue(*pnd)

    R_all = Ring(PS)
    cTf = A.alloc([128, 32], F32, "cTf")
    csig = A.alloc([128, 32], F32, "csig")
    silu = A.alloc([128, 32, 1], BF16, "silu")
    badaT = A.alloc([128, 192], F32, "badaT")
    dma(SP, cTf.t[:], cT_d, [], [cTf.b])
    dma(SP, badaT.t[:], badaT_d, [], [badaT.b])
    act(csig.t[:], cTf.t[:], AF.Sigmoid, [cTf.b], [csig.b])
    tt(DVE, silu.t[:, :, 0], cTf.t[:], csig.t[:], ALU.mult, [cTf.b, csig.b], [silu.b])

    def mod_gen(tile_lo, tile_hi, bank_fn, ahead=3):
        Wv = w_ada.rearrange("(kc p) n -> p kc n", p=128)
        work = [(t, s_) for t in range(tile_lo, tile_hi) for s_ in range(4)]
        issued = {}

        def issue(i):
            t, s_ = work[i]
            w = wring.next()
            wv = w.t[:].rearrange("p k c -> p (k c)").rearrange("p (k c) -> p k c", k=8)
            S.emit(POOL, (lambda e, wv=wv, t=t, s_=s_: e.dma_start(
                out=wv, in_=Wv[:, s_ * 8:(s_ + 1) * 8, t * 512:(t + 1) * 512])), [], [w.b], dma=True)
            issued[i] = (w, wv)

        for i in range(min(ahead, len(work))):
            issue(i)
        for i, (t, s_) in enumerate(work):
            if i + ahead < len(work):
                issue(i + ahead)
            w, wv = issued.pop(i)
            bank = bank_fn()
            for j in range(4):
                for k in range(8):
                    mm(bank.t[:, j:j + 1], wv[:, k, j * 128:(j + 1) * 128], silu.t[:, s_ * 8 + k, 0:1],
                       s_ == 0 and j == 0 and k == 0, s_ == 3 and k == 7, [w.b, silu.b], [bank.b])
            if s_ == 3:
                tt(DVE, modT.t[:, 4 * t:4 * t + 4], bank.t[:, 0:4], badaT.t[:, 4 * t:4 * t + 4], ALU.add,
                   [badaT.b], [modT.b, bank.b])
            yield

    for _ in mod_gen(0, 16, lambda: PS[0]):
        pass
    mod_bg = mod_gen(16, 48, lambda: mod_bank[0])
    mod_bank = [PS[6]]

    def mod_step(n=1):
        for _ in range(n):
            next(mod_bg, None)

    stt(s1T.t[:], modT.t[:, 32:64], 1.0, gT.t[:, 0:32], ALU.add, ALU.mult, [modT.b, gT.b], [s1T.b])

    def build_hT(src, row0, ntok, hT, shift_cols, sT, region_mark):
        A.reset(region_mark)
        xts = Ring([A.alloc([128, D], F32, f"xt{i}") for i in range(2)])
        xhs = [A.alloc([128, D], BF16, f"xh{j}") for j in range(4)]
        ss = Ring([A.alloc([128, 4], F32, f"ss{i}") for i in range(4)])
        pr = Ring(PS)
        for grp in range(ntok // 512):
            for j in range(4):
                r0 = row0 + grp * 512 + j * 128
                xt = xts.next()
                st = ss.next()
                dma(SP, xt.t[:], src[r0:r0 + 128, :], [], [xt.b])
                xh = xhs[j]
                S.emit(DVE, lambda e, xt=xt, st=st, xh=xh: e.scalar_tensor_tensor(
                    out=xh.t[:], in0=xt.t[:], scalar=1.0, in1=xt.t[:],
                    op0=ALU.mult, op1=ALU.mult, accum_out=st.t[:, 0:1]), [xt.b], [xh.b, st.b])
                act(st.t[:, 1:2], st.t[:, 0:1], AF.Ln, [epsc.b], [st.b], scale=1.0 / D, bias=epsc.t[:, 0:1])
                act(st.t[:, 2:3], st.t[:, 1:2], AF.Exp, [], [st.b], scale=-0.5)
                act(xh.t[:], xt.t[:], AF.Identity, [xt.b, st.b], [xh.b], scale=st.t[:, 2:3])
            for kc in range(32):
                p = pr.next()
                pb = PSB[PS.index(p)]
                for j in range(4):
                    tr(pb[:, j * 128:(j + 1) * 128], xhs[j].t[:, kc * 128:(kc + 1) * 128], ident, [xhs[j].b, cb.b], [p.b])
                dst = hT.t[:, kc, grp * 512:(grp + 1) * 512]
                if kc % 2 == 0:
                    act(dst, pb[:, 0:512], AF.Identity, [sT.b, modT.b], [hT.b, p.b],
                        scale=sT.t[:, kc:kc + 1], bias=modT.t[:, shift_cols + kc:shift_cols + kc + 1])
                else:
                    ts(DVE, dst, pb[:, 0:512], sT.t[:, kc:kc + 1], modT.t[:, shift_cols + kc:shift_cols + kc + 1],
                       ALU.mult, ALU.add, [sT.b, modT.b], [hT.b, p.b])

    epsc = A.alloc([128, 1], F32, "epsc")
    S.emit(DVE, lambda e: e.memset(epsc.t[:], EPS), [], [epsc.b])
    onec = A.alloc([128, 1], F32, "onec")
    bfg = A.alloc([16, 1], F32, "bfg")
    dma(SP, bfg.t[:], bfg_d, [], [bfg.b])
    S.emit(DVE, lambda e: e.memset(onec.t[:], 1.0), [], [onec.b])
    m0 = A.mark()
    hT = A.alloc([128, 32, TOWN], BF16, "hT")
    mR2 = A.mark()

    if phases >= 2:
        def proj_phase(half):
            tokc0 = half * TOWN
            S.barrier()
            build_hT(xc, tokc0, TOWN, hT, 0, s1T, mR2)
            if half == 1:
                dump("hT", hT.t[:], [128, 32, TOWN], [hT.b], BF16)
            S.barrier()
            A.reset(mR2)
            ropeC = A.alloc([128, TOWN], F32, "ropeC")
            ropeS = A.alloc([128, TOWN], F32, "ropeS")
            dma(SP, ropeC.t[:], ropeC_d[:, tokc0:tokc0 + TOWN], [], [ropeC.b])
            dma(SP, ropeS.t[:], ropeS_d[:, tokc0:tokc0 + TOWN], [], [ropeS.b])
            NB = 3
            sqr = Ring([A.alloc([128, 512], BF16, f"sq{i}") for i in range(NB)])
            lnr = Ring([A.alloc([128, 512], F32, f"ln{i}") for i in range(NB)])
            rsr = Ring([A.alloc([128, 512], F32, f"rs{i}") for i in range(NB)])
            yfr = Ring([A.alloc([128, 512], F32, f"yf{i}") for i in range(NB)])
            ybr = Ring([A.alloc([128, 512], BF16, f"yb{i}") for i in range(NB)])
            t1r = Ring([A.alloc([32, 512], F32, f"t1{i}") for i in range(NB)])
            t2r = Ring([A.alloc([32, 512], F32, f"t2{i}") for i in range(NB)])
            vtr = Ring([A.alloc([128, 4, 128], BF16, f"vt{i}") for i in range(NB)])
            gsr = Ring([A.alloc([48, 512], F32, f"gs{i}") for i in range(2)])
            Rm = Ring(PS[0:4])
            Rx = Ring(PS[4:8])
            src = lambda col0: [dict(W=w_in, K=D, col0=col0, inT=hT, kc_off=0)]

            def epi_raw(dst, g_dst):
                def f(ci, ncol, outs):
                    for (p, t0, tn) in outs[0]:
                        yb = ybr.next()
                        cp(ACT, yb.t[:, 0:tn], p.t[:, 0:tn], [], [yb.b, p.b])
                        tq = (tokc0 + t0) // 512
                        dma(SP, dst[ci, :, tokc0 + t0:tokc0 + t0 + tn], yb.t[:, 0:tn], [yb.b], [g_dst[ci][tq]])
                return f

            def epi_norm(dst, g_dst, gi, rope, ctx_tok):
                def f(ci, ncol, outs):
                    for (p, t0, tn) in outs[0]:
                        sq, ln, rs, yb = sqr.next(), lnr.next(), rsr.next(), ybr.next()
                        px = Rx.next()
                        act(sq.t[:], p.t[:], AF.Square, [], [sq.b, p.b])
                        mm(px.t[:], onesb, sq.t[:], True, True, [sq.b, cb.b], [px.b])
                        act(ln.t[:], px.t[:], AF.Ln, [epsc.b], [ln.b, px.b], scale=1.0 / 128, bias=epsc.t[:, 0:1])
                        act(rs.t[:], ln.t[:], AF.Exp, [ln.b], [rs.b], scale=-0.5)
                        d0 = (tokc0 if ctx_tok else 0) + t0
                        tq = d0 // 512
                        if not rope:
                            stt(yb.t[:], p.t[:], qkg.t[:, gi:gi + 1], rs.t[:], ALU.mult, ALU.mult,
                                [qkg.b, rs.b], [yb.b, p.b])
                        else:
                            yf, t1, t2 = yfr.next(), t1r.next(), t2r.next()
                            px2 = Rx.next()
                            stt(yf.t[:], p.t[:], qkg.t[:, gi:gi + 1], rs.t[:], ALU.mult, ALU.mult,
                                [qkg.b, rs.b], [yf.b, p.b])
                            cp(ACT, yb.t[:], yf.t[:], [yf.b], [yb.b])
                            mm(px2.t[0:32, :], permT, yb.t[:], True, True, [yb.b, cb.b], [px2.b])
                            tt(DVE, t2.t[:], px2.t[0:32, :], ropeS.t[0:32, t0:t0 + 512], ALU.mult,
                               [ropeS.b], [t2.b, px2.b])
                            tt(DVE, t1.t[:], yf.t[0:32, :], ropeC.t[0:32, t0:t0 + 512], ALU.mult,
                               [ropeC.b, yf.b], [t1.b])
                            tt(DVE, yb.t[0:32, :], t1.t[:], t2.t[:], ALU.add, [t1.b, t2.b], [yb.b])
                        dma(SP, dst[ci, :, d0:d0 + tn], yb.t[:, 0:tn], [yb.b], [g_dst[ci][tq]])
                return f

            def epi_vtok(dst, g_dst):
                def f(ci, ncol, outs):
                    for (p, t0, tn) in outs[0]:
                        yb, vt = ybr.next(), vtr.next()
                        px = Rx.next()
                        pxb = PSB[PS.index(px)]
                        cp(ACT, yb.t[:], p.t[:], [], [yb.b, p.b])
                        for j in range(4):
                            tr(pxb[:, j * 128:(j + 1) * 128], yb.t[:, j * 128:(j + 1) * 128], ident, [yb.b, cb.b], [px.b])
                        cp(DVE, vt.t[:], pxb[:, 0:512].rearrange("p (j d) -> p j d", j=4), [], [vt.b, px.b])
                        r0 = tokc0 + t0
                        dma(SP, dst[r0:r0 + 512, ci, :].rearrange("(j p) d -> p j d", p=128), vt.t[:], [vt.b],
                            [g_dst[ci][r0 // 512]])
                return f

            def epi_gate(ci, ncol, outs):
                for (p, t0, tn) in outs[0]:
                    gs = gsr.next()
                    px = Rx.next()
                    act(gs.t[:], p.t[0:48, :], AF.Sigmoid, [], [gs.b, p.b])
                    for j in range(4):
                        tr(px.t[:, j * 48:(j + 1) * 48], gs.t[0:48, j * 128:(j + 1) * 128], identf[0:48, 0:48],
                           [gs.b, cf.b], [px.b])
                    q0 = t0 // 128
                    cp(DVE, gate_tok.t[:, q0:q0 + 4, :], px.t[:, 0:192].rearrange("p (j c) -> p j c", j=4), [],
                       [gate_tok.b, px.b])

            def epi_f(ci, ncol, outs):
                for (p, t0, tn) in outs[0]:
                    z, ez = lnr.next(), rsr.next()
                    ts(DVE, z.t[0:16, 0:tn], p.t[0:16, 0:tn], bfg.t[:, 0:1], None, ALU.add, None, [bfg.b], [z.b, p.b])
                    act(ez.t[0:16, 0:tn], z.t[0:16, 0:tn], AF.Exp, [z.b], [ez.b], scale=-1.0)
                    act(lsp.t[:, tokc0 + t0:tokc0 + t0 + tn], ez.t[0:16, 0:tn], AF.Ln, [ez.b, onec.b], [lsp.b],
                        scale=1.0, bias=onec.t[0:16, 0:1])

            def epi_merge(ci, ncol, outs):
                for (p, t0, tn) in outs[0]:
                    yb = ybr.next()
                    act(yb.t[:], p.t[:], AF.Sigmoid, [], [yb.b, p.b])
                    dma(SP, d_mg[ci, :, t0:t0 + tn], yb.t[:, 0:tn], [yb.b], [g_mg[ci][t0 // 512]])

            KV0 = 2048
            FX0 = 5168
            if half == 1:
                gemm(src(0), 2048, TOWN, [Rm], epi_norm(d_q, g_q, 0, True, False))
            gemm(src(KV0 + 0), 512, TOWN, [Rm], epi_raw(d_kc, g_kc))
            gemm(src(KV0 + 512), 512, TOWN, [Rm], epi_raw(d_vc, g_vc))
            gemm(src(KV0 + 1024), 512, TOWN, [Rm], epi_norm(d_ks, g_ks, 1, True, True))
            gemm(src(KV0 + 1536), 512, TOWN, [Rm], epi_vtok(d_vs, g_vs))
            gemm(src(KV0 + 2048), 512, TOWN, [Rm], epi_norm(d_kw, g_kw, 1, True, True))
            gemm(src(KV0 + 2560), 512, TOWN, [Rm], epi_vtok(d_vw, g_vw))
            if half == 1:
                gemm(src(5120), 48, TOWN, [Rm], epi_gate)
                gemm(src(FX0), 2048, TOWN, [Rm], epi_norm(d_fq, g_fq, 2, False, False))
            gemm(src(FX0 + 2048), 2048, TOWN, [Rm], epi_norm(d_fk, g_fk, 3, False, True))
            gemm(src(FX0 + 4096), 2048, TOWN, [Rm], epi_vtok(d_fv, g_fv))
            gemm(src(11312), 16, TOWN, [Rm], epi_f)
            if half == 1:
                gemm(src(11328), 8192, TOWN, [Rm], epi_merge)

        proj_phase(0)
        proj_phase(1)
        S.barrier()
        if "qn" in dbg:
            dump_d = nc.dram_tensor("dbg_qn", [16, 128, TOWN], BF16, kind="ExternalOutput").ap()
            b_ = Buf()
            dma(SP, dump_d, d_q, [x for r in g_q for x in r], [b_])
            dbg_out["qn"] = b_
        if "fk" in dbg:
            dump_d = nc.dram_tensor("dbg_fk", [16, 128, TCTX], BF16, kind="ExternalOutput").ap()
            b_ = Buf()
            dma(SP, dump_d, d_fk, [x for r in g_fk for x in r], [b_])
            dbg_out["fk"] = b_
        if "fv" in dbg:
            dump_d = nc.dram_tensor("dbg_fv", [TCTX, 16, 128], BF16, kind="ExternalOutput").ap()
            b_ = Buf()
            dma(SP, dump_d, d_fv, [x for r in g_fv for x in r], [b_])
            dbg_out["fv"] = b_
        if "ks" in dbg:
            dump_d = nc.dram_tensor("dbg_ks", [4, 128, TCTX], BF16, kind="ExternalOutput").ap()
            b_ = Buf()
            dma(SP, dump_d, d_ks, [x for r in g_ks for x in r], [b_])
            dbg_out["ks"] = b_
        dump("gate_tok", gate_tok.t[:], [128, 8, 48], [gate_tok.b])
        dump("lsp", lsp.t[:], [16, TCTX], [lsp.b])


    if phases >= 3:
        S.barrier()
        A.reset(m0)
        oT = A.alloc([128, 32, TOWN], BF16, "oT")
        m3 = A.mark()
        prevb = cf.t[:, CF_PREV:CF_PREV + 1]
        Rst = Ring(PS[0:3])
        Oacc = PS[3:7]
        Pm = PS[7]
        PmB = PSB[7]

        def all_(g, idx):
            return [b for b in g[idx]]

        if True:
            ones16 = A.alloc([16, TCTX], BF16, "ones16")
            cn = A.alloc([16, TCTX], F32, "cn")
            cntok = A.alloc([128, 16, 16], F32, "cntok")
            dg = A.alloc([16, 8, 16], F32, "dg")
            cbc = A.alloc([128, 8, 16], F32, "cbc")
            fbias = A.alloc([128, 8, 16, 16], F32, "fbias")
            S.emit(DVE, lambda e: e.memset(ones16.t[:], 1.0), [], [ones16.b])
            S.emit(DVE, lambda e: e.tensor_tensor_scan(out=cn.t[:], data0=ones16.t[:], data1=lsp.t[:], initial=0.0,
                                                       op0=ALU.mult, op1=ALU.add), [ones16.b, lsp.b], [cn.b])
            for kt in range(16):
                tr(Pm.t[:, kt * 16:(kt + 1) * 16], cn.t[0:16, kt * 128:(kt + 1) * 128],
                   cf.t[0:16, CF_ID16:CF_ID16 + 16], [cn.b, cf.b], [Pm.b])
            cp(DVE, cntok.t[:], Pm.t[:, 0:256].rearrange("p (k h) -> p k h", k=16), [], [cntok.b, Pm.b])
            cend = cn.t[0:16, TOWN + 127:TCTX:128]
            tt(DVE, dg.t[:], cend.unsqueeze(2).broadcast_to([16, 8, 16]),
               cf.t[0:16, CF_ID16:CF_ID16 + 16].unsqueeze(1).broadcast_to([16, 8, 16]), ALU.mult,
               [cn.b, cf.b], [dg.b])
            mm(Pm.t[:, 0:128], cf.t[0:16, CF_ONES:CF_ONES + 128], dg.t[:].rearrange("p a b -> p (a b)"), True, True,
               [dg.b, cf.b], [Pm.b])
            cp(DVE, cbc.t[:], Pm.t[:, 0:128].rearrange("p (a b) -> p a b", a=8), [], [cbc.b, Pm.b])
            for i in range(8):
                tt(DVE, fbias.t[:, i], cntok.t[:], cbc.t[:, i:i + 1, :].broadcast_to([128, 16, 16]), ALU.subtract,
                   [cntok.b, cbc.b], [fbias.b])
            ts(DVE, fbias.t[:, :, 0:8, :], fbias.t[:, :, 0:8, :], prevb, None, ALU.add, None, [cf.b], [fbias.b])
            dump("fbias", fbias.t[:], [128, 8, 16, 16], [fbias.b])
            mF = A.mark()
            fk4 = [A.alloc([128, TCTX], BF16, f"fk{i}") for i in range(4)]
            fq4 = [A.alloc([128, TOWN], BF16, f"fq{i}") for i in range(4)]
            fv4 = [A.alloc([128, 16, 130], BF16, f"fv{i}") for i in range(4)]
            pTr = Ring([A.alloc([128, 128], BF16, f"pT{i}") for i in range(6)])
            ofb = [A.alloc([128, 128], BF16, f"ofb{i}") for i in range(4)]
            rDr = Ring([A.alloc([128, 2], F32, f"rD{i}") for i in range(4)])
            for v in fv4:
                S.emit(DVE, lambda e, v=v: e.memset(v.t[:, :, 128:130], 1.0), [], [v.b])
            Rst2 = Ring(PS[0:2])
            Rof = Ring(PS[2:6])
            for hg in range(4):
                for j in range(4):
                    h = hg * 4 + j
                    dma(SP, fk4[j].t[:], d_fk[h], all_(g_fk, h), [fk4[j].b])
                    dma(SP, fq4[j].t[:], d_fq[h], all_(g_fq, h), [fq4[j].b])
                    dma(SP, fv4[j].t[:, :, 0:128], d_fv[:, h, :].rearrange("(k p) d -> p k d", p=128),
                        all_(g_fv, h), [fv4[j].b])
                for qt in range(8):
                    qc = 8 + qt
                    for j in range(4):
                        h = hg * 4 + j
                        of = Rof.next()
                        for g0 in range(0, qc + 1, 4):
                            kts = list(range(g0, min(g0 + 4, qc + 1)))
                            st = Rst2.next()
                            for idx, kt in enumerate(kts):
                                mm(st.t[:, idx * 128:(idx + 1) * 128], fk4[j].t[:, kt * 128:(kt + 1) * 128],
                                   fq4[j].t[:, qt * 128:(qt + 1) * 128], idx == 0, kt != qc,
                                   [fk4[j].b, fq4[j].b], [st.b])
                                if kt == qc:
                                    mm(st.t[:, idx * 128:(idx + 1) * 128], ident, mdiag4[:, 0:128], False, True,
                                       [cb.b], [st.b])
                            for idx, kt in enumerate(kts):
                                pT = pTr.next()
                                act(pT.t[:], st.t[:, idx * 128:(idx + 1) * 128], AF.Exp, [fbias.b], [pT.b, st.b],
                                    scale=SCALE, bias=fbias.t[:, qt, kt, h:h + 1])
                                mm(of.t[:, 0:129], pT.t[:], fv4[j].t[:, kt, 0:129], kt == 0, kt == qc,
                                   [pT.b, fv4[j].b], [of.b])
                        rD = rDr.next()
                        S.emit(DVE, lambda e, rD=rD, of=of: e.reciprocal(out=rD.t[:, 0:1], in_=of.t[:, 128:129]),
                               [], [rD.b, of.b])
                        ts(DVE, ofb[j].t[:], of.t[:, 0:128], rD.t[:, 0:1], None, ALU.mult, None, [rD.b],
                           [ofb[j].b, of.b])
                        if j % 2 == 1:
                            mod_step(1)
                    for j in range(4):
                        tr(PmB[:, j * 128:(j + 1) * 128], ofb[j].t[:], ident, [ofb[j].b, cb.b], [Pm.b])
                    cp(DVE, oT.t[:, 16 + hg * 4:16 + hg * 4 + 4, qt * 128:(qt + 1) * 128],
                       PmB[:, 0:512].rearrange("p (j t) -> p j t", j=4), [], [oT.b, Pm.b])
            dump("ofoxT", oT.t[:, 16:32, :], [128, 16, TOWN], [oT.b], BF16)

        if phases >= 4:
            S.barrier()
            A.reset(m3)
            Rst = Ring(PS[0:2])
            mod_bank[0] = PS[2]
            w1k = A.alloc([128, 32, 128], BF16, "w1k")
            w1v = A.alloc([128, 32, 128], BF16, "w1v")
            w2k = A.alloc([128, 128], BF16, "w2k")
            w2v = A.alloc([128, 128], BF16, "w2v")
            posT = A.alloc([128, 64], BF16, "posT")
            rCc = A.alloc([128, 128], F32, "rCc")
            rSc = A.alloc([128, 128], F32, "rSc")
            cbk = A.alloc([128, 2], F32, "cbk")
            dma(POOL, w1k.t[:], w_ck1.rearrange("(l d) j -> d l j", d=128), [], [w1k.b])
            dma(POOL, w1v.t[:], w_cv1.rearrange("(l d) j -> d l j", d=128), [], [w1v.b])
            dma(POOL, w2k.t[:], w_ck2, [], [w2k.b])
            dma(POOL, w2v.t[:], w_cv2, [], [w2v.b])
            dma(POOL, posT.t[:], cposT_d, [], [posT.b])
            dma(SP, rCc.t[:], ropeCc_d, [], [rCc.b])
            dma(SP, rSc.t[:], ropeSc_d, [], [rSc.b])
            for wi, w1 in enumerate((w1k, w1v)):
                for l in range(32):
                    mm(Pm.t[:, 0:1], w1.t[:, l, :], posT.t[:, wi * 32 + l:wi * 32 + l + 1], l == 0, l == 31,
                       [w1.b, posT.b], [Pm.b])
                cp(DVE, cbk.t[:, wi:wi + 1], Pm.t[:, 0:1], [], [cbk.b, Pm.b])
            ksT = A.alloc([128, TCTX], BF16, "ksT")
            kwT = A.alloc([128, TCTX], BF16, "kwT")
            kcT = A.alloc([128, TCTX], BF16, "kcT")
            vcT = A.alloc([128, TCTX], BF16, "vcT")
            vs = A.alloc([128, 16, 130], BF16, "vs")
            vw = A.alloc([128, 16, 130], BF16, "vw")
            q4 = A.alloc([128, 4, TOWN], BF16, "q4")
            kcmpT = A.alloc([128, 128], BF16, "kcmpT")
            vcmp = A.alloc([128, 162], BF16, "vcmp")
            ca = A.alloc([128, 128], F32, "ca")
            ca2 = A.alloc([128, 128], F32, "ca2")
            cu = A.alloc([128, 128], F32, "cu")
            gl = A.alloc([128, 128], BF16, "gl")
            csq = A.alloc([128, 128], BF16, "csq")
            cln = A.alloc([128, 128], F32, "cln")
            cyf = A.alloc([128, 128], F32, "cyf")
            ct1 = A.alloc([32, 128], F32, "ct1")
            ct2 = A.alloc([32, 128], F32, "ct2")
            pT4 = Ring([A.alloc([128, 512], BF16, f"pq{i}") for i in range(3)])
            oaccS = A.alloc([128, 4, 128], F32, "oaccS")
            onsa = A.alloc([128, 4, 128], BF16, "onsa")
            sm = Ring([A.alloc([128, 16], F32, f"sm{i}") for i in range(8)])
            imp = A.alloc([128, 32], F32, "imp")
            sc = A.alloc([128, 32], F32, "sc")
            sc2 = A.alloc([128, 32], F32, "sc2")
            m8a = A.alloc([128, 8], F32, "m8a")
            m8b = A.alloc([128, 8], F32, "m8b")
            selm = A.alloc([128, 32], F32, "selm")
            selv = A.alloc([128, 32], F32, "selv")
            sbb = A.alloc([128, 32], BF16, "sbb")
            selT = A.alloc([32, 128], BF16, "selT")
            S.emit(DVE, lambda e: e.memset(vs.t[:, :, 128:130], 1.0), [], [vs.b])
            S.emit(DVE, lambda e: e.memset(vw.t[:, :, 128:130], 1.0), [], [vw.b])
            S.emit(DVE, lambda e: e.memset(vcmp.t[:, 128:129], 1.0), [], [vcmp.b])
            cp(DVE, vcmp.t[:, 129:161], cb.t[:, CB_OV:CB_OV + 32], [cb.b], [vcmp.b])
            S.emit(DVE, lambda e: e.memset(kcmpT.t[:], 0.0), [], [kcmpT.b])
            cmpb = cb.t[:, CB_CMPB:CB_CMPB + 1024]
            ftab = cf.t[:, CF_FTAB:CF_FTAB + 256]

            def weight_cols(p, r, gcol, smt):
                ts(DVE, smt.t[:, 4 + r:5 + r], p.t[:, 128:129], 1e-30, None, ALU.max, None, [], [smt.b, p.b])
                S.emit(DVE, lambda e: e.reciprocal(out=smt.t[:, 8 + r:9 + r], in_=smt.t[:, 4 + r:5 + r]), [], [smt.b])
                tt(DVE, smt.t[:, r:r + 1], smt.t[:, 8 + r:9 + r], gcol, ALU.mult, [gate_tok.b], [smt.b])

            for g in range(4):
                dma(SP, ksT.t[:], d_ks[g], all_(g_ks, g), [ksT.b])
                dma(SP, kwT.t[:], d_kw[g], all_(g_kw, g), [kwT.b])
                dma(SP, kcT.t[:], d_kc[g], all_(g_kc, g), [kcT.b])
                dma(SP, vcT.t[:], d_vc[g], all_(g_vc, g), [vcT.b])
                dma(SP, vs.t[:, :, 0:128], d_vs[:, g, :].rearrange("(k p) d -> p k d", p=128), all_(g_vs, g), [vs.b])
                dma(SP, vw.t[:, :, 0:128], d_vw[:, g, :].rearrange("(k p) d -> p k d", p=128), all_(g_vw, g), [vw.b])
                dma(SP, q4.t[:], d_q[4 * g:4 * g + 4].rearrange("h d t -> d h t"),
                    [b for h in range(4 * g, 4 * g + 4) for b in g_q[h]], [q4.b])
                for wi, (w1, w2, xT) in enumerate(((w1k, w2k, kcT), (w1v, w2v, vcT))):
                    hp = Rst.next()
                    for l in range(32):
                        mm(hp.t[:, 0:127], w1.t[:, l, :], xT.t[:, l:l + 16 * 126 + 1:16], l == 0, l == 31,
                           [w1.b, xT.b], [hp.b])
                    ts(DVE, ca.t[:, 0:127], hp.t[:, 0:127], cbk.t[:, wi:wi + 1], None, ALU.add, None, [cbk.b],
                       [ca.b, hp.b])
                    tt(DVE, ca2.t[:, 0:127], ca.t[:, 0:127], ca.t[:, 0:127], ALU.mult, [ca.b], [ca2.b])
                    ts(DVE, ca2.t[:, 0:127], ca2.t[:, 0:127], 0.044715, 1.0, ALU.mult, ALU.add, [], [ca2.b])
                    tt(DVE, cu.t[:, 0:127], ca2.t[:, 0:127], ca.t[:, 0:127], ALU.mult, [ca2.b, ca.b], [cu.b])
                    act(cu.t[:, 0:127], cu.t[:, 0:127], AF.Sigmoid, [], [cu.b], scale=1.5957691216)
                    tt(DVE, gl.t[:, 0:127], cu.t[:, 0:127], ca.t[:, 0:127], ALU.mult, [cu.b, ca.b], [gl.b])
                    kp = Rst.next()
                    if wi == 0:
                        mm(kp.t[:, 0:127], w2.t[:], gl.t[:, 0:127], True, True, [w2.b, gl.b], [kp.b])
                        act(csq.t[:, 0:127], kp.t[:, 0:127], AF.Square, [], [csq.b, kp.b])
                        mm(Pm.t[:, 0:127], onesb, csq.t[:, 0:127], True, True, [csq.b, cb.b], [Pm.b])
                        act(cln.t[:, 0:127], Pm.t[:, 0:127], AF.Ln, [epsc.b], [cln.b, Pm.b], scale=1.0 / 128,
                            bias=epsc.t[:, 0:1])
                        act(cln.t[:, 0:127], cln.t[:, 0:127], AF.Exp, [], [cln.b], scale=-0.5)
                        stt(cyf.t[:, 0:127], kp.t[:, 0:127], qkg.t[:, 1:2], cln.t[:, 0:127], ALU.mult, ALU.mult,
                            [qkg.b, cln.b], [cyf.b, kp.b])
                        cp(ACT, kcmpT.t[:, 0:127], cyf.t[:, 0:127], [cyf.b], [kcmpT.b])
                        mm(Pm.t[0:32, 0:127], permT, kcmpT.t[:, 0:127], True, True, [kcmpT.b, cb.b], [Pm.b])
                        tt(DVE, ct2.t[:, 0:127], Pm.t[0:32, 0:127], rSc.t[0:32, 0:127], ALU.mult, [rSc.b],
                           [ct2.b, Pm.b])
                        tt(DVE, ct1.t[:, 0:127], cyf.t[0:32, 0:127], rCc.t[0:32, 0:127], ALU.mult, [rCc.b, cyf.b],
                           [ct1.b])
                        tt(DVE, kcmpT.t[0:32, 0:127], ct1.t[:, 0:127], ct2.t[:, 0:127], ALU.add, [ct1.b, ct2.b],
                           [kcmpT.b])
                    else:
                        mm(kp.t[0:127, 0:128], gl.t[:, 0:127], w2.t[:], True, True, [w2.b, gl.b], [kp.b])
                        cp(DVE, vcmp.t[0:127, 0:128], kp.t[0:127, 0:128], [], [vcmp.b, kp.b])
                for qt in range(8):
                    qc = 8 + qt
                    q4s = q4.t[:, :, qt * 128:(qt + 1) * 128]
                    st4 = lambda st, np_=128: st.t[0:np_, 0:512].rearrange("p (r t) -> p r t", r=4)
                    st = Rst.next()
                    mm(st4(st, 127), kcmpT.t[:, 0:127], q4s, True, False, [kcmpT.b, q4.b], [st.b])
                    mm(st4(st, 127), ident[0:127, 0:127],
                       cmpb[0:127, qt * 128:(qt + 1) * 128].unsqueeze(1).broadcast_to([127, 4, 128]), False, True,
                       [cb.b], [st.b])
                    pT = pT4.next()
                    act(pT.t[0:127, :], st.t[0:127, 0:512], AF.Exp, [], [pT.b, st.b], scale=SCALE)
                    smt = sm.next()
                    for r in range(4):
                        mm(Oacc[r].t[:, 0:161], pT.t[0:127, r * 128:(r + 1) * 128], vcmp.t[0:127, 0:161], True, True,
                           [pT.b, vcmp.b], [Oacc[r].b])
                    for r in range(4):
                        h = 4 * g + r
                        p = Oacc[r]
                        weight_cols(p, r, gate_tok.t[:, qt, 3 * h:3 * h + 1], smt)
                        if r == 0:
                            ts(DVE, imp.t[:], p.t[:, 129:161], smt.t[:, 8 + r:9 + r], None, ALU.mult, None, [smt.b],
                               [imp.b, p.b])
                        else:
                            stt(imp.t[:], p.t[:, 129:161], smt.t[:, 8 + r:9 + r], imp.t[:], ALU.mult, ALU.add,
                                [smt.b], [imp.b, p.b])
                        ts(DVE, oaccS.t[:, r, :], p.t[:, 0:128], smt.t[:, r:r + 1], None, ALU.mult, None, [smt.b],
                           [oaccS.b, p.b])
                    tt(DVE, sc.t[:], imp.t[:], ftab[:, qt * 32:(qt + 1) * 32], ALU.add, [imp.b, cf.b], [sc.b])
                    S.emit(DVE, lambda e: e.max(out=m8a.t[:], in_=sc.t[:]), [sc.b], [m8a.b])
                    S.emit(DVE, lambda e: e.match_replace(out=sc2.t[:], in_to_replace=m8a.t[:], in_values=sc.t[:],
                                                          imm_value=-1.0e9), [sc.b, m8a.b], [sc2.b])
                    S.emit(DVE, lambda e: e.max(out=m8b.t[:], in_=sc2.t[:]), [sc2.b], [m8b.b])
                    ts(DVE, selm.t[:], sc.t[:], m8b.t[:, 7:8], None, ALU.is_ge, None, [sc.b, m8b.b], [selm.b])
                    ts(DVE, selv.t[:], sc.t[:], -0.5 * BIG, None, ALU.is_gt, None, [sc.b], [selv.b])
                    tt(DVE, selm.t[:], selm.t[:], selv.t[:], ALU.mult, [selv.b], [selm.b])
                    ts(DVE, sbb.t[:], selm.t[:], -NEG, NEG, ALU.mult, ALU.add, [selm.b], [sbb.b])
                    if g == 0 and qt == 7:
                        dump("selm", selm.t[:], [128, 32], [selm.b])
                        dump("imp", imp.t[:], [128, 32], [imp.b])
                    selT4 = selT.t[0:32, :].unsqueeze(1).broadcast_to([32, 4, 128])
                    for kt in range(qc - 4, qc + 1):
                        st = Rst.next()
                        last_qk = not (kt == qc - 4 or kt == qc)
                        mm(st4(st), kwT.t[:, kt * 128:(kt + 1) * 128], q4s, True, last_qk, [kwT.b, q4.b], [st.b])
                        if kt == qc - 4:
                            mm(st.t[:, 0:512], ident, mwin4, False, True, [cb.b], [st.b])
                        if kt == qc:
                            mm(st.t[:, 0:512], ident, mdiag4, False, True, [cb.b], [st.b])
                        pT = pT4.next()
                        if kt < 8:
                            act(pT.t[:], st.t[:, 0:512], AF.Exp, [cf.b], [pT.b, st.b], scale=SCALE, bias=prevb)
                        else:
                            act(pT.t[:], st.t[:, 0:512], AF.Exp, [], [pT.b, st.b], scale=SCALE)
                        for r in range(4):
                            mm(Oacc[r].t[:, 0:129], pT.t[:, r * 128:(r + 1) * 128], vw.t[:, kt, 0:129], kt == qc - 4,
                               kt == qc, [pT.b, vw.b], [Oacc[r].b])
                    smt = sm.next()
                    for r in range(4):
                        h = 4 * g + r
                        p = Oacc[r]
                        weight_cols(p, r, gate_tok.t[:, qt, 3 * h + 2:3 * h + 3], smt)
                        stt(oaccS.t[:, r, :], p.t[:, 0:128], smt.t[:, r:r + 1], oaccS.t[:, r, :], ALU.mult, ALU.add,
                            [smt.b], [oaccS.b, p.b])
                    tr(PmB[0:32, 0:128], sbb.t[:, 0:32], ident, [sbb.b, cb.b], [Pm.b])
                    cp(DVE, selT.t[:], PmB[0:32, 0:128], [], [selT.b, Pm.b])
                    for kt in range(qc + 1):
                        st = Rst.next()
                        mm(st4(st), ksT.t[:, kt * 128:(kt + 1) * 128], q4s, True, False, [ksT.b, q4.b], [st.b])
                        mm(st4(st), cb.t[0:32, CB_ESEL + kt * 128:CB_ESEL + (kt + 1) * 128], selT4, False, kt != qc,
                           [selT.b, cb.b], [st.b])
                        if kt == qc:
                            mm(st.t[:, 0:512], ident, mdiag4, False, True, [cb.b], [st.b])
                        pT = pT4.next()
                        act(pT.t[:], st.t[:, 0:512], AF.Exp, [], [pT.b, st.b], scale=SCALE)
                        for r in range(4):
                            mm(Oacc[r].t[:, 0:129], pT.t[:, r * 128:(r + 1) * 128], vs.t[:, kt, 0:129], kt == 0,
                               kt == qc, [pT.b, vs.b], [Oacc[r].b])
                    smt = sm.next()
                    for r in range(4):
                        h = 4 * g + r
                        p = Oacc[r]
                        weight_cols(p, r, gate_tok.t[:, qt, 3 * h + 1:3 * h + 2], smt)
                        stt(onsa.t[:, r, :], p.t[:, 0:128], smt.t[:, r:r + 1], oaccS.t[:, r, :], ALU.mult, ALU.add,
                            [smt.b, oaccS.b], [onsa.b, p.b])
                    mod_step(2)
                    for r in range(4):
                        tr(PmB[:, r * 128:(r + 1) * 128], onsa.t[:, r, :], ident, [onsa.b, cb.b], [Pm.b])
                    cp(DVE, oT.t[:, 4 * g:4 * g + 4, qt * 128:(qt + 1) * 128],
                       PmB[:, 0:512].rearrange("p (j t) -> p j t", j=4), [], [oT.b, Pm.b])
            dump("onsaT", oT.t[:, 0:16, :], [128, 16, TOWN], [oT.b], BF16)

    if phases >= 5:
        for _ in mod_bg:
            pass
        stt(s2T.t[:], modT.t[:, 128:160], 1.0, gT.t[:, 32:64], ALU.add, ALU.mult, [modT.b, gT.b], [s2T.b])
        dump("modT", modT.t[:], [128, 192], [modT.b])
        S.barrier()
        A.reset(m3)
        yT = A.alloc([128, 32, TOWN], BF16, "yT")
        m5 = A.mark()
        sgr = Ring([A.alloc([128, 512], BF16, f"sg{i}") for i in range(4)])
        t1r = Ring([A.alloc([128, 512], F32, f"u1{i}") for i in range(2)])
        t2r = Ring([A.alloc([128, 512], F32, f"u2{i}") for i in range(2)])

        def epi_up(ci, ncol, outs):
            for ti in range(2):
                pa, t0, tn = outs[0][ti]
                pb_, _, _ = outs[1][ti]
                sa, sb_ = sgr.next(), sgr.next()
                t1, t2 = t1r.next(), t2r.next()
                dma(SP, sa.t[:], d_mg[ci, :, t0:t0 + 512], [g_mg[ci][ti]], [sa.b])
                dma(SP, sb_.t[:], d_mg[32 + ci, :, t0:t0 + 512], [g_mg[32 + ci][ti]], [sb_.b])
                tt(DVE, t1.t[:], pa.t[:], sa.t[:], ALU.mult, [sa.b], [t1.b, pa.b])
                tt(DVE, t2.t[:], pb_.t[:], sb_.t[:], ALU.mult, [sb_.b], [t2.b, pb_.b])
                tt(POOL, yT.t[:, ci, t0:t0 + 512], t1.t[:], t2.t[:], ALU.add, [t1.b, t2.b], [yT.b])

        gemm([dict(W=w_upa, K=2048, col0=0, inT=oT, kc_off=0), dict(W=w_upb, K=2048, col0=0, inT=oT, kc_off=16)],
             D, TOWN, [Ring(PS[0:4]), Ring(PS[4:8])], epi_up)
        dump("yT", yT.t[:], [128, 32, TOWN], [yT.b], BF16)

    if phases >= 6:
        S.barrier()
        A.reset(m5)
        tmpr = Ring([A.alloc([128, 512], F32, f"tm{i}") for i in range(2)])
        xpr = Ring([A.alloc([128, 4, 128], F32, f"xp{i}") for i in range(2)])
        x1r = Ring([A.alloc([128, 4, 128], F32, f"x1{i}") for i in range(2)])
        Rz = Ring(PS[4:8])

        def epi_res(gate_col, src, row0, src_grid):
            def f(ci, ncol, outs):
                for ti, (p, t0, tn) in enumerate(outs[0]):
                    tmp, xp, x1 = tmpr.next(), xpr.next(), x1r.next()
                    pz = Rz.next()
                    act(tmp.t[:], p.t[:], AF.Identity, [modT.b], [tmp.b, p.b],
                        scale=modT.t[:, gate_col + ci:gate_col + ci + 1])
                    for j in range(4):
                        tr(pz.t[:, j * 128:(j + 1) * 128], tmp.t[:, j * 128:(j + 1) * 128], identf, [tmp.b, cf.b],
                           [pz.b])
                    dma(SP, xp.t[:], src[row0 + t0:row0 + t0 + 512, ci * 128:(ci + 1) * 128].rearrange(
                        "(j p) c -> p j c", p=128), [src_grid[ci][ti]] if src_grid else [], [xp.b])
                    tt(DVE, x1.t[:], pz.t[:, 0:512].rearrange("p (j c) -> p j c", j=4), xp.t[:], ALU.add, [xp.b],
                       [x1.b, pz.b])
                    dma(SP, out[t0:t0 + 512, ci * 128:(ci + 1) * 128].rearrange("(j p) c -> p j c", p=128), x1.t[:],
                        [x1.b], [g_out[ci][ti]])
            return f

        gemm([dict(W=w_out, K=D, col0=0, inT=yT, kc_off=0)], D, TOWN, [Ring(PS[0:4])],
             epi_res(64, xc, TOWN, None))

    if phases >= 7:
        S.barrier()
        h2T = TB(oT.t, oT.b)
        build_hT(out, 0, TOWN, h2T, 96, s2T, m3)
        S.barrier()
        A.reset(m3)
        uT = A.alloc([128, 32, TOWN], BF16, "uT")
        rr = Ring([A.alloc([128, 512], F32, f"rr{i}") for i in range(2)])
        tmpr = Ring([A.alloc([128, 512], F32, f"tn{i}") for i in range(1)])
        xpr = Ring([A.alloc([128, 4, 128], F32, f"xq{i}") for i in range(2)])
        x1r = Ring([A.alloc([128, 4, 128], F32, f"xr{i}") for i in range(1)])
        Rz = Ring(PS[4:8])

        def epi_ff1(ci, ncol, outs):
            for (p, t0, tn) in outs[0]:
                r = rr.next()
                act(r.t[:], p.t[:], AF.Relu, [], [r.b, p.b])
                tt(DVE, uT.t[:, ci, t0:t0 + 512], r.t[:], r.t[:], ALU.mult, [r.b], [uT.b])

        for fc in range(4):
            gemm([dict(W=w_ff1, K=D, col0=fc * 4096, inT=h2T, kc_off=0)], 4096, TOWN, [Ring(PS[0:4])], epi_ff1)
            gemm([dict(W=w_ff2[fc * 4096:(fc + 1) * 4096, :], K=4096, col0=0, inT=uT, kc_off=0)], D, TOWN,
                 [Ring(PS[0:4])], epi_res(160, out, 0, g_out))

    finals = [b for row in g_out for b in row] + list(dbg_out.values())
    if phases < 9:
        z = A.alloc([128, 512], F32, "zero")
        S.emit(DVE, lambda e: e.memset(z.t[:], 0.0), [], [z.b])
        dma(SP, out[0:128, 0:512], z.t[:], [z.b], [g_out[0][0]])
    S.finish(finals)
    return nc


def _consts(half):
    f32 = np.float32
    cbp = np.zeros((128, NCB), f32)
    cbp[:, CB_ID:CB_ID + 128] = np.eye(128, dtype=f32)
    cbp[:, CB_ONES:CB_ONES + 128] = 1.0
    for m in range(32):
        srcm = m + 16 if m < 16 else m - 16
        cbp[srcm, CB_PERM + m] = 1.0
    s = np.arange(128)[:, None]
    t = np.arange(128)[None, :]
    md = np.where(s <= t, 0.0, NEG).astype(f32)
    mw = np.where(s > t, 0.0, NEG).astype(f32)
    cbp[:, CB_MDIAG:CB_MDIAG + 512] = np.tile(md, (1, 4))
    cbp[:, CB_MWIN:CB_MWIN + 512] = np.tile(mw, (1, 4))
    for kt in range(16):
        for ss in range(128):
            cbp[2 * kt + (1 if ss >= 64 else 0), CB_ESEL + kt * 128 + ss] = 1.0
    n = np.arange(127)
    ci = n[:, None] * 16
    sj = np.arange(32)[None, :] * 64
    cbp[:127, CB_OV:CB_OV + 32] = ((ci < sj + 64) & (ci + 32 > sj)).astype(f32)
    tctx = TOWN + np.arange(TOWN)[None, :]
    valid = (n[:, None] >= 64) if half == 0 else np.ones((127, 1), bool)
    cbp[:127, CB_CMPB:CB_CMPB + 1024] = np.where(valid & (16 * n[:, None] + 31 <= tctx), 0.0, NEG)
    cbp[127, CB_CMPB:CB_CMPB + 1024] = NEG

    cfp = np.zeros((128, NCF), f32)
    cfp[:, CF_ID:CF_ID + 128] = np.eye(128, dtype=f32)
    j0 = 16 * (1 - half)
    for qt in range(8):
        tc = TOWN + qt * 128 + np.arange(128)
        cur = tc // 64
        jj = np.arange(32)[None, :]
        forced = (jj == j0) | (jj == cur[:, None]) | ((jj == cur[:, None] - 1) & (cur[:, None] - 1 >= j0))
        causal = (jj >= j0) & (jj <= cur[:, None])
        cfp[:, CF_FTAB + qt * 32:CF_FTAB + (qt + 1) * 32] = np.where(forced, BIG, np.where(causal, 0.0, -BIG))
    cfp[:, CF_PREV] = 0.0 if half == 1 else NEG
    cfp[:, CF_ONES:CF_ONES + 128] = 1.0
    cfp[:16, CF_ID16:CF_ID16 + 16] = np.eye(16, dtype=f32)

    inv = (500000.0 ** (-np.arange(16, dtype=f32) / 16)).astype(f32)

    def rope_tabs(pos):
        pos = pos.astype(f32)
        ang = pos[None, :] * inv[:, None]
        c, s_ = np.cos(ang).astype(f32), np.sin(ang).astype(f32)
        C = np.ones((128, pos.shape[0]), f32)
        Sg = np.zeros((128, pos.shape[0]), f32)
        C[0:16] = c
        C[16:32] = c
        Sg[0:16] = -s_
        Sg[16:32] = s_
        return C, Sg

    pos_ctx = np.arange(TCTX) - TOWN * (1 - half)
    ropeC, ropeS = rope_tabs(pos_ctx)
    endp = np.arange(128) * 16 + 31 - TOWN * (1 - half)
    ropeCc, ropeSc = rope_tabs(endp)
    return dict(cb=cbp, cf=cfp, ropeC=ropeC, ropeS=ropeS, ropeCc=ropeCc, ropeSc=ropeSc)


def make_in_maps(inp, cores=range(8)):
    f32 = np.float32
    A_ = lambda a: np.ascontiguousarray(np.asarray(a, dtype=f32))
    shared = dict(
        w_ada=A_(inp["w_ada"][0]), b_adaT=A_(np.asarray(inp["b_ada"][0]).reshape(192, 128).T),
        gT=A_(np.concatenate([np.asarray(inp["norm1_g"][0]).reshape(32, 128).T,
                              np.asarray(inp["norm2_g"][0]).reshape(32, 128).T], axis=1)),
        w_in=A_(inp["w_in"][0]), bfg=A_(np.asarray(inp["b_forget"][0]).reshape(16, 1)),
        qkg=A_(np.stack([np.asarray(inp["nsa_q_norm"][0]), np.asarray(inp["nsa_k_norm"][0]),
                         np.asarray(inp["fox_q_norm"][0]), np.asarray(inp["fox_k_norm"][0])], axis=1)),
        cposT=A_(np.concatenate([np.asarray(inp["cmp_pos_k"][0]).T, np.asarray(inp["cmp_pos_v"][0]).T], axis=1)),
        w_ck1=A_(inp["w_cmp_k1"][0]), w_ck2=A_(inp["w_cmp_k2"][0]),
        w_cv1=A_(inp["w_cmp_v1"][0]), w_cv2=A_(inp["w_cmp_v2"][0]),
        w_upa=A_(inp["w_up_nsa"][0]), w_upb=A_(inp["w_up_fox"][0]), w_out=A_(inp["w_out"][0]),
        w_ff1=A_(inp["w_ff1"][0]), w_ff2=A_(inp["w_ff2"][0]),
    )
    consts = [_consts(0), _consts(1)]
    x = np.asarray(inp["x"], dtype=f32)
    c = np.asarray(inp["c"], dtype=f32)
    maps = []
    for k in cores:
        b, half = k // 2, k % 2
        if half == 1:
            xcc = x[b]
        else:
            xcc = np.concatenate([np.zeros((TOWN, D), f32), x[b, :TOWN]], axis=0)
        m = dict(shared)
        m["xc"] = np.ascontiguousarray(xcc)
        m["cT"] = A_(c[b].reshape(32, 128).T)
        m.update(consts[half])
        maps.append(m)
    return maps


_NC_CACHE = {}


def kernel(**inputs):
    if "nc" not in _NC_CACHE:
        _NC_CACHE["nc"] = build_program()
    nc = _NC_CACHE["nc"]
    maps = make_in_maps(inputs)
    res = run_bass_kernel_spmd(nc, maps, core_ids=list(range(8)))
    outp = np.zeros((4, 2048, D), np.float32)
    for k in range(8):
        b, half = k // 2, k % 2
        outp[b, half * TOWN:(half + 1) * TOWN] = np.asarray(res.results[k]["out"])
    return outp
```

```python
import numpy as np
import concourse.bass as bass
import concourse.mybir as mybir
from concourse.alu_op_type import AluOpType as ALU
from concourse.bass_utils import run_bass_kernel_spmd

AF = mybir.ActivationFunctionType
F32 = mybir.dt.float32
BF16 = mybir.dt.bfloat16

D = 4096
TOWN = 1024
TCTX = 2048
DIN = 19520
DFF = 16384
SCALE = 128 ** -0.5
EPS = 1e-6
NEG = -30000.0
BIG = 1.0e4

CB_ID, CB_ONES, CB_PERM, CB_MDIAG, CB_MWIN, CB_ESEL, CB_OV, CB_CMPB = 0, 128, 256, 384, 896, 1408, 3456, 3488
NCB = 3488 + 1024
CF_ID, CF_FTAB, CF_PREV, CF_ONES, CF_ID16 = 0, 128, 384, 385, 513
NCF = 513 + 16


class Buf:
    __slots__ = ("w", "r")

    def __init__(self):
        self.w = None
        self.r = {}


class Sem:
    __slots__ = ("h", "val")

    def __init__(self, h):
        self.h = h
        self.val = 0


class Queue:
    def __init__(self, name, sem, inorder=False):
        self.name = name
        self.sem = sem
        self.ops = []
        self.waited = {}
        self.inorder = inorder
        self.ring = []
        self.ring_i = 0


class Sched:
    def __init__(self, nc, n_dma_sems=20):
        self.nc = nc
        mk = lambda n: Sem(nc.alloc_semaphore(n))
        self.pe = Queue("pe", mk("s_pe"), inorder=True)
        self.act = Queue("act", mk("s_act"))
        self.dve = Queue("dve", mk("s_dve"))
        self.pool = Queue("pool", mk("s_pool"))
        self.sp = Queue("sp", mk("s_sp"))
        self.queues = [self.pe, self.act, self.dve, self.pool, self.sp]
        for q in (self.sp, self.pool):
            q.ring = [mk(f"d_{q.name}{i}") for i in range(n_dma_sems)]
        self.n_ops = 0

    def _wait(self, q, sem, val):
        if q.waited.get(sem, 0) >= val:
            return
        q.waited[sem] = val
        q.ops.append(("w", sem, val))

    def emit(self, q, fn, reads=(), writes=(), dma=False):
        deps = {}
        for b in reads:
            if b.w is not None:
                s, v = b.w
                if deps.get(s, 0) < v:
                    deps[s] = v
        for b in writes:
            if b.w is not None:
                s, v = b.w
                if deps.get(s, 0) < v:
                    deps[s] = v
            for s, v in b.r.items():
                if deps.get(s, 0) < v:
                    deps[s] = v
        for s, v in deps.items():
            if s is q.sem and q.inorder and not dma:
                continue
            self._wait(q, s, v)
        if dma:
            sem = q.ring[q.ring_i % len(q.ring)]
            q.ring_i += 1
            if sem.val > 0:
                self._wait(q, sem, sem.val)
            sem.val += 16
            tok = (sem, sem.val)
            q.ops.append(("o", fn, sem, 16))
        else:
            q.sem.val += 1
            tok = (q.sem, q.sem.val)
            q.ops.append(("o", fn, q.sem, 1))
        for b in reads:
            if b.r.get(tok[0], 0) < tok[1]:
                b.r[tok[0]] = tok[1]
        for b in writes:
            b.w = tok
            b.r = {}
        self.n_ops += 1
        return tok

    def barrier(self):
        sems = [q.sem for q in self.queues] + [s for q in self.queues for s in q.ring]
        for q in self.queues:
            for s in sems:
                if s.val > 0 and s is not q.sem:
                    self._wait(q, s, s.val)

    def finish(self, final_bufs):
        q = self.sp
        for b in final_bufs:
            if b.w is not None:
                self._wait(q, b.w[0], b.w[1])
        nc = self.nc

        def run(queue):
            def body(e):
                for op in queue.ops:
                    if op[0] == "w":
                        e.wait_ge(op[1].h, op[2])
                    else:
                        op[1](e).then_inc(op[2].h, op[3])
            return body

        with nc.Block() as block:
            block.tensor(run(self.pe))
            block.scalar(run(self.act))
            block.vector(run(self.dve))
            block.gpsimd(run(self.pool))
            block.sync(run(self.sp))


class TB:
    __slots__ = ("t", "b")

    def __init__(self, t, b=None):
        self.t = t
        self.b = b or Buf()


class Ring:
    def __init__(self, items):
        self.items = items
        self.i = 0

    def next(self):
        x = self.items[self.i % len(self.items)]
        self.i += 1
        return x


class Arena:
    def __init__(self, nc):
        self.nc = nc
        self.off = (nc.sbuf_base + 31) // 32 * 32
        self.top = nc.sbuf_top
        self.n = 0

    def alloc(self, shape, dtype, name=None):
        per = 1
        for s in shape[1:]:
            per *= s
        size = per * (2 if dtype == BF16 else 4)
        off = self.off
        self.off += (size + 31) // 32 * 32
        assert self.off <= self.top, f"SBUF overflow {self.off} > {self.top} ({name})"
        self.n += 1
        return TB(self.nc.alloc_sbuf_tensor_at(name or f"t{self.n}", list(shape), dtype, offset=off))

    def mark(self):
        return self.off

    def reset(self, m):
        self.off = m


def build_program(phases=9, dbg=()):
    nc = bass.Bass("TRN2", target_bir_lowering=False)
    S = Sched(nc)
    A = Arena(nc)
    PE, ACT, DVE, POOL, SP = S.pe, S.act, S.dve, S.pool, S.sp

    def din(name, shape):
        return nc.dram_tensor(name, list(shape), F32, kind="ExternalInput").ap()

    xc = din("xc", [TCTX, D])
    cT_d = din("cT", [128, 32])
    w_ada = din("w_ada", [D, 6 * D])
    badaT_d = din("b_adaT", [128, 192])
    gT_d = din("gT", [128, 64])
    w_in = din("w_in", [D, DIN])
    bfg_d = din("bfg", [16, 1])
    qkg_d = din("qkg", [128, 4])
    cposT_d = din("cposT", [128, 64])
    w_ck1 = din("w_ck1", [4096, 128])
    w_ck2 = din("w_ck2", [128, 128])
    w_cv1 = din("w_cv1", [4096, 128])
    w_cv2 = din("w_cv2", [128, 128])
    w_upa = din("w_upa", [2048, D])
    w_upb = din("w_upb", [2048, D])
    w_out = din("w_out", [D, D])
    w_ff1 = din("w_ff1", [D, DFF])
    w_ff2 = din("w_ff2", [DFF, D])
    ropeC_d = din("ropeC", [128, TCTX])
    ropeS_d = din("ropeS", [128, TCTX])
    ropeCc_d = din("ropeCc", [128, 128])
    ropeSc_d = din("ropeSc", [128, 128])
    cb_d = din("cb", [128, NCB])
    cf_d = din("cf", [128, NCF])
    out = nc.dram_tensor("out", [TOWN, D], F32, kind="ExternalOutput").ap()
    dbg_out = {}

    def dscr(name, shape, dt=BF16):
        return nc.dram_tensor(name, list(shape), dt).ap()

    d_q = dscr("d_q", [16, 128, TOWN])
    d_kc = dscr("d_kc", [4, 128, TCTX])
    d_vc = dscr("d_vc", [4, 128, TCTX])
    d_ks = dscr("d_ks", [4, 128, TCTX])
    d_kw = dscr("d_kw", [4, 128, TCTX])
    d_vs = dscr("d_vs", [TCTX, 4, 128])
    d_vw = dscr("d_vw", [TCTX, 4, 128])
    d_fq = dscr("d_fq", [16, 128, TOWN])
    d_fk = dscr("d_fk", [16, 128, TCTX])
    d_fv = dscr("d_fv", [TCTX, 16, 128])
    d_mg = dscr("d_mg", [64, 128, TOWN])
    grid = lambda n: [[Buf() for _ in range(4)] for _ in range(n)]
    g_q, g_kc, g_vc, g_ks, g_kw, g_vs, g_vw = grid(16), grid(4), grid(4), grid(4), grid(4), grid(4), grid(4)
    g_fq, g_fk, g_fv, g_mg = grid(16), grid(16), grid(16), grid(64)
    g_out = [[Buf() for _ in range(2)] for _ in range(32)]

    def mm(out_, lhsT, rhs, start, stop, reads, writes):
        S.emit(PE, lambda e: e.matmul(out_, lhsT, rhs, start=start, stop=stop, skip_group_check=True), reads, writes)

    def tr(out_, in_, ident, reads, writes):
        S.emit(PE, lambda e: e.transpose(out_, in_, ident), reads, writes)

    def act(out_, in_, func, reads, writes, **kw):
        S.emit(ACT, lambda e: e.activation(out=out_, in_=in_, func=func, **kw), reads, writes)

    def tt(q, out_, in0, in1, op, reads, writes):
        S.emit(q, lambda e: e.tensor_tensor(out=out_, in0=in0, in1=in1, op=op), reads, writes)

    def ts(q, out_, in0, s1, s2, op0, op1, reads, writes):
        if op1 is None:
            S.emit(q, lambda e: e.tensor_scalar(out=out_, in0=in0, scalar1=s1, scalar2=None, op0=op0), reads, writes)
        else:
            S.emit(q, lambda e: e.tensor_scalar(out=out_, in0=in0, scalar1=s1, scalar2=s2, op0=op0, op1=op1),
                   reads, writes)

    def stt(out_, in0, scalar, in1, op0, op1, reads, writes):
        S.emit(DVE, lambda e: e.scalar_tensor_tensor(out=out_, in0=in0, scalar=scalar, in1=in1, op0=op0, op1=op1),
               reads, writes)

    def cp(q, out_, in_, reads, writes):
        if q is ACT:
            S.emit(q, lambda e: e.activation(out=out_, in_=in_, func=AF.Copy), reads, writes)
        else:
            S.emit(q, lambda e: e.tensor_copy(out=out_, in_=in_), reads, writes)

    def dma(q, out_, in_, reads, writes):
        S.emit(q, lambda e: e.dma_start(out=out_, in_=in_), reads, writes, dma=True)

    def dump(name, src_ap, shape, reads, dt=F32, q=None):
        if name not in dbg:
            return
        t = nc.dram_tensor("dbg_" + name, list(shape), dt, kind="ExternalOutput").ap()
        b = Buf()
        dma(q or (POOL if dt != src_ap.dtype else SP), t, src_ap, reads, [b])
        dbg_out[name] = b

    PS = [TB(nc.alloc_psum_tensor(f"ps{i}", [128, 512], F32)) for i in range(8)]
    PSB = [p.t.bitcast(BF16) for p in PS]

    cb = A.alloc([128, NCB], BF16, "cb")
    cf = A.alloc([128, NCF], F32, "cf")
    modT = A.alloc([128, 192], F32, "modT")
    gT = A.alloc([128, 64], F32, "gT")
    s1T = A.alloc([128, 32], F32, "s1T")
    s2T = A.alloc([128, 32], F32, "s2T")
    qkg = A.alloc([128, 4], F32, "qkg")
    gate_tok = A.alloc([128, 8, 48], F32, "gate_tok")
    lsp = TB(nc.alloc_sbuf_tensor_at("lsp", [16, TCTX], F32, offset=(A.top - 8192) // 32 * 32))
    NSLOT = 5
    wslots = [A.alloc([128, 32, 128], BF16, f"ws{i}") for i in range(NSLOT)]
    wring = Ring(wslots)
    m0 = A.mark()

    ident = cb.t[:, CB_ID:CB_ID + 128]
    onesb = cb.t[:, CB_ONES:CB_ONES + 128]
    permT = cb.t[:, CB_PERM:CB_PERM + 32]
    mdiag4 = cb.t[:, CB_MDIAG:CB_MDIAG + 512]
    mwin4 = cb.t[:, CB_MWIN:CB_MWIN + 512]
    identf = cf.t[:, CF_ID:CF_ID + 128]

    dma(POOL, cb.t[:, 0:2048], cb_d[:, 0:2048], [], [cb.b])
    dma(POOL, cb.t[:, 2048:NCB], cb_d[:, 2048:NCB], [], [cb.b])
    dma(SP, cf.t[:], cf_d, [], [cf.b])
    dma(SP, gT.t[:], gT_d, [], [gT.b])
    dma(SP, qkg.t[:], qkg_d, [], [qkg.b])

    def gemm_multi(specs, defer=1):
        tiles = []
        for sp in specs:
            for ci, c in enumerate(range(0, sp["ncols"], 128)):
                tiles.append((sp, ci, c, min(128, sp["ncols"] - c)))
        need = lambda sp: sum((s_["K"] // 128 + 31) // 32 for s_ in sp["srcs"])
        loads = {}
        state = {"next": 0, "out": 0}

        def issue(ti):
            sp, ci, c, ncol = tiles[ti]
            sl = []
            for s_ in sp["srcs"]:
                KC = s_["K"] // 128
                Wv = s_["W"].rearrange("(kc p) n -> p kc n", p=128)
                for kc0 in range(0, KC, 32):
                    nk = min(32, KC - kc0)
                    w = wring.next()
                    c0 = s_["col0"] + c
                    S.emit(POOL, (lambda e, w=w, Wv=Wv, kc0=kc0, nk=nk, c0=c0, ncol=ncol:
                                  e.dma_start(out=w.t[:, 0:nk, 0:ncol], in_=Wv[:, kc0:kc0 + nk, c0:c0 + ncol])),
                           [], [w.b], dma=True)
                    sl.append((s_, w, kc0, nk))
            loads[ti] = sl

        def prefetch():
            while state["next"] < len(tiles) and state["out"] + need(tiles[state["next"]][0]) <= NSLOT:
                issue(state["next"])
                state["out"] += need(tiles[state["next"]][0])
                state["next"] += 1

        pending = []
        cur_sp = None
        for ti, (sp, ci, c, ncol) in enumerate(tiles):
            if sp is not cur_sp:
                for pnd in pending:
                    pnd[0](*pnd[1])
                pending = []
                cur_sp = sp
            prefetch()
            T = sp["T"]
            ttiles = [(t0, min(512, T - t0)) for t0 in range(0, T, 512)]
            outs = []
            for si, s_ in enumerate(sp["srcs"]):
                KC = s_["K"] // 128
                row = []
                for (t0, tn) in ttiles:
                    p = sp["prings"][si].next()
                    i = 0
                    for (s2, w, kc0, nk) in loads[ti]:
                        if s2 is not s_:
                            continue
                        for k in range(nk):
                            mm(p.t[0:ncol, 0:tn], w.t[:, k, 0:ncol],
                               s_["inT"].t[:, s_["kc_off"] + kc0 + k, t0:t0 + tn],
                               i == 0, i == KC - 1, [w.b, s_["inT"].b], [p.b])
                            i += 1
                    row.append((p, t0, tn))
                outs.append(row)
            state["out"] -= need(sp)
            del loads[ti]
            prefetch()
            pending.append((sp["epilogue"], (ci, ncol, outs)))
            if len(pending) > defer:
                pnd = pending.pop(0)
                pnd[0](*pnd[1])
        for pnd in pending:
            pnd[0](*pnd[1])

    def gemm(srcs, ncols, T, prings, epilogue, defer=1):
        gemm_multi([dict(srcs=srcs, ncols=ncols, T=T, prings=prings, epilogue=epilogue)], defer=defer)

    R_all = Ring(PS)
    cTf = A.alloc([128, 32], F32, "cTf")
    csig = A.alloc([128, 32], F32, "csig")
    silu = A.alloc([128, 32, 1], BF16, "silu")
    badaT = A.alloc([128, 192], F32, "badaT")
    dma(SP, cTf.t[:], cT_d, [], [cTf.b])
    dma(SP, badaT.t[:], badaT_d, [], [badaT.b])
    act(csig.t[:], cTf.t[:], AF.Sigmoid, [cTf.b], [csig.b])
    tt(DVE, silu.t[:, :, 0], cTf.t[:], csig.t[:], ALU.mult, [cTf.b, csig.b], [silu.b])

    def mod_gen(tile_lo, tile_hi, bank_fn, ahead=3):
        Wv = w_ada.rearrange("(kc p) n -> p kc n", p=128)
        work = [(t, s_) for t in range(tile_lo, tile_hi) for s_ in range(4)]
        issued = {}

        def issue(i):
            t, s_ = work[i]
            w = wring.next()
            wv = w.t[:].rearrange("p k c -> p (k c)").rearrange("p (k c) -> p k c", k=8)
            S.emit(POOL, (lambda e, wv=wv, t=t, s_=s_: e.dma_start(
                out=wv, in_=Wv[:, s_ * 8:(s_ + 1) * 8, t * 512:(t + 1) * 512])), [], [w.b], dma=True)
            issued[i] = (w, wv)

        for i in range(min(ahead, len(work))):
            issue(i)
        for i, (t, s_) in enumerate(work):
            if i + ahead < len(work):
                issue(i + ahead)
            w, wv = issued.pop(i)
            bank = bank_fn()
            for j in range(4):
                for k in range(8):
                    mm(bank.t[:, j:j + 1], wv[:, k, j * 128:(j + 1) * 128], silu.t[:, s_ * 8 + k, 0:1],
                       s_ == 0 and j == 0 and k == 0, s_ == 3 and k == 7, [w.b, silu.b], [bank.b])
            if s_ == 3:
                tt(DVE, modT.t[:, 4 * t:4 * t + 4], bank.t[:, 0:4], badaT.t[:, 4 * t:4 * t + 4], ALU.add,
                   [badaT.b], [modT.b, bank.b])
            yield

    for _ in mod_gen(0, 16, lambda: PS[0]):
        pass
    mod_bg = mod_gen(16, 48, lambda: mod_bank[0])
    mod_bank = [PS[6]]

    def mod_step(n=1):
        for _ in range(n):
            next(mod_bg, None)

    stt(s1T.t[:], modT.t[:, 32:64], 1.0, gT.t[:, 0:32], ALU.add, ALU.mult, [modT.b, gT.b], [s1T.b])

    def build_hT(src, row0, ntok, hT, shift_cols, sT, region_mark):
        A.reset(region_mark)
        xts = Ring([A.alloc([128, D], F32, f"xt{i}") for i in range(2)])
        xhs = [A.alloc([128, D], BF16, f"xh{j}") for j in range(4)]
        ss = Ring([A.alloc([128, 4], F32, f"ss{i}") for i in range(4)])
        pr = Ring(PS)
        for grp in range(ntok // 512):
            for j in range(4):
                r0 = row0 + grp * 512 + j * 128
                xt = xts.next()
                st = ss.next()
                dma(SP, xt.t[:], src[r0:r0 + 128, :], [], [xt.b])
                xh = xhs[j]
                S.emit(DVE, lambda e, xt=xt, st=st, xh=xh: e.scalar_tensor_tensor(
                    out=xh.t[:], in0=xt.t[:], scalar=1.0, in1=xt.t[:],
                    op0=ALU.mult, op1=ALU.mult, accum_out=st.t[:, 0:1]), [xt.b], [xh.b, st.b])
                act(st.t[:, 1:2], st.t[:, 0:1], AF.Ln, [epsc.b], [st.b], scale=1.0 / D, bias=epsc.t[:, 0:1])
                act(st.t[:, 2:3], st.t[:, 1:2], AF.Exp, [], [st.b], scale=-0.5)
                act(xh.t[:], xt.t[:], AF.Identity, [xt.b, st.b], [xh.b], scale=st.t[:, 2:3])
            for kc in range(32):
                p = pr.next()
                pb = PSB[PS.index(p)]
                for j in range(4):
                    tr(pb[:, j * 128:(j + 1) * 128], xhs[j].t[:, kc * 128:(kc + 1) * 128], ident, [xhs[j].b, cb.b], [p.b])
                dst = hT.t[:, kc, grp * 512:(grp + 1) * 512]
                if kc % 2 == 0:
                    act(dst, pb[:, 0:512], AF.Identity, [sT.b, modT.b], [hT.b, p.b],
                        scale=sT.t[:, kc:kc + 1], bias=modT.t[:, shift_cols + kc:shift_cols + kc + 1])
                else:
                    ts(DVE, dst, pb[:, 0:512], sT.t[:, kc:kc + 1], modT.t[:, shift_cols + kc:shift_cols + kc + 1],
                       ALU.mult, ALU.add, [sT.b, modT.b], [hT.b, p.b])

    epsc = A.alloc([128, 1], F32, "epsc")
    S.emit(DVE, lambda e: e.memset(epsc.t[:], EPS), [], [epsc.b])
    onec = A.alloc([128, 1], F32, "onec")
    bfg = A.alloc([16, 1], F32, "bfg")
    dma(SP, bfg.t[:], bfg_d, [], [bfg.b])
    S.emit(DVE, lambda e: e.memset(onec.t[:], 1.0), [], [onec.b])
    m0 = A.mark()
    hT = A.alloc([128, 32, TOWN], BF16, "hT")
    mR2 = A.mark()

    if phases >= 2:
        def proj_phase(half):
            tokc0 = half * TOWN
            S.barrier()
            build_hT(xc, tokc0, TOWN, hT, 0, s1T, mR2)
            if half == 1:
                dump("hT", hT.t[:], [128, 32, TOWN], [hT.b], BF16)
            S.barrier()
            A.reset(mR2)
            ropeC = A.alloc([128, TOWN], F32, "ropeC")
            ropeS = A.alloc([128, TOWN], F32, "ropeS")
            dma(SP, ropeC.t[:], ropeC_d[:, tokc0:tokc0 + TOWN], [], [ropeC.b])
            dma(SP, ropeS.t[:], ropeS_d[:, tokc0:tokc0 + TOWN], [], [ropeS.b])
            NB = 3
            sqr = Ring([A.alloc([128, 512], BF16, f"sq{i}") for i in range(NB)])
            lnr = Ring([A.alloc([128, 512], F32, f"ln{i}") for i in range(NB)])
            rsr = Ring([A.alloc([128, 512], F32, f"rs{i}") for i in range(NB)])
            yfr = Ring([A.alloc([128, 512], F32, f"yf{i}") for i in range(NB)])
            ybr = Ring([A.alloc([128, 512], BF16, f"yb{i}") for i in range(NB)])
            t1r = Ring([A.alloc([32, 512], F32, f"t1{i}") for i in range(NB)])
            t2r = Ring([A.alloc([32, 512], F32, f"t2{i}") for i in range(NB)])
            vtr = Ring([A.alloc([128, 4, 128], BF16, f"vt{i}") for i in range(NB)])
            gsr = Ring([A.alloc([48, 512], F32, f"gs{i}") for i in range(2)])
            Rm = Ring(PS[0:4])
            Rx = Ring(PS[4:8])
            src = lambda col0: [dict(W=w_in, K=D, col0=col0, inT=hT, kc_off=0)]

            def epi_raw(dst, g_dst):
                def f(ci, ncol, outs):
                    for (p, t0, tn) in outs[0]:
                        yb = ybr.next()
                        cp(ACT, yb.t[:, 0:tn], p.t[:, 0:tn], [], [yb.b, p.b])
                        tq = (tokc0 + t0) // 512
                        dma(SP, dst[ci, :, tokc0 + t0:tokc0 + t0 + tn], yb.t[:, 0:tn], [yb.b], [g_dst[ci][tq]])
                return f

            def epi_norm(dst, g_dst, gi, rope, ctx_tok):
                def f(ci, ncol, outs):
                    for (p, t0, tn) in outs[0]:
                        sq, ln, rs, yb = sqr.next(), lnr.next(), rsr.next(), ybr.next()
                        px = Rx.next()
                        act(sq.t[:], p.t[:], AF.Square, [], [sq.b, p.b])
                        mm(px.t[:], onesb, sq.t[:], True, True, [sq.b, cb.b], [px.b])
                        act(ln.t[:], px.t[:], AF.Ln, [epsc.b], [ln.b, px.b], scale=1.0 / 128, bias=epsc.t[:, 0:1])
                        act(rs.t[:], ln.t[:], AF.Exp, [ln.b], [rs.b], scale=-0.5)
                        d0 = (tokc0 if ctx_tok else 0) + t0
                        tq = d0 // 512
                        if not rope:
                            stt(yb.t[:], p.t[:], qkg.t[:, gi:gi + 1], rs.t[:], ALU.mult, ALU.mult,
                                [qkg.b, rs.b], [yb.b, p.b])
                        else:
                            yf, t1, t2 = yfr.next(), t1r.next(), t2r.next()
                            px2 = Rx.next()
                            stt(yf.t[:], p.t[:], qkg.t[:, gi:gi + 1], rs.t[:], ALU.mult, ALU.mult,
                                [qkg.b, rs.b], [yf.b, p.b])
                            cp(ACT, yb.t[:], yf.t[:], [yf.b], [yb.b])
                            mm(px2.t[0:32, :], permT, yb.t[:], True, True, [yb.b, cb.b], [px2.b])
                            tt(DVE, t2.t[:], px2.t[0:32, :], ropeS.t[0:32, t0:t0 + 512], ALU.mult,
                               [ropeS.b], [t2.b, px2.b])
                            tt(DVE, t1.t[:], yf.t[0:32, :], ropeC.t[0:32, t0:t0 + 512], ALU.mult,
                               [ropeC.b, yf.b], [t1.b])
                            tt(DVE, yb.t[0:32, :], t1.t[:], t2.t[:], ALU.add, [t1.b, t2.b], [yb.b])
                        dma(SP, dst[ci, :, d0:d0 + tn], yb.t[:, 0:tn], [yb.b], [g_dst[ci][tq]])
                return f

            def epi_vtok(dst, g_dst):
                def f(ci, ncol, outs):
                    for (p, t0, tn) in outs[0]:
                        yb, vt = ybr.next(), vtr.next()
                        px = Rx.next()
                        pxb = PSB[PS.index(px)]
                        cp(ACT, yb.t[:], p.t[:], [], [yb.b, p.b])
                        for j in range(4):
                            tr(pxb[:, j * 128:(j + 1) * 128], yb.t[:, j * 128:(j + 1) * 128], ident, [yb.b, cb.b], [px.b])
                        cp(DVE, vt.t[:], pxb[:, 0:512].rearrange("p (j d) -> p j d", j=4), [], [vt.b, px.b])
                        r0 = tokc0 + t0
                        dma(SP, dst[r0:r0 + 512, ci, :].rearrange("(j p) d -> p j d", p=128), vt.t[:], [vt.b],
                            [g_dst[ci][r0 // 512]])
                return f

            def epi_gate(ci, ncol, outs):
                for (p, t0, tn) in outs[0]:
                    gs = gsr.next()
                    px = Rx.next()
                    act(gs.t[:], p.t[0:48, :], AF.Sigmoid, [], [gs.b, p.b])
                    for j in range(4):
                        tr(px.t[:, j * 48:(j + 1) * 48], gs.t[0:48, j * 128:(j + 1) * 128], identf[0:48, 0:48],
                           [gs.b, cf.b], [px.b])
                    q0 = t0 // 128
                    cp(DVE, gate_tok.t[:, q0:q0 + 4, :], px.t[:, 0:192].rearrange("p (j c) -> p j c", j=4), [],
                       [gate_tok.b, px.b])

            def epi_f(ci, ncol, outs):
                for (p, t0, tn) in outs[0]:
                    z, ez = lnr.next(), rsr.next()
                    ts(DVE, z.t[0:16, 0:tn], p.t[0:16, 0:tn], bfg.t[:, 0:1], None, ALU.add, None, [bfg.b], [z.b, p.b])
                    act(ez.t[0:16, 0:tn], z.t[0:16, 0:tn], AF.Exp, [z.b], [ez.b], scale=-1.0)
                    act(lsp.t[:, tokc0 + t0:tokc0 + t0 + tn], ez.t[0:16, 0:tn], AF.Ln, [ez.b, onec.b], [lsp.b],
                        scale=1.0, bias=onec.t[0:16, 0:1])

            def epi_merge(ci, ncol, outs):
                for (p, t0, tn) in outs[0]:
                    yb = ybr.next()
                    act(yb.t[:], p.t[:], AF.Sigmoid, [], [yb.b, p.b])
                    dma(SP, d_mg[ci, :, t0:t0 + tn], yb.t[:, 0:tn], [yb.b], [g_mg[ci][t0 // 512]])

            KV0 = 2048
            FX0 = 5168
            specs = []

            def add(col0, ncols, epi):
                specs.append(dict(srcs=src(col0), ncols=ncols, T=TOWN, prings=[Rm], epilogue=epi))

            if half == 1:
                add(0, 2048, epi_norm(d_q, g_q, 0, True, False))
            add(KV0 + 0, 512, epi_raw(d_kc, g_kc))
            add(KV0 + 512, 512, epi_raw(d_vc, g_vc))
            add(KV0 + 1024, 512, epi_norm(d_ks, g_ks, 1, True, True))
            add(KV0 + 1536, 512, epi_vtok(d_vs, g_vs))
            add(KV0 + 2048, 512, epi_norm(d_kw, g_kw, 1, True, True))
            add(KV0 + 2560, 512, epi_vtok(d_vw, g_vw))
            if half == 1:
                add(5120, 48, epi_gate)
                add(FX0, 2048, epi_norm(d_fq, g_fq, 2, False, False))
            add(FX0 + 2048, 2048, epi_norm(d_fk, g_fk, 3, False, True))
            add(FX0 + 4096, 2048, epi_vtok(d_fv, g_fv))
            add(11312, 16, epi_f)
            if half == 1:
                add(11328, 8192, epi_merge)
            gemm_multi(specs)

        proj_phase(0)
        proj_phase(1)
        S.barrier()
        if "qn" in dbg:
            dump_d = nc.dram_tensor("dbg_qn", [16, 128, TOWN], BF16, kind="ExternalOutput").ap()
            b_ = Buf()
            dma(SP, dump_d, d_q, [x for r in g_q for x in r], [b_])
            dbg_out["qn"] = b_
        if "fk" in dbg:
            dump_d = nc.dram_tensor("dbg_fk", [16, 128, TCTX], BF16, kind="ExternalOutput").ap()
            b_ = Buf()
            dma(SP, dump_d, d_fk, [x for r in g_fk for x in r], [b_])
            dbg_out["fk"] = b_
        if "fv" in dbg:
            dump_d = nc.dram_tensor("dbg_fv", [TCTX, 16, 128], BF16, kind="ExternalOutput").ap()
            b_ = Buf()
            dma(SP, dump_d, d_fv, [x for r in g_fv for x in r], [b_])
            dbg_out["fv"] = b_
        if "ks" in dbg:
            dump_d = nc.dram_tensor("dbg_ks", [4, 128, TCTX], BF16, kind="ExternalOutput").ap()
            b_ = Buf()
            dma(SP, dump_d, d_ks, [x for r in g_ks for x in r], [b_])
            dbg_out["ks"] = b_
        dump("gate_tok", gate_tok.t[:], [128, 8, 48], [gate_tok.b])
        dump("lsp", lsp.t[:], [16, TCTX], [lsp.b])


    if phases >= 3:
        S.barrier()
        A.reset(m0)
        oT = A.alloc([128, 32, TOWN], BF16, "oT")
        m3 = A.mark()
        prevb = cf.t[:, CF_PREV:CF_PREV + 1]
        Rst = Ring(PS[0:3])
        Oacc = PS[3:7]
        Pm = PS[7]
        PmB = PSB[7]

        def all_(g, idx):
            return [b for b in g[idx]]

        if True:
            ones16 = A.alloc([16, TCTX], BF16, "ones16")
            cn = A.alloc([16, TCTX], F32, "cn")
            cntok = A.alloc([128, 16, 16], F32, "cntok")
            dg = A.alloc([16, 8, 16], F32, "dg")
            cbc = A.alloc([128, 8, 16], F32, "cbc")
            fbias = A.alloc([128, 8, 16, 16], F32, "fbias")
            S.emit(DVE, lambda e: e.memset(ones16.t[:], 1.0), [], [ones16.b])
            S.emit(DVE, lambda e: e.tensor_tensor_scan(out=cn.t[:], data0=ones16.t[:], data1=lsp.t[:], initial=0.0,
                                                       op0=ALU.mult, op1=ALU.add), [ones16.b, lsp.b], [cn.b])
            for kt in range(16):
                tr(Pm.t[:, kt * 16:(kt + 1) * 16], cn.t[0:16, kt * 128:(kt + 1) * 128],
                   cf.t[0:16, CF_ID16:CF_ID16 + 16], [cn.b, cf.b], [Pm.b])
            cp(DVE, cntok.t[:], Pm.t[:, 0:256].rearrange("p (k h) -> p k h", k=16), [], [cntok.b, Pm.b])
            cend = cn.t[0:16, TOWN + 127:TCTX:128]
            tt(DVE, dg.t[:], cend.unsqueeze(2).broadcast_to([16, 8, 16]),
               cf.t[0:16, CF_ID16:CF_ID16 + 16].unsqueeze(1).broadcast_to([16, 8, 16]), ALU.mult,
               [cn.b, cf.b], [dg.b])
            mm(Pm.t[:, 0:128], cf.t[0:16, CF_ONES:CF_ONES + 128], dg.t[:].rearrange("p a b -> p (a b)"), True, True,
               [dg.b, cf.b], [Pm.b])
            cp(DVE, cbc.t[:], Pm.t[:, 0:128].rearrange("p (a b) -> p a b", a=8), [], [cbc.b, Pm.b])
            for i in range(8):
                tt(DVE, fbias.t[:, i], cntok.t[:], cbc.t[:, i:i + 1, :].broadcast_to([128, 16, 16]), ALU.subtract,
                   [cntok.b, cbc.b], [fbias.b])
            ts(DVE, fbias.t[:, :, 0:8, :], fbias.t[:, :, 0:8, :], prevb, None, ALU.add, None, [cf.b], [fbias.b])
            dump("fbias", fbias.t[:], [128, 8, 16, 16], [fbias.b])
            mF = A.mark()
            fk4 = [A.alloc([128, TCTX], BF16, f"fk{i}") for i in range(4)]
            fq4 = [A.alloc([128, TOWN], BF16, f"fq{i}") for i in range(4)]
            fv4 = [A.alloc([128, 16, 130], BF16, f"fv{i}") for i in range(4)]
            pTr = Ring([A.alloc([128, 128], BF16, f"pT{i}") for i in range(10)])
            ofb = [A.alloc([128, 128], BF16, f"ofb{i}") for i in range(4)]
            rDr = Ring([A.alloc([128, 2], F32, f"rD{i}") for i in range(4)])
            for v in fv4:
                S.emit(DVE, lambda e, v=v: e.memset(v.t[:, :, 128:130], 1.0), [], [v.b])
            Rst2 = Ring(PS[0:2])
            Rof = Ring(PS[2:6])
            fox_pend = []

            def fox_flush():
                for f_ in fox_pend:
                    f_()
                del fox_pend[:]
            for hg in range(4):
                for j in range(4):
                    h = hg * 4 + j
                    dma(SP, fk4[j].t[:], d_fk[h], all_(g_fk, h), [fk4[j].b])
                    dma(SP, fq4[j].t[:], d_fq[h], all_(g_fq, h), [fq4[j].b])
                    dma(SP, fv4[j].t[:, :, 0:128], d_fv[:, h, :].rearrange("(k p) d -> p k d", p=128),
                        all_(g_fv, h), [fv4[j].b])
                for qt in range(8):
                    qc = 8 + qt
                    for j in range(4):
                        h = hg * 4 + j
                        of = Rof.next()
                        for g0 in range(0, qc + 1, 4):
                            kts = list(range(g0, min(g0 + 4, qc + 1)))
                            st = Rst2.next()
                            for idx, kt in enumerate(kts):
                                mm(st.t[:, idx * 128:(idx + 1) * 128], fk4[j].t[:, kt * 128:(kt + 1) * 128],
                                   fq4[j].t[:, qt * 128:(qt + 1) * 128], idx == 0, kt != qc,
                                   [fk4[j].b, fq4[j].b], [st.b])
                                if kt == qc:
                                    mm(st.t[:, idx * 128:(idx + 1) * 128], ident, mdiag4[:, 0:128], False, True,
                                       [cb.b], [st.b])
                            pTs = []
                            for idx, kt in enumerate(kts):
                                pT = pTr.next()
                                act(pT.t[:], st.t[:, idx * 128:(idx + 1) * 128], AF.Exp, [fbias.b], [pT.b, st.b],
                                    scale=SCALE, bias=fbias.t[:, qt, kt, h:h + 1])
                                pTs.append((kt, pT))
                            fox_flush()

                            def pv(pTs=pTs, of=of, j=j, qc=qc):
                                for (kt, pT) in pTs:
                                    mm(of.t[:, 0:129], pT.t[:], fv4[j].t[:, kt, 0:129], kt == 0, kt == qc,
                                       [pT.b, fv4[j].b], [of.b])
                            fox_pend.append(pv)

                        def epi(of=of, j=j):
                            rD = rDr.next()
                            S.emit(DVE, lambda e, rD=rD, of=of: e.reciprocal(out=rD.t[:, 0:1], in_=of.t[:, 128:129]),
                                   [], [rD.b, of.b])
                            ts(DVE, ofb[j].t[:], of.t[:, 0:128], rD.t[:, 0:1], None, ALU.mult, None, [rD.b],
                               [ofb[j].b, of.b])
                        fox_pend.append(epi)
                        if j % 2 == 1:
                            mod_step(1)
                    fox_flush()
                    for j in range(4):
                        tr(PmB[:, j * 128:(j + 1) * 128], ofb[j].t[:], ident, [ofb[j].b, cb.b], [Pm.b])
                    cp(DVE, oT.t[:, 16 + hg * 4:16 + hg * 4 + 4, qt * 128:(qt + 1) * 128],
                       PmB[:, 0:512].rearrange("p (j t) -> p j t", j=4), [], [oT.b, Pm.b])
            dump("ofoxT", oT.t[:, 16:32, :], [128, 16, TOWN], [oT.b], BF16)

        if phases >= 4:
            S.barrier()
            A.reset(m3)
            Rst = Ring(PS[0:2])
            mod_bank[0] = PS[2]
            w1k = A.alloc([128, 32, 128], BF16, "w1k")
            w1v = A.alloc([128, 32, 128], BF16, "w1v")
            w2k = A.alloc([128, 128], BF16, "w2k")
            w2v = A.alloc([128, 128], BF16, "w2v")
            posT = A.alloc([128, 64], BF16, "posT")
            rCc = A.alloc([128, 128], F32, "rCc")
            rSc = A.alloc([128, 128], F32, "rSc")
            cbk = A.alloc([128, 2], F32, "cbk")
            dma(POOL, w1k.t[:], w_ck1.rearrange("(l d) j -> d l j", d=128), [], [w1k.b])
            dma(POOL, w1v.t[:], w_cv1.rearrange("(l d) j -> d l j", d=128), [], [w1v.b])
            dma(POOL, w2k.t[:], w_ck2, [], [w2k.b])
            dma(POOL, w2v.t[:], w_cv2, [], [w2v.b])
            dma(POOL, posT.t[:], cposT_d, [], [posT.b])
            dma(SP, rCc.t[:], ropeCc_d, [], [rCc.b])
            dma(SP, rSc.t[:], ropeSc_d, [], [rSc.b])
            for wi, w1 in enumerate((w1k, w1v)):
                for l in range(32):
                    mm(Pm.t[:, 0:1], w1.t[:, l, :], posT.t[:, wi * 32 + l:wi * 32 + l + 1], l == 0, l == 31,
                       [w1.b, posT.b], [Pm.b])
                cp(DVE, cbk.t[:, wi:wi + 1], Pm.t[:, 0:1], [], [cbk.b, Pm.b])
            ksT = A.alloc([128, TCTX], BF16, "ksT")
            kwT = A.alloc([128, TCTX], BF16, "kwT")
            kcT = A.alloc([128, TCTX], BF16, "kcT")
            vcT = A.alloc([128, TCTX], BF16, "vcT")
            vs = A.alloc([128, 16, 130], BF16, "vs")
            vw = A.alloc([128, 16, 130], BF16, "vw")
            q4 = A.alloc([128, 4, TOWN], BF16, "q4")
            kcmpT = A.alloc([128, 128], BF16, "kcmpT")
            vcmp = A.alloc([128, 162], BF16, "vcmp")
            ca = A.alloc([128, 128], F32, "ca")
            ca2 = A.alloc([128, 128], F32, "ca2")
            cu = A.alloc([128, 128], F32, "cu")
            gl = A.alloc([128, 128], BF16, "gl")
            csq = A.alloc([128, 128], BF16, "csq")
            cln = A.alloc([128, 128], F32, "cln")
            cyf = A.alloc([128, 128], F32, "cyf")
            ct1 = A.alloc([32, 128], F32, "ct1")
            ct2 = A.alloc([32, 128], F32, "ct2")
            pT4 = Ring([A.alloc([128, 512], BF16, f"pq{i}") for i in range(4)])
            oaccS = A.alloc([128, 4, 128], F32, "oaccS")
            onsa = A.alloc([128, 4, 128], BF16, "onsa")
            sm = Ring([A.alloc([128, 16], F32, f"sm{i}") for i in range(8)])
            imp = A.alloc([128, 32], F32, "imp")
            sc = A.alloc([128, 32], F32, "sc")
            sc2 = A.alloc([128, 32], F32, "sc2")
            m8a = A.alloc([128, 8], F32, "m8a")
            m8b = A.alloc([128, 8], F32, "m8b")
            selm = A.alloc([128, 32], F32, "selm")
            selv = A.alloc([128, 32], F32, "selv")
            sbb = A.alloc([128, 32], BF16, "sbb")
            selT = A.alloc([32, 128], BF16, "selT")
            S.emit(DVE, lambda e: e.memset(vs.t[:, :, 128:130], 1.0), [], [vs.b])
            S.emit(DVE, lambda e: e.memset(vw.t[:, :, 128:130], 1.0), [], [vw.b])
            S.emit(DVE, lambda e: e.memset(vcmp.t[:, 128:129], 1.0), [], [vcmp.b])
            cp(DVE, vcmp.t[:, 129:161], cb.t[:, CB_OV:CB_OV + 32], [cb.b], [vcmp.b])
            S.emit(DVE, lambda e: e.memset(kcmpT.t[:], 0.0), [], [kcmpT.b])
            cmpb = cb.t[:, CB_CMPB:CB_CMPB + 1024]
            ftab = cf.t[:, CF_FTAB:CF_FTAB + 256]

            def weight_cols(p, r, gcol, smt):
                ts(DVE, smt.t[:, 4 + r:5 + r], p.t[:, 128:129], 1e-30, None, ALU.max, None, [], [smt.b, p.b])
                S.emit(DVE, lambda e: e.reciprocal(out=smt.t[:, 8 + r:9 + r], in_=smt.t[:, 4 + r:5 + r]), [], [smt.b])
                tt(DVE, smt.t[:, r:r + 1], smt.t[:, 8 + r:9 + r], gcol, ALU.mult, [gate_tok.b], [smt.b])

            for g in range(4):
                dma(SP, ksT.t[:], d_ks[g], all_(g_ks, g), [ksT.b])
                dma(SP, kwT.t[:], d_kw[g], all_(g_kw, g), [kwT.b])
                dma(SP, kcT.t[:], d_kc[g], all_(g_kc, g), [kcT.b])
                dma(SP, vcT.t[:], d_vc[g], all_(g_vc, g), [vcT.b])
                dma(SP, vs.t[:, :, 0:128], d_vs[:, g, :].rearrange("(k p) d -> p k d", p=128), all_(g_vs, g), [vs.b])
                dma(SP, vw.t[:, :, 0:128], d_vw[:, g, :].rearrange("(k p) d -> p k d", p=128), all_(g_vw, g), [vw.b])
                dma(SP, q4.t[:], d_q[4 * g:4 * g + 4].rearrange("h d t -> d h t"),
                    [b for h in range(4 * g, 4 * g + 4) for b in g_q[h]], [q4.b])
                for wi, (w1, w2, xT) in enumerate(((w1k, w2k, kcT), (w1v, w2v, vcT))):
                    hp = Rst.next()
                    for l in range(32):
                        mm(hp.t[:, 0:127], w1.t[:, l, :], xT.t[:, l:l + 16 * 126 + 1:16], l == 0, l == 31,
                           [w1.b, xT.b], [hp.b])
                    ts(DVE, ca.t[:, 0:127], hp.t[:, 0:127], cbk.t[:, wi:wi + 1], None, ALU.add, None, [cbk.b],
                       [ca.b, hp.b])
                    tt(DVE, ca2.t[:, 0:127], ca.t[:, 0:127], ca.t[:, 0:127], ALU.mult, [ca.b], [ca2.b])
                    ts(DVE, ca2.t[:, 0:127], ca2.t[:, 0:127], 0.044715, 1.0, ALU.mult, ALU.add, [], [ca2.b])
                    tt(DVE, cu.t[:, 0:127], ca2.t[:, 0:127], ca.t[:, 0:127], ALU.mult, [ca2.b, ca.b], [cu.b])
                    act(cu.t[:, 0:127], cu.t[:, 0:127], AF.Sigmoid, [], [cu.b], scale=1.5957691216)
                    tt(DVE, gl.t[:, 0:127], cu.t[:, 0:127], ca.t[:, 0:127], ALU.mult, [cu.b, ca.b], [gl.b])
                    kp = Rst.next()
                    if wi == 0:
                        mm(kp.t[:, 0:127], w2.t[:], gl.t[:, 0:127], True, True, [w2.b, gl.b], [kp.b])
                        act(csq.t[:, 0:127], kp.t[:, 0:127], AF.Square, [], [csq.b, kp.b])
                        mm(Pm.t[:, 0:127], onesb, csq.t[:, 0:127], True, True, [csq.b, cb.b], [Pm.b])
                        act(cln.t[:, 0:127], Pm.t[:, 0:127], AF.Ln, [epsc.b], [cln.b, Pm.b], scale=1.0 / 128,
                            bias=epsc.t[:, 0:1])
                        act(cln.t[:, 0:127], cln.t[:, 0:127], AF.Exp, [], [cln.b], scale=-0.5)
                        stt(cyf.t[:, 0:127], kp.t[:, 0:127], qkg.t[:, 1:2], cln.t[:, 0:127], ALU.mult, ALU.mult,
                            [qkg.b, cln.b], [cyf.b, kp.b])
                        cp(ACT, kcmpT.t[:, 0:127], cyf.t[:, 0:127], [cyf.b], [kcmpT.b])
                        mm(Pm.t[0:32, 0:127], permT, kcmpT.t[:, 0:127], True, True, [kcmpT.b, cb.b], [Pm.b])
                        tt(DVE, ct2.t[:, 0:127], Pm.t[0:32, 0:127], rSc.t[0:32, 0:127], ALU.mult, [rSc.b],
                           [ct2.b, Pm.b])
                        tt(DVE, ct1.t[:, 0:127], cyf.t[0:32, 0:127], rCc.t[0:32, 0:127], ALU.mult, [rCc.b, cyf.b],
                           [ct1.b])
                        tt(DVE, kcmpT.t[0:32, 0:127], ct1.t[:, 0:127], ct2.t[:, 0:127], ALU.add, [ct1.b, ct2.b],
                           [kcmpT.b])
                    else:
                        mm(kp.t[0:127, 0:128], gl.t[:, 0:127], w2.t[:], True, True, [w2.b, gl.b], [kp.b])
                        cp(DVE, vcmp.t[0:127, 0:128], kp.t[0:127, 0:128], [], [vcmp.b, kp.b])
                for qt in range(8):
                    qc = 8 + qt
                    q4s = q4.t[:, :, qt * 128:(qt + 1) * 128]
                    st4 = lambda st, np_=128: st.t[0:np_, 0:512].rearrange("p (r t) -> p r t", r=4)
                    st = Rst.next()
                    mm(st4(st, 127), kcmpT.t[:, 0:127], q4s, True, False, [kcmpT.b, q4.b], [st.b])
                    mm(st4(st, 127), ident[0:127, 0:127],
                       cmpb[0:127, qt * 128:(qt + 1) * 128].unsqueeze(1).broadcast_to([127, 4, 128]), False, True,
                       [cb.b], [st.b])
                    pT = pT4.next()
                    act(pT.t[0:127, :], st.t[0:127, 0:512], AF.Exp, [], [pT.b, st.b], scale=SCALE)
                    smt = sm.next()
                    for r in range(4):
                        mm(Oacc[r].t[:, 0:161], pT.t[0:127, r * 128:(r + 1) * 128], vcmp.t[0:127, 0:161], True, True,
                           [pT.b, vcmp.b], [Oacc[r].b])
                    for r in range(4):
                        h = 4 * g + r
                        p = Oacc[r]
                        weight_cols(p, r, gate_tok.t[:, qt, 3 * h:3 * h + 1], smt)
                        if r == 0:
                            ts(DVE, imp.t[:], p.t[:, 129:161], smt.t[:, 8 + r:9 + r], None, ALU.mult, None, [smt.b],
                               [imp.b, p.b])
                        else:
                            stt(imp.t[:], p.t[:, 129:161], smt.t[:, 8 + r:9 + r], imp.t[:], ALU.mult, ALU.add,
                                [smt.b], [imp.b, p.b])
                        ts(DVE, oaccS.t[:, r, :], p.t[:, 0:128], smt.t[:, r:r + 1], None, ALU.mult, None, [smt.b],
                           [oaccS.b, p.b])
                    tt(DVE, sc.t[:], imp.t[:], ftab[:, qt * 32:(qt + 1) * 32], ALU.add, [imp.b, cf.b], [sc.b])
                    S.emit(DVE, lambda e: e.max(out=m8a.t[:], in_=sc.t[:]), [sc.b], [m8a.b])
                    S.emit(DVE, lambda e: e.match_replace(out=sc2.t[:], in_to_replace=m8a.t[:], in_values=sc.t[:],
                                                          imm_value=-1.0e9), [sc.b, m8a.b], [sc2.b])
                    S.emit(DVE, lambda e: e.max(out=m8b.t[:], in_=sc2.t[:]), [sc2.b], [m8b.b])
                    ts(DVE, selm.t[:], sc.t[:], m8b.t[:, 7:8], None, ALU.is_ge, None, [sc.b, m8b.b], [selm.b])
                    ts(DVE, selv.t[:], sc.t[:], -0.5 * BIG, None, ALU.is_gt, None, [sc.b], [selv.b])
                    tt(DVE, selm.t[:], selm.t[:], selv.t[:], ALU.mult, [selv.b], [selm.b])
                    ts(DVE, sbb.t[:], selm.t[:], -NEG, NEG, ALU.mult, ALU.add, [selm.b], [sbb.b])
                    if g == 0 and qt == 7:
                        dump("selm", selm.t[:], [128, 32], [selm.b])
                        dump("imp", imp.t[:], [128, 32], [imp.b])
                    selT4 = selT.t[0:32, :].unsqueeze(1).broadcast_to([32, 4, 128])
                    pend = None
                    for kt in range(qc - 4, qc + 1):
                        st = Rst.next()
                        last_qk = not (kt == qc - 4 or kt == qc)
                        mm(st4(st), kwT.t[:, kt * 128:(kt + 1) * 128], q4s, True, last_qk, [kwT.b, q4.b], [st.b])
                        if kt == qc - 4:
                            mm(st.t[:, 0:512], ident, mwin4, False, True, [cb.b], [st.b])
                        if kt == qc:
                            mm(st.t[:, 0:512], ident, mdiag4, False, True, [cb.b], [st.b])
                        pT = pT4.next()
                        if kt < 8:
                            act(pT.t[:], st.t[:, 0:512], AF.Exp, [cf.b], [pT.b, st.b], scale=SCALE, bias=prevb)
                        else:
                            act(pT.t[:], st.t[:, 0:512], AF.Exp, [], [pT.b, st.b], scale=SCALE)
                        if pend is not None:
                            pend()

                        def pend(kt=kt, pT=pT, qc=qc):
                            for r in range(4):
                                mm(Oacc[r].t[:, 0:129], pT.t[:, r * 128:(r + 1) * 128], vw.t[:, kt, 0:129],
                                   kt == qc - 4, kt == qc, [pT.b, vw.b], [Oacc[r].b])
                    pend()
                    smt = sm.next()
                    for r in range(4):
                        h = 4 * g + r
                        p = Oacc[r]
                        weight_cols(p, r, gate_tok.t[:, qt, 3 * h + 2:3 * h + 3], smt)
                        stt(oaccS.t[:, r, :], p.t[:, 0:128], smt.t[:, r:r + 1], oaccS.t[:, r, :], ALU.mult, ALU.add,
                            [smt.b], [oaccS.b, p.b])
                    tr(PmB[0:32, 0:128], sbb.t[:, 0:32], ident, [sbb.b, cb.b], [Pm.b])
                    cp(DVE, selT.t[:], PmB[0:32, 0:128], [], [selT.b, Pm.b])
                    pend = None
                    for kt in range(qc + 1):
                        st = Rst.next()
                        mm(st4(st), ksT.t[:, kt * 128:(kt + 1) * 128], q4s, True, False, [ksT.b, q4.b], [st.b])
                        mm(st4(st), cb.t[0:32, CB_ESEL + kt * 128:CB_ESEL + (kt + 1) * 128], selT4, False, kt != qc,
                           [selT.b, cb.b], [st.b])
                        if kt == qc:
                            mm(st.t[:, 0:512], ident, mdiag4, False, True, [cb.b], [st.b])
                        pT = pT4.next()
                        act(pT.t[:], st.t[:, 0:512], AF.Exp, [], [pT.b, st.b], scale=SCALE)
                        if pend is not None:
                            pend()

                        def pend(kt=kt, pT=pT, qc=qc):
                            for r in range(4):
                                mm(Oacc[r].t[:, 0:129], pT.t[:, r * 128:(r + 1) * 128], vs.t[:, kt, 0:129], kt == 0,
                                   kt == qc, [pT.b, vs.b], [Oacc[r].b])
                    pend()
                    smt = sm.next()
                    for r in range(4):
                        h = 4 * g + r
                        p = Oacc[r]
                        weight_cols(p, r, gate_tok.t[:, qt, 3 * h + 1:3 * h + 2], smt)
                        stt(onsa.t[:, r, :], p.t[:, 0:128], smt.t[:, r:r + 1], oaccS.t[:, r, :], ALU.mult, ALU.add,
                            [smt.b, oaccS.b], [onsa.b, p.b])
                    mod_step(2)
                    for r in range(4):
                        tr(PmB[:, r * 128:(r + 1) * 128], onsa.t[:, r, :], ident, [onsa.b, cb.b], [Pm.b])
                    cp(DVE, oT.t[:, 4 * g:4 * g + 4, qt * 128:(qt + 1) * 128],
                       PmB[:, 0:512].rearrange("p (j t) -> p j t", j=4), [], [oT.b, Pm.b])
            dump("onsaT", oT.t[:, 0:16, :], [128, 16, TOWN], [oT.b], BF16)

    if phases >= 5:
        for _ in mod_bg:
            pass
        stt(s2T.t[:], modT.t[:, 128:160], 1.0, gT.t[:, 32:64], ALU.add, ALU.mult, [modT.b, gT.b], [s2T.b])
        dump("modT", modT.t[:], [128, 192], [modT.b])
        S.barrier()
        A.reset(m3)
        yT = A.alloc([128, 32, TOWN], BF16, "yT")
        m5 = A.mark()
        sgr = Ring([A.alloc([128, 512], BF16, f"sg{i}") for i in range(4)])
        t1r = Ring([A.alloc([128, 512], F32, f"u1{i}") for i in range(2)])
        t2r = Ring([A.alloc([128, 512], F32, f"u2{i}") for i in range(2)])

        def epi_up(ci, ncol, outs):
            for ti in range(2):
                pa, t0, tn = outs[0][ti]
                pb_, _, _ = outs[1][ti]
                sa, sb_ = sgr.next(), sgr.next()
                t1, t2 = t1r.next(), t2r.next()
                dma(SP, sa.t[:], d_mg[ci, :, t0:t0 + 512], [g_mg[ci][ti]], [sa.b])
                dma(SP, sb_.t[:], d_mg[32 + ci, :, t0:t0 + 512], [g_mg[32 + ci][ti]], [sb_.b])
                tt(DVE, t1.t[:], pa.t[:], sa.t[:], ALU.mult, [sa.b], [t1.b, pa.b])
                tt(DVE, t2.t[:], pb_.t[:], sb_.t[:], ALU.mult, [sb_.b], [t2.b, pb_.b])
                tt(POOL, yT.t[:, ci, t0:t0 + 512], t1.t[:], t2.t[:], ALU.add, [t1.b, t2.b], [yT.b])

        gemm([dict(W=w_upa, K=2048, col0=0, inT=oT, kc_off=0), dict(W=w_upb, K=2048, col0=0, inT=oT, kc_off=16)],
             D, TOWN, [Ring(PS[0:4]), Ring(PS[4:8])], epi_up)
        dump("yT", yT.t[:], [128, 32, TOWN], [yT.b], BF16)

    if phases >= 6:
        S.barrier()
        A.reset(m5)
        tmpr = Ring([A.alloc([128, 512], F32, f"tm{i}") for i in range(2)])
        xpr = Ring([A.alloc([128, 4, 128], F32, f"xp{i}") for i in range(2)])
        x1r = Ring([A.alloc([128, 4, 128], F32, f"x1{i}") for i in range(2)])
        Rz = Ring(PS[4:8])

        def epi_res(gate_col, src, row0, src_grid):
            def f(ci, ncol, outs):
                for ti, (p, t0, tn) in enumerate(outs[0]):
                    tmp, xp, x1 = tmpr.next(), xpr.next(), x1r.next()
                    pz = Rz.next()
                    act(tmp.t[:], p.t[:], AF.Identity, [modT.b], [tmp.b, p.b],
                        scale=modT.t[:, gate_col + ci:gate_col + ci + 1])
                    for j in range(4):
                        tr(pz.t[:, j * 128:(j + 1) * 128], tmp.t[:, j * 128:(j + 1) * 128], identf, [tmp.b, cf.b],
                           [pz.b])
                    dma(SP, xp.t[:], src[row0 + t0:row0 + t0 + 512, ci * 128:(ci + 1) * 128].rearrange(
                        "(j p) c -> p j c", p=128), [src_grid[ci][ti]] if src_grid else [], [xp.b])
                    tt(DVE, x1.t[:], pz.t[:, 0:512].rearrange("p (j c) -> p j c", j=4), xp.t[:], ALU.add, [xp.b],
                       [x1.b, pz.b])
                    dma(SP, out[t0:t0 + 512, ci * 128:(ci + 1) * 128].rearrange("(j p) c -> p j c", p=128), x1.t[:],
                        [x1.b], [g_out[ci][ti]])
            return f

        gemm([dict(W=w_out, K=D, col0=0, inT=yT, kc_off=0)], D, TOWN, [Ring(PS[0:4])],
             epi_res(64, xc, TOWN, None))

    if phases >= 7:
        S.barrier()
        h2T = TB(oT.t, oT.b)
        build_hT(out, 0, TOWN, h2T, 96, s2T, m3)
        S.barrier()
        A.reset(m3)
        uT = A.alloc([128, 32, TOWN], BF16, "uT")
        rr = Ring([A.alloc([128, 512], F32, f"rr{i}") for i in range(2)])
        tmpr = Ring([A.alloc([128, 512], F32, f"tn{i}") for i in range(1)])
        xpr = Ring([A.alloc([128, 4, 128], F32, f"xq{i}") for i in range(2)])
        x1r = Ring([A.alloc([128, 4, 128], F32, f"xr{i}") for i in range(1)])
        Rz = Ring(PS[4:8])

        def epi_ff1(ci, ncol, outs):
            for (p, t0, tn) in outs[0]:
                r = rr.next()
                act(r.t[:], p.t[:], AF.Relu, [], [r.b, p.b])
                tt(DVE, uT.t[:, ci, t0:t0 + 512], r.t[:], r.t[:], ALU.mult, [r.b], [uT.b])

        specs = []
        for fc in range(4):
            specs.append(dict(srcs=[dict(W=w_ff1, K=D, col0=fc * 4096, inT=h2T, kc_off=0)], ncols=4096, T=TOWN,
                              prings=[Ring(PS[0:4])], epilogue=epi_ff1))
            specs.append(dict(srcs=[dict(W=w_ff2[fc * 4096:(fc + 1) * 4096, :], K=4096, col0=0, inT=uT, kc_off=0)],
                              ncols=D, T=TOWN, prings=[Ring(PS[0:4])], epilogue=epi_res(160, out, 0, g_out)))
        gemm_multi(specs)

    finals = [b for row in g_out for b in row] + list(dbg_out.values())
    if phases < 9:
        z = A.alloc([128, 512], F32, "zero")
        S.emit(DVE, lambda e: e.memset(z.t[:], 0.0), [], [z.b])
        dma(SP, out[0:128, 0:512], z.t[:], [z.b], [g_out[0][0]])
    S.finish(finals)
    return nc


def _consts(half):
    f32 = np.float32
    cbp = np.zeros((128, NCB), f32)
    cbp[:, CB_ID:CB_ID + 128] = np.eye(128, dtype=f32)
    cbp[:, CB_ONES:CB_ONES + 128] = 1.0
    for m in range(32):
        srcm = m + 16 if m < 16 else m - 16
        cbp[srcm, CB_PERM + m] = 1.0
    s = np.arange(128)[:, None]
    t = np.arange(128)[None, :]
    md = np.where(s <= t, 0.0, NEG).astype(f32)
    mw = np.where(s > t, 0.0, NEG).astype(f32)
    cbp[:, CB_MDIAG:CB_MDIAG + 512] = np.tile(md, (1, 4))
    cbp[:, CB_MWIN:CB_MWIN + 512] = np.tile(mw, (1, 4))
    for kt in range(16):
        for ss in range(128):
            cbp[2 * kt + (1 if ss >= 64 else 0), CB_ESEL + kt * 128 + ss] = 1.0
    n = np.arange(127)
    ci = n[:, None] * 16
    sj = np.arange(32)[None, :] * 64
    cbp[:127, CB_OV:CB_OV + 32] = ((ci < sj + 64) & (ci + 32 > sj)).astype(f32)
    tctx = TOWN + np.arange(TOWN)[None, :]
    valid = (n[:, None] >= 64) if half == 0 else np.ones((127, 1), bool)
    cbp[:127, CB_CMPB:CB_CMPB + 1024] = np.where(valid & (16 * n[:, None] + 31 <= tctx), 0.0, NEG)
    cbp[127, CB_CMPB:CB_CMPB + 1024] = NEG

    cfp = np.zeros((128, NCF), f32)
    cfp[:, CF_ID:CF_ID + 128] = np.eye(128, dtype=f32)
    j0 = 16 * (1 - half)
    for qt in range(8):
        tc = TOWN + qt * 128 + np.arange(128)
        cur = tc // 64
        jj = np.arange(32)[None, :]
        forced = (jj == j0) | (jj == cur[:, None]) | ((jj == cur[:, None] - 1) & (cur[:, None] - 1 >= j0))
        causal = (jj >= j0) & (jj <= cur[:, None])
        cfp[:, CF_FTAB + qt * 32:CF_FTAB + (qt + 1) * 32] = np.where(forced, BIG, np.where(causal, 0.0, -BIG))
    cfp[:, CF_PREV] = 0.0 if half == 1 else NEG
    cfp[:, CF_ONES:CF_ONES + 128] = 1.0
    cfp[:16, CF_ID16:CF_ID16 + 16] = np.eye(16, dtype=f32)

    inv = (500000.0 ** (-np.arange(16, dtype=f32) / 16)).astype(f32)

    def rope_tabs(pos):
        pos = pos.astype(f32)
        ang = pos[None, :] * inv[:, None]
        c, s_ = np.cos(ang).astype(f32), np.sin(ang).astype(f32)
        C = np.ones((128, pos.shape[0]), f32)
        Sg = np.zeros((128, pos.shape[0]), f32)
        C[0:16] = c
        C[16:32] = c
        Sg[0:16] = -s_
        Sg[16:32] = s_
        return C, Sg

    pos_ctx = np.arange(TCTX) - TOWN * (1 - half)
    ropeC, ropeS = rope_tabs(pos_ctx)
    endp = np.arange(128) * 16 + 31 - TOWN * (1 - half)
    ropeCc, ropeSc = rope_tabs(endp)
    return dict(cb=cbp, cf=cfp, ropeC=ropeC, ropeS=ropeS, ropeCc=ropeCc, ropeSc=ropeSc)


def make_in_maps(inp, cores=range(8)):
    f32 = np.float32
    A_ = lambda a: np.ascontiguousarray(np.asarray(a, dtype=f32))
    shared = dict(
        w_ada=A_(inp["w_ada"][0]), b_adaT=A_(np.asarray(inp["b_ada"][0]).reshape(192, 128).T),
        gT=A_(np.concatenate([np.asarray(inp["norm1_g"][0]).reshape(32, 128).T,
                              np.asarray(inp["norm2_g"][0]).reshape(32, 128).T], axis=1)),
        w_in=A_(inp["w_in"][0]), bfg=A_(np.asarray(inp["b_forget"][0]).reshape(16, 1)),
        qkg=A_(np.stack([np.asarray(inp["nsa_q_norm"][0]), np.asarray(inp["nsa_k_norm"][0]),
                         np.asarray(inp["fox_q_norm"][0]), np.asarray(inp["fox_k_norm"][0])], axis=1)),
        cposT=A_(np.concatenate([np.asarray(inp["cmp_pos_k"][0]).T, np.asarray(inp["cmp_pos_v"][0]).T], axis=1)),
        w_ck1=A_(inp["w_cmp_k1"][0]), w_ck2=A_(inp["w_cmp_k2"][0]),
        w_cv1=A_(inp["w_cmp_v1"][0]), w_cv2=A_(inp["w_cmp_v2"][0]),
        w_upa=A_(inp["w_up_nsa"][0]), w_upb=A_(inp["w_up_fox"][0]), w_out=A_(inp["w_out"][0]),
        w_ff1=A_(inp["w_ff1"][0]), w_ff2=A_(inp["w_ff2"][0]),
    )
    consts = [_consts(0), _consts(1)]
    x = np.asarray(inp["x"], dtype=f32)
    c = np.asarray(inp["c"], dtype=f32)
    maps = []
    for k in cores:
        b, half = k // 2, k % 2
        if half == 1:
            xcc = x[b]
        else:
            xcc = np.concatenate([np.zeros((TOWN, D), f32), x[b, :TOWN]], axis=0)
        m = dict(shared)
        m["xc"] = np.ascontiguousarray(xcc)
        m["cT"] = A_(c[b].reshape(32, 128).T)
        m.update(consts[half])
        maps.append(m)
    return maps


_NC_CACHE = {}


def kernel(**inputs):
    if "nc" not in _NC_CACHE:
        _NC_CACHE["nc"] = build_program()
    nc = _NC_CACHE["nc"]
    maps = make_in_maps(inputs)
    res = run_bass_kernel_spmd(nc, maps, core_ids=list(range(8)))
    outp = np.zeros((4, 2048, D), np.float32)
    for k in range(8):
        b, half = k // 2, k % 2
        outp[b, half * TOWN:(half + 1) * TOWN] = np.asarray(res.results[k]["out"])
    return outp
```

```python
import numpy as np
import concourse.bass as bass
import concourse.mybir as mybir
from concourse.alu_op_type import AluOpType as ALU
from concourse.bass_utils import run_bass_kernel_spmd

AF = mybir.ActivationFunctionType
F32 = mybir.dt.float32
BF16 = mybir.dt.bfloat16

D = 4096
TOWN = 1024
TCTX = 2048
DIN = 19520
DFF = 16384
SCALE = 128 ** -0.5
EPS = 1e-6
NEG = -30000.0
BIG = 1.0e4

CB_ID, CB_ONES, CB_PERM, CB_MDIAG, CB_MWIN, CB_ESEL, CB_OV, CB_CMPB = 0, 128, 256, 384, 896, 1408, 3456, 3488
NCB = 3488 + 1024
CF_ID, CF_FTAB, CF_PREV, CF_ONES, CF_ID16 = 0, 128, 384, 385, 513
NCF = 513 + 16


class Buf:
    __slots__ = ("w", "r")

    def __init__(self):
        self.w = None
        self.r = {}


class Sem:
    __slots__ = ("h", "val")

    def __init__(self, h):
        self.h = h
        self.val = 0


class Queue:
    def __init__(self, name, sem, inorder=False):
        self.name = name
        self.sem = sem
        self.ops = []
        self.waited = {}
        self.inorder = inorder
        self.ring = []
        self.ring_i = 0


class Sched:
    def __init__(self, nc, n_dma_sems=20):
        self.nc = nc
        mk = lambda n: Sem(nc.alloc_semaphore(n))
        self.pe = Queue("pe", mk("s_pe"), inorder=True)
        self.act = Queue("act", mk("s_act"))
        self.dve = Queue("dve", mk("s_dve"))
        self.pool = Queue("pool", mk("s_pool"))
        self.sp = Queue("sp", mk("s_sp"))
        self.queues = [self.pe, self.act, self.dve, self.pool, self.sp]
        for q in (self.sp, self.pool):
            q.ring = [mk(f"d_{q.name}{i}") for i in range(n_dma_sems)]
        self.n_ops = 0

    def _wait(self, q, sem, val):
        if q.waited.get(sem, 0) >= val:
            return
        q.waited[sem] = val
        q.ops.append(("w", sem, val))

    def emit(self, q, fn, reads=(), writes=(), dma=False):
        deps = {}
        for b in reads:
            if b.w is not None:
                s, v = b.w
                if deps.get(s, 0) < v:
                    deps[s] = v
        for b in writes:
            if b.w is not None:
                s, v = b.w
                if deps.get(s, 0) < v:
                    deps[s] = v
            for s, v in b.r.items():
                if deps.get(s, 0) < v:
                    deps[s] = v
        for s, v in deps.items():
            if s is q.sem and q.inorder and not dma:
                continue
            self._wait(q, s, v)
        if dma:
            sem = q.ring[q.ring_i % len(q.ring)]
            q.ring_i += 1
            if sem.val > 0:
                self._wait(q, sem, sem.val)
            sem.val += 16
            tok = (sem, sem.val)
            q.ops.append(("o", fn, sem, 16))
        else:
            q.sem.val += 1
            tok = (q.sem, q.sem.val)
            q.ops.append(("o", fn, q.sem, 1))
        for b in reads:
            if b.r.get(tok[0], 0) < tok[1]:
                b.r[tok[0]] = tok[1]
        for b in writes:
            b.w = tok
            b.r = {}
        self.n_ops += 1
        return tok

    def barrier(self):
        sems = [q.sem for q in self.queues] + [s for q in self.queues for s in q.ring]
        for q in self.queues:
            for s in sems:
                if s.val > 0 and s is not q.sem:
                    self._wait(q, s, s.val)

    def finish(self, final_bufs):
        q = self.sp
        for b in final_bufs:
            if b.w is not None:
                self._wait(q, b.w[0], b.w[1])
        nc = self.nc

        def run(queue):
            def body(e):
                for op in queue.ops:
                    if op[0] == "w":
                        e.wait_ge(op[1].h, op[2])
                    else:
                        op[1](e).then_inc(op[2].h, op[3])
            return body

        with nc.Block() as block:
            block.tensor(run(self.pe))
            block.scalar(run(self.act))
            block.vector(run(self.dve))
            block.gpsimd(run(self.pool))
            block.sync(run(self.sp))


class TB:
    __slots__ = ("t", "b")

    def __init__(self, t, b=None):
        self.t = t
        self.b = b or Buf()


class Ring:
    def __init__(self, items):
        self.items = items
        self.i = 0

    def next(self):
        x = self.items[self.i % len(self.items)]
        self.i += 1
        return x


class Arena:
    def __init__(self, nc):
        self.nc = nc
        self.off = (nc.sbuf_base + 31) // 32 * 32
        self.top = nc.sbuf_top
        self.n = 0

    def alloc(self, shape, dtype, name=None):
        per = 1
        for s in shape[1:]:
            per *= s
        size = per * (2 if dtype == BF16 else 4)
        off = self.off
        self.off += (size + 31) // 32 * 32
        assert self.off <= self.top, f"SBUF overflow {self.off} > {self.top} ({name})"
        self.n += 1
        return TB(self.nc.alloc_sbuf_tensor_at(name or f"t{self.n}", list(shape), dtype, offset=off))

    def mark(self):
        return self.off

    def reset(self, m):
        self.off = m


def build_program(phases=9, dbg=()):
    nc = bass.Bass("TRN2", target_bir_lowering=False)
    S = Sched(nc)
    A = Arena(nc)
    PE, ACT, DVE, POOL, SP = S.pe, S.act, S.dve, S.pool, S.sp

    def din(name, shape):
        return nc.dram_tensor(name, list(shape), F32, kind="ExternalInput").ap()

    xc = din("xc", [TCTX, D])
    cT_d = din("cT", [128, 32])
    w_ada = din("w_ada", [D, 6 * D])
    badaT_d = din("b_adaT", [128, 192])
    gT_d = din("gT", [128, 64])
    w_in = din("w_in", [D, DIN])
    bfg_d = din("bfg", [16, 1])
    qkg_d = din("qkg", [128, 4])
    cposT_d = din("cposT", [128, 64])
    w_ck1 = din("w_ck1", [4096, 128])
    w_ck2 = din("w_ck2", [128, 128])
    w_cv1 = din("w_cv1", [4096, 128])
    w_cv2 = din("w_cv2", [128, 128])
    w_upa = din("w_upa", [2048, D])
    w_upb = din("w_upb", [2048, D])
    w_out = din("w_out", [D, D])
    w_ff1 = din("w_ff1", [D, DFF])
    w_ff2 = din("w_ff2", [DFF, D])
    ropeC_d = din("ropeC", [128, TCTX])
    ropeS_d = din("ropeS", [128, TCTX])
    ropeCc_d = din("ropeCc", [128, 128])
    ropeSc_d = din("ropeSc", [128, 128])
    cb_d = din("cb", [128, NCB])
    cf_d = din("cf", [128, NCF])
    out = nc.dram_tensor("out", [TOWN, D], F32, kind="ExternalOutput").ap()
    dbg_out = {}

    def dscr(name, shape, dt=BF16):
        return nc.dram_tensor(name, list(shape), dt).ap()

    d_q = dscr("d_q", [16, 128, TOWN])
    d_kc = dscr("d_kc", [4, 128, TCTX])
    d_vc = dscr("d_vc", [4, 128, TCTX])
    d_ks = dscr("d_ks", [4, 128, TCTX])
    d_kw = dscr("d_kw", [4, 128, TCTX])
    d_vs = dscr("d_vs", [TCTX, 4, 128])
    d_vw = dscr("d_vw", [TCTX, 4, 128])
    d_fq = dscr("d_fq", [16, 128, TOWN])
    d_fk = dscr("d_fk", [16, 128, TCTX])
    d_fv = dscr("d_fv", [TCTX, 16, 128])
    d_mg = dscr("d_mg", [64, 128, TOWN])
    grid = lambda n: [[Buf() for _ in range(4)] for _ in range(n)]
    g_q, g_kc, g_vc, g_ks, g_kw, g_vs, g_vw = grid(16), grid(4), grid(4), grid(4), grid(4), grid(4), grid(4)
    g_fq, g_fk, g_fv, g_mg = grid(16), grid(16), grid(16), grid(64)
    g_out = [[Buf() for _ in range(2)] for _ in range(32)]

    def mm(out_, lhsT, rhs, start, stop, reads, writes):
        S.emit(PE, lambda e: e.matmul(out_, lhsT, rhs, start=start, stop=stop, skip_group_check=True), reads, writes)

    def tr(out_, in_, ident, reads, writes):
        S.emit(PE, lambda e: e.transpose(out_, in_, ident), reads, writes)

    def act(out_, in_, func, reads, writes, **kw):
        S.emit(ACT, lambda e: e.activation(out=out_, in_=in_, func=func, **kw), reads, writes)

    def tt(q, out_, in0, in1, op, reads, writes):
        S.emit(q, lambda e: e.tensor_tensor(out=out_, in0=in0, in1=in1, op=op), reads, writes)

    def ts(q, out_, in0, s1, s2, op0, op1, reads, writes):
        if op1 is None:
            S.emit(q, lambda e: e.tensor_scalar(out=out_, in0=in0, scalar1=s1, scalar2=None, op0=op0), reads, writes)
        else:
            S.emit(q, lambda e: e.tensor_scalar(out=out_, in0=in0, scalar1=s1, scalar2=s2, op0=op0, op1=op1),
                   reads, writes)

    def stt(out_, in0, scalar, in1, op0, op1, reads, writes):
        S.emit(DVE, lambda e: e.scalar_tensor_tensor(out=out_, in0=in0, scalar=scalar, in1=in1, op0=op0, op1=op1),
               reads, writes)

    def cp(q, out_, in_, reads, writes):
        if q is ACT:
            S.emit(q, lambda e: e.activation(out=out_, in_=in_, func=AF.Copy), reads, writes)
        else:
            S.emit(q, lambda e: e.tensor_copy(out=out_, in_=in_), reads, writes)

    def dma(q, out_, in_, reads, writes):
        S.emit(q, lambda e: e.dma_start(out=out_, in_=in_), reads, writes, dma=True)

    def dump(name, src_ap, shape, reads, dt=F32, q=None):
        if name not in dbg:
            return
        t = nc.dram_tensor("dbg_" + name, list(shape), dt, kind="ExternalOutput").ap()
        b = Buf()
        dma(q or (POOL if dt != src_ap.dtype else SP), t, src_ap, reads, [b])
        dbg_out[name] = b

    PS = [TB(nc.alloc_psum_tensor(f"ps{i}", [128, 512], F32)) for i in range(8)]
    PSB = [p.t.bitcast(BF16) for p in PS]

    cb = A.alloc([128, NCB], BF16, "cb")
    cf = A.alloc([128, NCF], F32, "cf")
    modT = A.alloc([128, 192], F32, "modT")
    gT = A.alloc([128, 64], F32, "gT")
    s1T = A.alloc([128, 32], F32, "s1T")
    s2T = A.alloc([128, 32], F32, "s2T")
    qkg = A.alloc([128, 4], F32, "qkg")
    gate_tok = A.alloc([128, 8, 48], F32, "gate_tok")
    lsp = TB(nc.alloc_sbuf_tensor_at("lsp", [16, TCTX], F32, offset=(A.top - 8192) // 32 * 32))
    NSLOT = 5
    wslots = [A.alloc([128, 32, 128], BF16, f"ws{i}") for i in range(NSLOT)]
    wring = Ring(wslots)
    m0 = A.mark()

    ident = cb.t[:, CB_ID:CB_ID + 128]
    onesb = cb.t[:, CB_ONES:CB_ONES + 128]
    permT = cb.t[:, CB_PERM:CB_PERM + 32]
    mdiag4 = cb.t[:, CB_MDIAG:CB_MDIAG + 512]
    mwin4 = cb.t[:, CB_MWIN:CB_MWIN + 512]
    identf = cf.t[:, CF_ID:CF_ID + 128]

    dma(POOL, cb.t[:, 0:2048], cb_d[:, 0:2048], [], [cb.b])
    dma(POOL, cb.t[:, 2048:NCB], cb_d[:, 2048:NCB], [], [cb.b])
    dma(SP, cf.t[:], cf_d, [], [cf.b])
    dma(SP, gT.t[:], gT_d, [], [gT.b])
    dma(SP, qkg.t[:], qkg_d, [], [qkg.b])

    def gemm_multi(specs, defer=1):
        tiles = []
        for sp in specs:
            for ci, c in enumerate(range(0, sp["ncols"], 128)):
                tiles.append((sp, ci, c, min(128, sp["ncols"] - c)))
        need = lambda sp: sum((s_["K"] // 128 + 31) // 32 for s_ in sp["srcs"])
        loads = {}
        state = {"next": 0, "out": 0}

        def issue(ti):
            sp, ci, c, ncol = tiles[ti]
            sl = []
            for s_ in sp["srcs"]:
                KC = s_["K"] // 128
                Wv = s_["W"].rearrange("(kc p) n -> p kc n", p=128)
                for kc0 in range(0, KC, 32):
                    nk = min(32, KC - kc0)
                    w = wring.next()
                    c0 = s_["col0"] + c
                    S.emit(POOL, (lambda e, w=w, Wv=Wv, kc0=kc0, nk=nk, c0=c0, ncol=ncol:
                                  e.dma_start(out=w.t[:, 0:nk, 0:ncol], in_=Wv[:, kc0:kc0 + nk, c0:c0 + ncol])),
                           [], [w.b], dma=True)
                    sl.append((s_, w, kc0, nk))
            loads[ti] = sl

        def prefetch():
            while state["next"] < len(tiles) and state["out"] + need(tiles[state["next"]][0]) <= NSLOT:
                issue(state["next"])
                state["out"] += need(tiles[state["next"]][0])
                state["next"] += 1

        pending = []
        cur_sp = None
        for ti, (sp, ci, c, ncol) in enumerate(tiles):
            if sp is not cur_sp:
                for pnd in pending:
                    pnd[0](*pnd[1])
                pending = []
                cur_sp = sp
            prefetch()
            T = sp["T"]
            ttiles = [(t0, min(512, T - t0)) for t0 in range(0, T, 512)]
            outs = []
            for si, s_ in enumerate(sp["srcs"]):
                KC = s_["K"] // 128
                row = []
                for (t0, tn) in ttiles:
                    p = sp["prings"][si].next()
                    i = 0
                    for (s2, w, kc0, nk) in loads[ti]:
                        if s2 is not s_:
                            continue
                        for k in range(nk):
                            mm(p.t[0:ncol, 0:tn], w.t[:, k, 0:ncol],
                               s_["inT"].t[:, s_["kc_off"] + kc0 + k, t0:t0 + tn],
                               i == 0, i == KC - 1, [w.b, s_["inT"].b], [p.b])
                            i += 1
                    row.append((p, t0, tn))
                outs.append(row)
            state["out"] -= need(sp)
            del loads[ti]
            prefetch()
            pending.append((sp["epilogue"], (ci, ncol, outs)))
            if len(pending) > defer:
                pnd = pending.pop(0)
                pnd[0](*pnd[1])
        for pnd in pending:
            pnd[0](*pnd[1])

    def gemm(srcs, ncols, T, prings, epilogue, defer=1):
        gemm_multi([dict(srcs=srcs, ncols=ncols, T=T, prings=prings, epilogue=epilogue)], defer=defer)

    R_all = Ring(PS)
    cTf = A.alloc([128, 32], F32, "cTf")
    csig = A.alloc([128, 32], F32, "csig")
    silu = A.alloc([128, 32, 1], BF16, "silu")
    badaT = A.alloc([128, 192], F32, "badaT")
    dma(SP, cTf.t[:], cT_d, [], [cTf.b])
    dma(SP, badaT.t[:], badaT_d, [], [badaT.b])
    act(csig.t[:], cTf.t[:], AF.Sigmoid, [cTf.b], [csig.b])
    tt(DVE, silu.t[:, :, 0], cTf.t[:], csig.t[:], ALU.mult, [cTf.b, csig.b], [silu.b])

    def mod_gen(tile_lo, tile_hi, bank_fn, ahead=3):
        Wv = w_ada.rearrange("(kc p) n -> p kc n", p=128)
        work = [(t, s_) for t in range(tile_lo, tile_hi) for s_ in range(4)]
        issued = {}

        def issue(i):
            t, s_ = work[i]
            w = wring.next()
            wv = w.t[:].rearrange("p k c -> p (k c)").rearrange("p (k c) -> p k c", k=8)
            S.emit(POOL, (lambda e, wv=wv, t=t, s_=s_: e.dma_start(
                out=wv, in_=Wv[:, s_ * 8:(s_ + 1) * 8, t * 512:(t + 1) * 512])), [], [w.b], dma=True)
            issued[i] = (w, wv)

        for i in range(min(ahead, len(work))):
            issue(i)
        for i, (t, s_) in enumerate(work):
            if i + ahead < len(work):
                issue(i + ahead)
            w, wv = issued.pop(i)
            bank = bank_fn()
            for j in range(4):
                for k in range(8):
                    mm(bank.t[:, j:j + 1], wv[:, k, j * 128:(j + 1) * 128], silu.t[:, s_ * 8 + k, 0:1],
                       s_ == 0 and j == 0 and k == 0, s_ == 3 and k == 7, [w.b, silu.b], [bank.b])
            if s_ == 3:
                tt(DVE, modT.t[:, 4 * t:4 * t + 4], bank.t[:, 0:4], badaT.t[:, 4 * t:4 * t + 4], ALU.add,
                   [badaT.b], [modT.b, bank.b])
            yield

    for _ in mod_gen(0, 16, lambda: PS[0]):
        pass
    mod_bg = mod_gen(16, 48, lambda: mod_bank[0])
    mod_bank = [PS[6]]

    def mod_step(n=1):
        for _ in range(n):
            next(mod_bg, None)

    stt(s1T.t[:], modT.t[:, 32:64], 1.0, gT.t[:, 0:32], ALU.add, ALU.mult, [modT.b, gT.b], [s1T.b])

    def build_hT(src, row0, ntok, hT, shift_cols, sT, region_mark):
        A.reset(region_mark)
        xts = Ring([A.alloc([128, D], F32, f"xt{i}") for i in range(2)])
        xhs = [A.alloc([128, D], BF16, f"xh{j}") for j in range(4)]
        ss = Ring([A.alloc([128, 4], F32, f"ss{i}") for i in range(4)])
        pr = Ring(PS)
        for grp in range(ntok // 512):
            for j in range(4):
                r0 = row0 + grp * 512 + j * 128
                xt = xts.next()
                st = ss.next()
                dma(SP, xt.t[:], src[r0:r0 + 128, :], [], [xt.b])
                xh = xhs[j]
                S.emit(DVE, lambda e, xt=xt, st=st, xh=xh: e.scalar_tensor_tensor(
                    out=xh.t[:], in0=xt.t[:], scalar=1.0, in1=xt.t[:],
                    op0=ALU.mult, op1=ALU.mult, accum_out=st.t[:, 0:1]), [xt.b], [xh.b, st.b])
                act(st.t[:, 1:2], st.t[:, 0:1], AF.Ln, [epsc.b], [st.b], scale=1.0 / D, bias=epsc.t[:, 0:1])
                act(st.t[:, 2:3], st.t[:, 1:2], AF.Exp, [], [st.b], scale=-0.5)
                act(xh.t[:], xt.t[:], AF.Identity, [xt.b, st.b], [xh.b], scale=st.t[:, 2:3])
            for kc in range(32):
                p = pr.next()
                pb = PSB[PS.index(p)]
                for j in range(4):
                    tr(pb[:, j * 128:(j + 1) * 128], xhs[j].t[:, kc * 128:(kc + 1) * 128], ident, [xhs[j].b, cb.b], [p.b])
                dst = hT.t[:, kc, grp * 512:(grp + 1) * 512]
                if kc % 2 == 0:
                    act(dst, pb[:, 0:512], AF.Identity, [sT.b, modT.b], [hT.b, p.b],
                        scale=sT.t[:, kc:kc + 1], bias=modT.t[:, shift_cols + kc:shift_cols + kc + 1])
                else:
                    ts(DVE, dst, pb[:, 0:512], sT.t[:, kc:kc + 1], modT.t[:, shift_cols + kc:shift_cols + kc + 1],
                       ALU.mult, ALU.add, [sT.b, modT.b], [hT.b, p.b])

    epsc = A.alloc([128, 1], F32, "epsc")
    S.emit(DVE, lambda e: e.memset(epsc.t[:], EPS), [], [epsc.b])
    onec = A.alloc([128, 1], F32, "onec")
    bfg = A.alloc([16, 1], F32, "bfg")
    dma(SP, bfg.t[:], bfg_d, [], [bfg.b])
    S.emit(DVE, lambda e: e.memset(onec.t[:], 1.0), [], [onec.b])
    m0 = A.mark()
    hT = A.alloc([128, 32, TOWN], BF16, "hT")
    mR2 = A.mark()

    if phases >= 2:
        def proj_phase(half):
            tokc0 = half * TOWN
            S.barrier()
            build_hT(xc, tokc0, TOWN, hT, 0, s1T, mR2)
            if half == 1:
                dump("hT", hT.t[:], [128, 32, TOWN], [hT.b], BF16)
            S.barrier()
            A.reset(mR2)
            ropeC = A.alloc([128, TOWN], F32, "ropeC")
            ropeS = A.alloc([128, TOWN], F32, "ropeS")
            dma(SP, ropeC.t[:], ropeC_d[:, tokc0:tokc0 + TOWN], [], [ropeC.b])
            dma(SP, ropeS.t[:], ropeS_d[:, tokc0:tokc0 + TOWN], [], [ropeS.b])
            NB = 3
            sqr = Ring([A.alloc([128, 512], BF16, f"sq{i}") for i in range(NB)])
            lnr = Ring([A.alloc([128, 512], F32, f"ln{i}") for i in range(NB)])
            rsr = Ring([A.alloc([128, 512], F32, f"rs{i}") for i in range(NB)])
            yfr = Ring([A.alloc([128, 512], F32, f"yf{i}") for i in range(NB)])
            ybr = Ring([A.alloc([128, 512], BF16, f"yb{i}") for i in range(NB)])
            t1r = Ring([A.alloc([32, 512], F32, f"t1{i}") for i in range(NB)])
            t2r = Ring([A.alloc([32, 512], F32, f"t2{i}") for i in range(NB)])
            vtr = Ring([A.alloc([128, 4, 128], BF16, f"vt{i}") for i in range(NB)])
            gsr = Ring([A.alloc([48, 512], F32, f"gs{i}") for i in range(2)])
            Rm = Ring(PS[0:4])
            Rx = Ring(PS[4:8])
            src = lambda col0: [dict(W=w_in, K=D, col0=col0, inT=hT, kc_off=0)]

            def epi_raw(dst, g_dst):
                def f(ci, ncol, outs):
                    for (p, t0, tn) in outs[0]:
                        yb = ybr.next()
                        cp(ACT, yb.t[:, 0:tn], p.t[:, 0:tn], [], [yb.b, p.b])
                        tq = (tokc0 + t0) // 512
                        dma(SP, dst[ci, :, tokc0 + t0:tokc0 + t0 + tn], yb.t[:, 0:tn], [yb.b], [g_dst[ci][tq]])
                return f

            def epi_norm(dst, g_dst, gi, rope, ctx_tok):
                def f(ci, ncol, outs):
                    for (p, t0, tn) in outs[0]:
                        sq, ln, rs, yb = sqr.next(), lnr.next(), rsr.next(), ybr.next()
                        px = Rx.next()
                        act(sq.t[:], p.t[:], AF.Square, [], [sq.b, p.b])
                        mm(px.t[:], onesb, sq.t[:], True, True, [sq.b, cb.b], [px.b])
                        act(ln.t[:], px.t[:], AF.Ln, [epsc.b], [ln.b, px.b], scale=1.0 / 128, bias=epsc.t[:, 0:1])
                        act(rs.t[:], ln.t[:], AF.Exp, [ln.b], [rs.b], scale=-0.5)
                        d0 = (tokc0 if ctx_tok else 0) + t0
                        tq = d0 // 512
                        if not rope:
                            stt(yb.t[:], p.t[:], qkg.t[:, gi:gi + 1], rs.t[:], ALU.mult, ALU.mult,
                                [qkg.b, rs.b], [yb.b, p.b])
                        else:
                            yf, t1, t2 = yfr.next(), t1r.next(), t2r.next()
                            px2 = Rx.next()
                            stt(yf.t[:], p.t[:], qkg.t[:, gi:gi + 1], rs.t[:], ALU.mult, ALU.mult,
                                [qkg.b, rs.b], [yf.b, p.b])
                            cp(ACT, yb.t[:], yf.t[:], [yf.b], [yb.b])
                            mm(px2.t[0:32, :], permT, yb.t[:], True, True, [yb.b, cb.b], [px2.b])
                            tt(DVE, t2.t[:], px2.t[0:32, :], ropeS.t[0:32, t0:t0 + 512], ALU.mult,
                               [ropeS.b], [t2.b, px2.b])
                            tt(DVE, t1.t[:], yf.t[0:32, :], ropeC.t[0:32, t0:t0 + 512], ALU.mult,
                               [ropeC.b, yf.b], [t1.b])
                            tt(DVE, yb.t[0:32, :], t1.t[:], t2.t[:], ALU.add, [t1.b, t2.b], [yb.b])
                        dma(SP, dst[ci, :, d0:d0 + tn], yb.t[:, 0:tn], [yb.b], [g_dst[ci][tq]])
                return f

            def epi_vtok(dst, g_dst):
                def f(ci, ncol, outs):
                    for (p, t0, tn) in outs[0]:
                        yb, vt = ybr.next(), vtr.next()
                        px = Rx.next()
                        pxb = PSB[PS.index(px)]
                        cp(ACT, yb.t[:], p.t[:], [], [yb.b, p.b])
                        for j in range(4):
                            tr(pxb[:, j * 128:(j + 1) * 128], yb.t[:, j * 128:(j + 1) * 128], ident, [yb.b, cb.b], [px.b])
                        cp(DVE, vt.t[:], pxb[:, 0:512].rearrange("p (j d) -> p j d", j=4), [], [vt.b, px.b])
                        r0 = tokc0 + t0
                        dma(SP, dst[r0:r0 + 512, ci, :].rearrange("(j p) d -> p j d", p=128), vt.t[:], [vt.b],
                            [g_dst[ci][r0 // 512]])
                return f

            def epi_gate(ci, ncol, outs):
                for (p, t0, tn) in outs[0]:
                    gs = gsr.next()
                    px = Rx.next()
                    act(gs.t[:], p.t[0:48, :], AF.Sigmoid, [], [gs.b, p.b])
                    for j in range(4):
                        tr(px.t[:, j * 48:(j + 1) * 48], gs.t[0:48, j * 128:(j + 1) * 128], identf[0:48, 0:48],
                           [gs.b, cf.b], [px.b])
                    q0 = t0 // 128
                    cp(DVE, gate_tok.t[:, q0:q0 + 4, :], px.t[:, 0:192].rearrange("p (j c) -> p j c", j=4), [],
                       [gate_tok.b, px.b])

            def epi_f(ci, ncol, outs):
                for (p, t0, tn) in outs[0]:
                    z, ez = lnr.next(), rsr.next()
                    ts(DVE, z.t[0:16, 0:tn], p.t[0:16, 0:tn], bfg.t[:, 0:1], None, ALU.add, None, [bfg.b], [z.b, p.b])
                    act(ez.t[0:16, 0:tn], z.t[0:16, 0:tn], AF.Exp, [z.b], [ez.b], scale=-1.0)
                    act(lsp.t[:, tokc0 + t0:tokc0 + t0 + tn], ez.t[0:16, 0:tn], AF.Ln, [ez.b, onec.b], [lsp.b],
                        scale=1.0, bias=onec.t[0:16, 0:1])

            def epi_merge(ci, ncol, outs):
                for (p, t0, tn) in outs[0]:
                    yb = ybr.next()
                    act(yb.t[:], p.t[:], AF.Sigmoid, [], [yb.b, p.b])
                    dma(SP, d_mg[ci, :, t0:t0 + tn], yb.t[:, 0:tn], [yb.b], [g_mg[ci][t0 // 512]])

            KV0 = 2048
            FX0 = 5168
            specs = []

            def add(col0, ncols, epi):
                specs.append(dict(srcs=src(col0), ncols=ncols, T=TOWN, prings=[Rm], epilogue=epi))

            if half == 1:
                add(0, 2048, epi_norm(d_q, g_q, 0, True, False))
            add(KV0 + 0, 512, epi_raw(d_kc, g_kc))
            add(KV0 + 512, 512, epi_raw(d_vc, g_vc))
            add(KV0 + 1024, 512, epi_norm(d_ks, g_ks, 1, True, True))
            add(KV0 + 1536, 512, epi_vtok(d_vs, g_vs))
            add(KV0 + 2048, 512, epi_norm(d_kw, g_kw, 1, True, True))
            add(KV0 + 2560, 512, epi_vtok(d_vw, g_vw))
            if half == 1:
                add(5120, 48, epi_gate)
                add(FX0, 2048, epi_norm(d_fq, g_fq, 2, False, False))
            add(FX0 + 2048, 2048, epi_norm(d_fk, g_fk, 3, False, True))
            add(FX0 + 4096, 2048, epi_vtok(d_fv, g_fv))
            add(11312, 16, epi_f)
            if half == 1:
                add(11328, 8192, epi_merge)
            gemm_multi(specs)

        proj_phase(0)
        proj_phase(1)
        S.barrier()
        if "qn" in dbg:
            dump_d = nc.dram_tensor("dbg_qn", [16, 128, TOWN], BF16, kind="ExternalOutput").ap()
            b_ = Buf()
            dma(SP, dump_d, d_q, [x for r in g_q for x in r], [b_])
            dbg_out["qn"] = b_
        if "fk" in dbg:
            dump_d = nc.dram_tensor("dbg_fk", [16, 128, TCTX], BF16, kind="ExternalOutput").ap()
            b_ = Buf()
            dma(SP, dump_d, d_fk, [x for r in g_fk for x in r], [b_])
            dbg_out["fk"] = b_
        if "fv" in dbg:
            dump_d = nc.dram_tensor("dbg_fv", [TCTX, 16, 128], BF16, kind="ExternalOutput").ap()
            b_ = Buf()
            dma(SP, dump_d, d_fv, [x for r in g_fv for x in r], [b_])
            dbg_out["fv"] = b_
        if "ks" in dbg:
            dump_d = nc.dram_tensor("dbg_ks", [4, 128, TCTX], BF16, kind="ExternalOutput").ap()
            b_ = Buf()
            dma(SP, dump_d, d_ks, [x for r in g_ks for x in r], [b_])
            dbg_out["ks"] = b_
        dump("gate_tok", gate_tok.t[:], [128, 8, 48], [gate_tok.b])
        dump("lsp", lsp.t[:], [16, TCTX], [lsp.b])


    if phases >= 3:
        S.barrier()
        A.reset(m0)
        oT = A.alloc([128, 32, TOWN], BF16, "oT")
        m3 = A.mark()
        prevb = cf.t[:, CF_PREV:CF_PREV + 1]
        Rst = Ring(PS[0:3])
        Oacc = PS[3:7]
        Pm = PS[7]
        PmB = PSB[7]

        def all_(g, idx):
            return [b for b in g[idx]]

        if True:
            ones16 = A.alloc([16, TCTX], BF16, "ones16")
            cn = A.alloc([16, TCTX], F32, "cn")
            cntok = A.alloc([128, 16, 16], F32, "cntok")
            dg = A.alloc([16, 8, 16], F32, "dg")
            cbc = A.alloc([128, 8, 16], F32, "cbc")
            fbias = A.alloc([128, 8, 16, 16], F32, "fbias")
            S.emit(DVE, lambda e: e.memset(ones16.t[:], 1.0), [], [ones16.b])
            S.emit(DVE, lambda e: e.tensor_tensor_scan(out=cn.t[:], data0=ones16.t[:], data1=lsp.t[:], initial=0.0,
                                                       op0=ALU.mult, op1=ALU.add), [ones16.b, lsp.b], [cn.b])
            for kt in range(16):
                tr(Pm.t[:, kt * 16:(kt + 1) * 16], cn.t[0:16, kt * 128:(kt + 1) * 128],
                   cf.t[0:16, CF_ID16:CF_ID16 + 16], [cn.b, cf.b], [Pm.b])
            cp(DVE, cntok.t[:], Pm.t[:, 0:256].rearrange("p (k h) -> p k h", k=16), [], [cntok.b, Pm.b])
            cend = cn.t[0:16, TOWN + 127:TCTX:128]
            tt(DVE, dg.t[:], cend.unsqueeze(2).broadcast_to([16, 8, 16]),
               cf.t[0:16, CF_ID16:CF_ID16 + 16].unsqueeze(1).broadcast_to([16, 8, 16]), ALU.mult,
               [cn.b, cf.b], [dg.b])
            mm(Pm.t[:, 0:128], cf.t[0:16, CF_ONES:CF_ONES + 128], dg.t[:].rearrange("p a b -> p (a b)"), True, True,
               [dg.b, cf.b], [Pm.b])
            cp(DVE, cbc.t[:], Pm.t[:, 0:128].rearrange("p (a b) -> p a b", a=8), [], [cbc.b, Pm.b])
            for i in range(8):
                tt(DVE, fbias.t[:, i], cntok.t[:], cbc.t[:, i:i + 1, :].broadcast_to([128, 16, 16]), ALU.subtract,
                   [cntok.b, cbc.b], [fbias.b])
            ts(DVE, fbias.t[:, :, 0:8, :], fbias.t[:, :, 0:8, :], prevb, None, ALU.add, None, [cf.b], [fbias.b])
            dump("fbias", fbias.t[:], [128, 8, 16, 16], [fbias.b])
            mF = A.mark()
            fk4 = [A.alloc([128, TCTX], BF16, f"fk{i}") for i in range(4)]
            fq4 = [A.alloc([128, TOWN], BF16, f"fq{i}") for i in range(4)]
            fv4 = [A.alloc([128, 16, 130], BF16, f"fv{i}") for i in range(4)]
            pTr = Ring([A.alloc([128, 256], BF16, f"pT{i}") for i in range(6)])
            ofb = [A.alloc([128, 2, 128], BF16, f"ofb{i}") for i in range(4)]
            rDr = Ring([A.alloc([128, 2], F32, f"rD{i}") for i in range(8)])
            for v in fv4:
                S.emit(DVE, lambda e, v=v: e.memset(v.t[:, :, 128:130], 1.0), [], [v.b])
            Rst2 = Ring(PS[0:2])
            Rof = Ring(PS[2:6])
            fox_pend = []

            def fox_flush():
                for f_ in fox_pend:
                    f_()
                del fox_pend[:]
            for hg in range(4):
                for j in range(4):
                    h = hg * 4 + j
                    dma(SP, fk4[j].t[:], d_fk[h], all_(g_fk, h), [fk4[j].b])
                    dma(SP, fq4[j].t[:], d_fq[h], all_(g_fq, h), [fq4[j].b])
                    dma(SP, fv4[j].t[:, :, 0:128], d_fv[:, h, :].rearrange("(k p) d -> p k d", p=128),
                        all_(g_fv, h), [fv4[j].b])
                for qp in range(4):
                    q0, q1 = 2 * qp, 2 * qp + 1
                    qc0, qc1 = 8 + q0, 8 + q1
                    for j in range(4):
                        h = hg * 4 + j
                        of0, of1 = Rof.next(), Rof.next()
                        for g0 in range(0, qc1 + 1, 2):
                            kts = list(range(g0, min(g0 + 2, qc1 + 1)))
                            st = Rst2.next()
                            for idx, kt in enumerate(kts):
                                c0 = idx * 256
                                if kt <= qc0:
                                    mm(st.t[:, c0:c0 + 256], fk4[j].t[:, kt * 128:(kt + 1) * 128],
                                       fq4[j].t[:, q0 * 128:(q1 + 1) * 128], idx == 0, kt != qc0,
                                       [fk4[j].b, fq4[j].b], [st.b])
                                    if kt == qc0:
                                        mm(st.t[:, c0:c0 + 128], ident, mdiag4[:, 0:128], False, True, [cb.b], [st.b])
                                else:
                                    mm(st.t[:, c0 + 128:c0 + 256], fk4[j].t[:, kt * 128:(kt + 1) * 128],
                                       fq4[j].t[:, q1 * 128:(q1 + 1) * 128], idx == 0, False,
                                       [fk4[j].b, fq4[j].b], [st.b])
                                    mm(st.t[:, c0 + 128:c0 + 256], ident, mdiag4[:, 0:128], False, True, [cb.b],
                                       [st.b])
                            pTs = []
                            for idx, kt in enumerate(kts):
                                c0 = idx * 256
                                lo = 0 if kt <= qc0 else 128
                                pT = pTr.next()
                                act(pT.t[:, lo:256], st.t[:, c0 + lo:c0 + 256], AF.Exp, [fbias.b], [pT.b, st.b],
                                    scale=SCALE, bias=fbias.t[:, q0, kt, h:h + 1])
                                pTs.append((kt, pT))
                            fox_flush()

                            def pv(pTs=pTs, of0=of0, of1=of1, j=j, qc0=qc0, qc1=qc1):
                                for (kt, pT) in pTs:
                                    if kt <= qc0:
                                        mm(of0.t[:, 0:129], pT.t[:, 0:128], fv4[j].t[:, kt, 0:129], kt == 0,
                                           kt == qc0, [pT.b, fv4[j].b], [of0.b])
                                    mm(of1.t[:, 0:129], pT.t[:, 128:256], fv4[j].t[:, kt, 0:129], kt == 0,
                                       kt == qc1, [pT.b, fv4[j].b], [of1.b])
                            fox_pend.append(pv)

                        def epi(of0=of0, of1=of1, j=j):
                            for qi, of in enumerate((of0, of1)):
                                rD = rDr.next()
                                S.emit(DVE, lambda e, rD=rD, of=of: e.reciprocal(out=rD.t[:, 0:1],
                                                                                 in_=of.t[:, 128:129]),
                                       [], [rD.b, of.b])
                                ts(DVE, ofb[j].t[:, qi, :], of.t[:, 0:128], rD.t[:, 0:1], None, ALU.mult, None,
                                   [rD.b], [ofb[j].b, of.b])
                        fox_pend.append(epi)
                        mod_step(1)
                    fox_flush()
                    for qi, qt in enumerate((q0, q1)):
                        for j in range(4):
                            tr(PmB[:, j * 128:(j + 1) * 128], ofb[j].t[:, qi, :], ident, [ofb[j].b, cb.b], [Pm.b])
                        cp(DVE, oT.t[:, 16 + hg * 4:16 + hg * 4 + 4, qt * 128:(qt + 1) * 128],
                           PmB[:, 0:512].rearrange("p (j t) -> p j t", j=4), [], [oT.b, Pm.b])
            dump("ofoxT", oT.t[:, 16:32, :], [128, 16, TOWN], [oT.b], BF16)

        if phases >= 4:
            S.barrier()
            A.reset(m3)
            Rst = Ring(PS[0:2])
            mod_bank[0] = PS[2]
            w1k = A.alloc([128, 32, 128], BF16, "w1k")
            w1v = A.alloc([128, 32, 128], BF16, "w1v")
            w2k = A.alloc([128, 128], BF16, "w2k")
            w2v = A.alloc([128, 128], BF16, "w2v")
            posT = A.alloc([128, 64], BF16, "posT")
            rCc = A.alloc([128, 128], F32, "rCc")
            rSc = A.alloc([128, 128], F32, "rSc")
            cbk = A.alloc([128, 2], F32, "cbk")
            dma(POOL, w1k.t[:], w_ck1.rearrange("(l d) j -> d l j", d=128), [], [w1k.b])
            dma(POOL, w1v.t[:], w_cv1.rearrange("(l d) j -> d l j", d=128), [], [w1v.b])
            dma(POOL, w2k.t[:], w_ck2, [], [w2k.b])
            dma(POOL, w2v.t[:], w_cv2, [], [w2v.b])
            dma(POOL, posT.t[:], cposT_d, [], [posT.b])
            dma(SP, rCc.t[:], ropeCc_d, [], [rCc.b])
            dma(SP, rSc.t[:], ropeSc_d, [], [rSc.b])
            for wi, w1 in enumerate((w1k, w1v)):
                for l in range(32):
                    mm(Pm.t[:, 0:1], w1.t[:, l, :], posT.t[:, wi * 32 + l:wi * 32 + l + 1], l == 0, l == 31,
                       [w1.b, posT.b], [Pm.b])
                cp(DVE, cbk.t[:, wi:wi + 1], Pm.t[:, 0:1], [], [cbk.b, Pm.b])
            ksT = A.alloc([128, TCTX], BF16, "ksT")
            kwT = A.alloc([128, TCTX], BF16, "kwT")
            kcT = A.alloc([128, TCTX], BF16, "kcT")
            vcT = A.alloc([128, TCTX], BF16, "vcT")
            vs = A.alloc([128, 16, 130], BF16, "vs")
            vw = A.alloc([128, 16, 130], BF16, "vw")
            q4 = A.alloc([128, 4, TOWN], BF16, "q4")
            kcmpT = A.alloc([128, 128], BF16, "kcmpT")
            vcmp = A.alloc([128, 162], BF16, "vcmp")
            ca = A.alloc([128, 128], F32, "ca")
            ca2 = A.alloc([128, 128], F32, "ca2")
            cu = A.alloc([128, 128], F32, "cu")
            gl = A.alloc([128, 128], BF16, "gl")
            csq = A.alloc([128, 128], BF16, "csq")
            cln = A.alloc([128, 128], F32, "cln")
            cyf = A.alloc([128, 128], F32, "cyf")
            ct1 = A.alloc([32, 128], F32, "ct1")
            ct2 = A.alloc([32, 128], F32, "ct2")
            pT4 = Ring([A.alloc([128, 512], BF16, f"pq{i}") for i in range(4)])
            oaccS = A.alloc([128, 4, 128], F32, "oaccS")
            onsa = A.alloc([128, 4, 128], BF16, "onsa")
            sm = Ring([A.alloc([128, 16], F32, f"sm{i}") for i in range(8)])
            imp = A.alloc([128, 32], F32, "imp")
            sc = A.alloc([128, 32], F32, "sc")
            sc2 = A.alloc([128, 32], F32, "sc2")
            m8a = A.alloc([128, 8], F32, "m8a")
            m8b = A.alloc([128, 8], F32, "m8b")
            selm = A.alloc([128, 32], F32, "selm")
            selv = A.alloc([128, 32], F32, "selv")
            sbb = A.alloc([128, 32], BF16, "sbb")
            selT = A.alloc([32, 128], BF16, "selT")
            S.emit(DVE, lambda e: e.memset(vs.t[:, :, 128:130], 1.0), [], [vs.b])
            S.emit(DVE, lambda e: e.memset(vw.t[:, :, 128:130], 1.0), [], [vw.b])
            S.emit(DVE, lambda e: e.memset(vcmp.t[:, 128:129], 1.0), [], [vcmp.b])
            cp(DVE, vcmp.t[:, 129:161], cb.t[:, CB_OV:CB_OV + 32], [cb.b], [vcmp.b])
            S.emit(DVE, lambda e: e.memset(kcmpT.t[:], 0.0), [], [kcmpT.b])
            cmpb = cb.t[:, CB_CMPB:CB_CMPB + 1024]
            ftab = cf.t[:, CF_FTAB:CF_FTAB + 256]

            def weight_cols(p, r, gcol, smt):
                ts(DVE, smt.t[:, 4 + r:5 + r], p.t[:, 128:129], 1e-30, None, ALU.max, None, [], [smt.b, p.b])
                S.emit(DVE, lambda e: e.reciprocal(out=smt.t[:, 8 + r:9 + r], in_=smt.t[:, 4 + r:5 + r]), [], [smt.b])
                tt(DVE, smt.t[:, r:r + 1], smt.t[:, 8 + r:9 + r], gcol, ALU.mult, [gate_tok.b], [smt.b])

            for g in range(4):
                dma(SP, ksT.t[:], d_ks[g], all_(g_ks, g), [ksT.b])
                dma(SP, kwT.t[:], d_kw[g], all_(g_kw, g), [kwT.b])
                dma(SP, kcT.t[:], d_kc[g], all_(g_kc, g), [kcT.b])
                dma(SP, vcT.t[:], d_vc[g], all_(g_vc, g), [vcT.b])
                dma(SP, vs.t[:, :, 0:128], d_vs[:, g, :].rearrange("(k p) d -> p k d", p=128), all_(g_vs, g), [vs.b])
                dma(SP, vw.t[:, :, 0:128], d_vw[:, g, :].rearrange("(k p) d -> p k d", p=128), all_(g_vw, g), [vw.b])
                dma(SP, q4.t[:], d_q[4 * g:4 * g + 4].rearrange("h d t -> d h t"),
                    [b for h in range(4 * g, 4 * g + 4) for b in g_q[h]], [q4.b])
                for wi, (w1, w2, xT) in enumerate(((w1k, w2k, kcT), (w1v, w2v, vcT))):
                    hp = Rst.next()
                    for l in range(32):
                        mm(hp.t[:, 0:127], w1.t[:, l, :], xT.t[:, l:l + 16 * 126 + 1:16], l == 0, l == 31,
                           [w1.b, xT.b], [hp.b])
                    ts(DVE, ca.t[:, 0:127], hp.t[:, 0:127], cbk.t[:, wi:wi + 1], None, ALU.add, None, [cbk.b],
                       [ca.b, hp.b])
                    tt(DVE, ca2.t[:, 0:127], ca.t[:, 0:127], ca.t[:, 0:127], ALU.mult, [ca.b], [ca2.b])
                    ts(DVE, ca2.t[:, 0:127], ca2.t[:, 0:127], 0.044715, 1.0, ALU.mult, ALU.add, [], [ca2.b])
                    tt(DVE, cu.t[:, 0:127], ca2.t[:, 0:127], ca.t[:, 0:127], ALU.mult, [ca2.b, ca.b], [cu.b])
                    act(cu.t[:, 0:127], cu.t[:, 0:127], AF.Sigmoid, [], [cu.b], scale=1.5957691216)
                    tt(DVE, gl.t[:, 0:127], cu.t[:, 0:127], ca.t[:, 0:127], ALU.mult, [cu.b, ca.b], [gl.b])
                    kp = Rst.next()
                    if wi == 0:
                        mm(kp.t[:, 0:127], w2.t[:], gl.t[:, 0:127], True, True, [w2.b, gl.b], [kp.b])
                        act(csq.t[:, 0:127], kp.t[:, 0:127], AF.Square, [], [csq.b, kp.b])
                        mm(Pm.t[:, 0:127], onesb, csq.t[:, 0:127], True, True, [csq.b, cb.b], [Pm.b])
                        act(cln.t[:, 0:127], Pm.t[:, 0:127], AF.Ln, [epsc.b], [cln.b, Pm.b], scale=1.0 / 128,
                            bias=epsc.t[:, 0:1])
                        act(cln.t[:, 0:127], cln.t[:, 0:127], AF.Exp, [], [cln.b], scale=-0.5)
                        stt(cyf.t[:, 0:127], kp.t[:, 0:127], qkg.t[:, 1:2], cln.t[:, 0:127], ALU.mult, ALU.mult,
                            [qkg.b, cln.b], [cyf.b, kp.b])
                        cp(ACT, kcmpT.t[:, 0:127], cyf.t[:, 0:127], [cyf.b], [kcmpT.b])
                        mm(Pm.t[0:32, 0:127], permT, kcmpT.t[:, 0:127], True, True, [kcmpT.b, cb.b], [Pm.b])
                        tt(DVE, ct2.t[:, 0:127], Pm.t[0:32, 0:127], rSc.t[0:32, 0:127], ALU.mult, [rSc.b],
                           [ct2.b, Pm.b])
                        tt(DVE, ct1.t[:, 0:127], cyf.t[0:32, 0:127], rCc.t[0:32, 0:127], ALU.mult, [rCc.b, cyf.b],
                           [ct1.b])
                        tt(DVE, kcmpT.t[0:32, 0:127], ct1.t[:, 0:127], ct2.t[:, 0:127], ALU.add, [ct1.b, ct2.b],
                           [kcmpT.b])
                    else:
                        mm(kp.t[0:127, 0:128], gl.t[:, 0:127], w2.t[:], True, True, [w2.b, gl.b], [kp.b])
                        cp(DVE, vcmp.t[0:127, 0:128], kp.t[0:127, 0:128], [], [vcmp.b, kp.b])
                for qt in range(8):
                    qc = 8 + qt
                    q4s = q4.t[:, :, qt * 128:(qt + 1) * 128]
                    st4 = lambda st, np_=128: st.t[0:np_, 0:512].rearrange("p (r t) -> p r t", r=4)
                    st = Rst.next()
                    mm(st4(st, 127), kcmpT.t[:, 0:127], q4s, True, False, [kcmpT.b, q4.b], [st.b])
                    mm(st4(st, 127), ident[0:127, 0:127],
                       cmpb[0:127, qt * 128:(qt + 1) * 128].unsqueeze(1).broadcast_to([127, 4, 128]), False, True,
                       [cb.b], [st.b])
                    pT = pT4.next()
                    act(pT.t[0:127, :], st.t[0:127, 0:512], AF.Exp, [], [pT.b, st.b], scale=SCALE)
                    smt = sm.next()
                    for r in range(4):
                        mm(Oacc[r].t[:, 0:161], pT.t[0:127, r * 128:(r + 1) * 128], vcmp.t[0:127, 0:161], True, True,
                           [pT.b, vcmp.b], [Oacc[r].b])
                    for r in range(4):
                        h = 4 * g + r
                        p = Oacc[r]
                        weight_cols(p, r, gate_tok.t[:, qt, 3 * h:3 * h + 1], smt)
                        if r == 0:
                            ts(DVE, imp.t[:], p.t[:, 129:161], smt.t[:, 8 + r:9 + r], None, ALU.mult, None, [smt.b],
                               [imp.b, p.b])
                        else:
                            stt(imp.t[:], p.t[:, 129:161], smt.t[:, 8 + r:9 + r], imp.t[:], ALU.mult, ALU.add,
                                [smt.b], [imp.b, p.b])
                        ts(DVE, oaccS.t[:, r, :], p.t[:, 0:128], smt.t[:, r:r + 1], None, ALU.mult, None, [smt.b],
                           [oaccS.b, p.b])
                    tt(DVE, sc.t[:], imp.t[:], ftab[:, qt * 32:(qt + 1) * 32], ALU.add, [imp.b, cf.b], [sc.b])
                    S.emit(DVE, lambda e: e.max(out=m8a.t[:], in_=sc.t[:]), [sc.b], [m8a.b])
                    S.emit(DVE, lambda e: e.match_replace(out=sc2.t[:], in_to_replace=m8a.t[:], in_values=sc.t[:],
                                                          imm_value=-1.0e9), [sc.b, m8a.b], [sc2.b])
                    S.emit(DVE, lambda e: e.max(out=m8b.t[:], in_=sc2.t[:]), [sc2.b], [m8b.b])
                    ts(DVE, selm.t[:], sc.t[:], m8b.t[:, 7:8], None, ALU.is_ge, None, [sc.b, m8b.b], [selm.b])
                    ts(DVE, selv.t[:], sc.t[:], -0.5 * BIG, None, ALU.is_gt, None, [sc.b], [selv.b])
                    tt(DVE, selm.t[:], selm.t[:], selv.t[:], ALU.mult, [selv.b], [selm.b])
                    ts(DVE, sbb.t[:], selm.t[:], -NEG, NEG, ALU.mult, ALU.add, [selm.b], [sbb.b])
                    if g == 0 and qt == 7:
                        dump("selm", selm.t[:], [128, 32], [selm.b])
                        dump("imp", imp.t[:], [128, 32], [imp.b])
                    selT4 = selT.t[0:32, :].unsqueeze(1).broadcast_to([32, 4, 128])
                    pend = None
                    for kt in range(qc - 4, qc + 1):
                        st = Rst.next()
                        last_qk = not (kt == qc - 4 or kt == qc)
                        mm(st4(st), kwT.t[:, kt * 128:(kt + 1) * 128], q4s, True, last_qk, [kwT.b, q4.b], [st.b])
                        if kt == qc - 4:
                            mm(st.t[:, 0:512], ident, mwin4, False, True, [cb.b], [st.b])
                        if kt == qc:
                            mm(st.t[:, 0:512], ident, mdiag4, False, True, [cb.b], [st.b])
                        pT = pT4.next()
                        if kt < 8:
                            act(pT.t[:], st.t[:, 0:512], AF.Exp, [cf.b], [pT.b, st.b], scale=SCALE, bias=prevb)
                        else:
                            act(pT.t[:], st.t[:, 0:512], AF.Exp, [], [pT.b, st.b], scale=SCALE)
                        if pend is not None:
                            pend()

                        def pend(kt=kt, pT=pT, qc=qc):
                            for r in range(4):
                                mm(Oacc[r].t[:, 0:129], pT.t[:, r * 128:(r + 1) * 128], vw.t[:, kt, 0:129],
                                   kt == qc - 4, kt == qc, [pT.b, vw.b], [Oacc[r].b])
                    pend()
                    smt = sm.next()
                    for r in range(4):
                        h = 4 * g + r
                        p = Oacc[r]
                        weight_cols(p, r, gate_tok.t[:, qt, 3 * h + 2:3 * h + 3], smt)
                        stt(oaccS.t[:, r, :], p.t[:, 0:128], smt.t[:, r:r + 1], oaccS.t[:, r, :], ALU.mult, ALU.add,
                            [smt.b], [oaccS.b, p.b])
                    tr(PmB[0:32, 0:128], sbb.t[:, 0:32], ident, [sbb.b, cb.b], [Pm.b])
                    cp(DVE, selT.t[:], PmB[0:32, 0:128], [], [selT.b, Pm.b])
                    pend = None
                    for kt in range(qc + 1):
                        st = Rst.next()
                        mm(st4(st), ksT.t[:, kt * 128:(kt + 1) * 128], q4s, True, False, [ksT.b, q4.b], [st.b])
                        mm(st4(st), cb.t[0:32, CB_ESEL + kt * 128:CB_ESEL + (kt + 1) * 128], selT4, False, kt != qc,
                           [selT.b, cb.b], [st.b])
                        if kt == qc:
                            mm(st.t[:, 0:512], ident, mdiag4, False, True, [cb.b], [st.b])
                        pT = pT4.next()
                        act(pT.t[:], st.t[:, 0:512], AF.Exp, [], [pT.b, st.b], scale=SCALE)
                        if pend is not None:
                            pend()

                        def pend(kt=kt, pT=pT, qc=qc):
                            for r in range(4):
                                mm(Oacc[r].t[:, 0:129], pT.t[:, r * 128:(r + 1) * 128], vs.t[:, kt, 0:129], kt == 0,
                                   kt == qc, [pT.b, vs.b], [Oacc[r].b])
                    pend()
                    smt = sm.next()
                    for r in range(4):
                        h = 4 * g + r
                        p = Oacc[r]
                        weight_cols(p, r, gate_tok.t[:, qt, 3 * h + 1:3 * h + 2], smt)
                        stt(onsa.t[:, r, :], p.t[:, 0:128], smt.t[:, r:r + 1], oaccS.t[:, r, :], ALU.mult, ALU.add,
                            [smt.b, oaccS.b], [onsa.b, p.b])
                    mod_step(2)
                    for r in range(4):
                        tr(PmB[:, r * 128:(r + 1) * 128], onsa.t[:, r, :], ident, [onsa.b, cb.b], [Pm.b])
                    cp(DVE, oT.t[:, 4 * g:4 * g + 4, qt * 128:(qt + 1) * 128],
                       PmB[:, 0:512].rearrange("p (j t) -> p j t", j=4), [], [oT.b, Pm.b])
            dump("onsaT", oT.t[:, 0:16, :], [128, 16, TOWN], [oT.b], BF16)

    if phases >= 5:
        for _ in mod_bg:
            pass
        stt(s2T.t[:], modT.t[:, 128:160], 1.0, gT.t[:, 32:64], ALU.add, ALU.mult, [modT.b, gT.b], [s2T.b])
        dump("modT", modT.t[:], [128, 192], [modT.b])
        S.barrier()
        A.reset(m3)
        yT = A.alloc([128, 32, TOWN], BF16, "yT")
        m5 = A.mark()
        sgr = Ring([A.alloc([128, 512], BF16, f"sg{i}") for i in range(4)])
        t1r = Ring([A.alloc([128, 512], F32, f"u1{i}") for i in range(2)])
        t2r = Ring([A.alloc([128, 512], F32, f"u2{i}") for i in range(2)])

        def epi_up(ci, ncol, outs):
            for ti in range(2):
                pa, t0, tn = outs[0][ti]
                pb_, _, _ = outs[1][ti]
                sa, sb_ = sgr.next(), sgr.next()
                t1, t2 = t1r.next(), t2r.next()
                dma(SP, sa.t[:], d_mg[ci, :, t0:t0 + 512], [g_mg[ci][ti]], [sa.b])
                dma(SP, sb_.t[:], d_mg[32 + ci, :, t0:t0 + 512], [g_mg[32 + ci][ti]], [sb_.b])
                tt(DVE, t1.t[:], pa.t[:], sa.t[:], ALU.mult, [sa.b], [t1.b, pa.b])
                tt(DVE, t2.t[:], pb_.t[:], sb_.t[:], ALU.mult, [sb_.b], [t2.b, pb_.b])
                tt(POOL, yT.t[:, ci, t0:t0 + 512], t1.t[:], t2.t[:], ALU.add, [t1.b, t2.b], [yT.b])

        gemm([dict(W=w_upa, K=2048, col0=0, inT=oT, kc_off=0), dict(W=w_upb, K=2048, col0=0, inT=oT, kc_off=16)],
             D, TOWN, [Ring(PS[0:4]), Ring(PS[4:8])], epi_up)
        dump("yT", yT.t[:], [128, 32, TOWN], [yT.b], BF16)

    if phases >= 6:
        S.barrier()
        A.reset(m5)
        tmpr = Ring([A.alloc([128, 512], F32, f"tm{i}") for i in range(2)])
        xpr = Ring([A.alloc([128, 4, 128], F32, f"xp{i}") for i in range(2)])
        x1r = Ring([A.alloc([128, 4, 128], F32, f"x1{i}") for i in range(2)])
        Rz = Ring(PS[4:8])

        def epi_res(gate_col, src, row0, src_grid):
            def f(ci, ncol, outs):
                for ti, (p, t0, tn) in enumerate(outs[0]):
                    tmp, xp, x1 = tmpr.next(), xpr.next(), x1r.next()
                    pz = Rz.next()
                    act(tmp.t[:], p.t[:], AF.Identity, [modT.b], [tmp.b, p.b],
                        scale=modT.t[:, gate_col + ci:gate_col + ci + 1])
                    for j in range(4):
                        tr(pz.t[:, j * 128:(j + 1) * 128], tmp.t[:, j * 128:(j + 1) * 128], identf, [tmp.b, cf.b],
                           [pz.b])
                    dma(SP, xp.t[:], src[row0 + t0:row0 + t0 + 512, ci * 128:(ci + 1) * 128].rearrange(
                        "(j p) c -> p j c", p=128), [src_grid[ci][ti]] if src_grid else [], [xp.b])
                    tt(DVE, x1.t[:], pz.t[:, 0:512].rearrange("p (j c) -> p j c", j=4), xp.t[:], ALU.add, [xp.b],
                       [x1.b, pz.b])
                    dma(SP, out[t0:t0 + 512, ci * 128:(ci + 1) * 128].rearrange("(j p) c -> p j c", p=128), x1.t[:],
                        [x1.b], [g_out[ci][ti]])
            return f

        gemm([dict(W=w_out, K=D, col0=0, inT=yT, kc_off=0)], D, TOWN, [Ring(PS[0:4])],
             epi_res(64, xc, TOWN, None))

    if phases >= 7:
        S.barrier()
        h2T = TB(oT.t, oT.b)
        build_hT(out, 0, TOWN, h2T, 96, s2T, m3)
        S.barrier()
        A.reset(m3)
        uT = A.alloc([128, 32, TOWN], BF16, "uT")
        rr = Ring([A.alloc([128, 512], F32, f"rr{i}") for i in range(2)])
        tmpr = Ring([A.alloc([128, 512], F32, f"tn{i}") for i in range(1)])
        xpr = Ring([A.alloc([128, 4, 128], F32, f"xq{i}") for i in range(2)])
        x1r = Ring([A.alloc([128, 4, 128], F32, f"xr{i}") for i in range(1)])
        Rz = Ring(PS[4:8])

        def epi_ff1(ci, ncol, outs):
            for (p, t0, tn) in outs[0]:
                r = rr.next()
                act(r.t[:], p.t[:], AF.Relu, [], [r.b, p.b])
                tt(DVE, uT.t[:, ci, t0:t0 + 512], r.t[:], r.t[:], ALU.mult, [r.b], [uT.b])

        specs = []
        for fc in range(4):
            specs.append(dict(srcs=[dict(W=w_ff1, K=D, col0=fc * 4096, inT=h2T, kc_off=0)], ncols=4096, T=TOWN,
                              prings=[Ring(PS[0:4])], epilogue=epi_ff1))
            specs.append(dict(srcs=[dict(W=w_ff2[fc * 4096:(fc + 1) * 4096, :], K=4096, col0=0, inT=uT, kc_off=0)],
                              ncols=D, T=TOWN, prings=[Ring(PS[0:4])], epilogue=epi_res(160, out, 0, g_out)))
        gemm_multi(specs)

    finals = [b for row in g_out for b in row] + list(dbg_out.values())
    if phases < 9:
        z = A.alloc([128, 512], F32, "zero")
        S.emit(DVE, lambda e: e.memset(z.t[:], 0.0), [], [z.b])
        dma(SP, out[0:128, 0:512], z.t[:], [z.b], [g_out[0][0]])
    S.finish(finals)
    return nc


def _consts(half):
    f32 = np.float32
    cbp = np.zeros((128, NCB), f32)
    cbp[:, CB_ID:CB_ID + 128] = np.eye(128, dtype=f32)
    cbp[:, CB_ONES:CB_ONES + 128] = 1.0
    for m in range(32):
        srcm = m + 16 if m < 16 else m - 16
        cbp[srcm, CB_PERM + m] = 1.0
    s = np.arange(128)[:, None]
    t = np.arange(128)[None, :]
    md = np.where(s <= t, 0.0, NEG).astype(f32)
    mw = np.where(s > t, 0.0, NEG).astype(f32)
    cbp[:, CB_MDIAG:CB_MDIAG + 512] = np.tile(md, (1, 4))
    cbp[:, CB_MWIN:CB_MWIN + 512] = np.tile(mw, (1, 4))
    for kt in range(16):
        for ss in range(128):
            cbp[2 * kt + (1 if ss >= 64 else 0), CB_ESEL + kt * 128 + ss] = 1.0
    n = np.arange(127)
    ci = n[:, None] * 16
    sj = np.arange(32)[None, :] * 64
    cbp[:127, CB_OV:CB_OV + 32] = ((ci < sj + 64) & (ci + 32 > sj)).astype(f32)
    tctx = TOWN + np.arange(TOWN)[None, :]
    valid = (n[:, None] >= 64) if half == 0 else np.ones((127, 1), bool)
    cbp[:127, CB_CMPB:CB_CMPB + 1024] = np.where(valid & (16 * n[:, None] + 31 <= tctx), 0.0, NEG)
    cbp[127, CB_CMPB:CB_CMPB + 1024] = NEG

    cfp = np.zeros((128, NCF), f32)
    cfp[:, CF_ID:CF_ID + 128] = np.eye(128, dtype=f32)
    j0 = 16 * (1 - half)
    for qt in range(8):
        tc = TOWN + qt * 128 + np.arange(128)
        cur = tc // 64
        jj = np.arange(32)[None, :]
        forced = (jj == j0) | (jj == cur[:, None]) | ((jj == cur[:, None] - 1) & (cur[:, None] - 1 >= j0))
        causal = (jj >= j0) & (jj <= cur[:, None])
        cfp[:, CF_FTAB + qt * 32:CF_FTAB + (qt + 1) * 32] = np.where(forced, BIG, np.where(causal, 0.0, -BIG))
    cfp[:, CF_PREV] = 0.0 if half == 1 else NEG
    cfp[:, CF_ONES:CF_ONES + 128] = 1.0
    cfp[:16, CF_ID16:CF_ID16 + 16] = np.eye(16, dtype=f32)

    inv = (500000.0 ** (-np.arange(16, dtype=f32) / 16)).astype(f32)

    def rope_tabs(pos):
        pos = pos.astype(f32)
        ang = pos[None, :] * inv[:, None]
        c, s_ = np.cos(ang).astype(f32), np.sin(ang).astype(f32)
        C = np.ones((128, pos.shape[0]), f32)
        Sg = np.zeros((128, pos.shape[0]), f32)
        C[0:16] = c
        C[16:32] = c
        Sg[0:16] = -s_
        Sg[16:32] = s_
        return C, Sg

    pos_ctx = np.arange(TCTX) - TOWN * (1 - half)
    ropeC, ropeS = rope_tabs(pos_ctx)
    endp = np.arange(128) * 16 + 31 - TOWN * (1 - half)
    ropeCc, ropeSc = rope_tabs(endp)
    return dict(cb=cbp, cf=cfp, ropeC=ropeC, ropeS=ropeS, ropeCc=ropeCc, ropeSc=ropeSc)


def make_in_maps(inp, cores=range(8)):
    f32 = np.float32
    A_ = lambda a: np.ascontiguousarray(np.asarray(a, dtype=f32))
    shared = dict(
        w_ada=A_(inp["w_ada"][0]), b_adaT=A_(np.asarray(inp["b_ada"][0]).reshape(192, 128).T),
        gT=A_(np.concatenate([np.asarray(inp["norm1_g"][0]).reshape(32, 128).T,
                              np.asarray(inp["norm2_g"][0]).reshape(32, 128).T], axis=1)),
        w_in=A_(inp["w_in"][0]), bfg=A_(np.asarray(inp["b_forget"][0]).reshape(16, 1)),
        qkg=A_(np.stack([np.asarray(inp["nsa_q_norm"][0]), np.asarray(inp["nsa_k_norm"][0]),
                         np.asarray(inp["fox_q_norm"][0]), np.asarray(inp["fox_k_norm"][0])], axis=1)),
        cposT=A_(np.concatenate([np.asarray(inp["cmp_pos_k"][0]).T, np.asarray(inp["cmp_pos_v"][0]).T], axis=1)),
        w_ck1=A_(inp["w_cmp_k1"][0]), w_ck2=A_(inp["w_cmp_k2"][0]),
        w_cv1=A_(inp["w_cmp_v1"][0]), w_cv2=A_(inp["w_cmp_v2"][0]),
        w_upa=A_(inp["w_up_nsa"][0]), w_upb=A_(inp["w_up_fox"][0]), w_out=A_(inp["w_out"][0]),
        w_ff1=A_(inp["w_ff1"][0]), w_ff2=A_(inp["w_ff2"][0]),
    )
    consts = [_consts(0), _consts(1)]
    x = np.asarray(inp["x"], dtype=f32)
    c = np.asarray(inp["c"], dtype=f32)
    maps = []
    for k in cores:
        b, half = k // 2, k % 2
        if half == 1:
            xcc = x[b]
        else:
            xcc = np.concatenate([np.zeros((TOWN, D), f32), x[b, :TOWN]], axis=0)
        m = dict(shared)
        m["xc"] = np.ascontiguousarray(xcc)
        m["cT"] = A_(c[b].reshape(32, 128).T)
        m.update(consts[half])
        maps.append(m)
    return maps


_NC_CACHE = {}


def kernel(**inputs):
    if "nc" not in _NC_CACHE:
        _NC_CACHE["nc"] = build_program()
    nc = _NC_CACHE["nc"]
    maps = make_in_maps(inputs)
    res = run_bass_kernel_spmd(nc, maps, core_ids=list(range(8)))
    outp = np.zeros((4, 2048, D), np.float32)
    for k in range(8):
        b, half = k // 2, k % 2
        outp[b, half * TOWN:(half + 1) * TOWN] = np.asarray(res.results[k]["out"])
    return outp
```
